# Optimizing a Trainium2 kernel written in Bass

```python
import math
import jax
import jax.numpy as jnp
from jax import lax
import numpy as np

D_MODEL = 1024
BATCH = 2
SEQ = 8192
DEPTH = 2

CONV_WIDTH = 4
ML_HEADS = 8
ML_DQK = 64
ML_DV = 128
ML_CHUNK = 64
ML_W = ML_HEADS * ML_DV
ML_QK_W = ML_HEADS * ML_DQK
SSM_HEADS = 16
SSM_HEADDIM = 64
SSM_GROUPS = 2
SSM_STATE = 128
SSM_CHUNK = 128
SSM_W = SSM_HEADS * SSM_HEADDIM
SSM_BC_W = SSM_GROUPS * SSM_STATE
FOX_HEADS = 8
FOX_HEADDIM = 128
FOX_QBLOCK = 128
FOX_W = FOX_HEADS * FOX_HEADDIM
PEER_HEADS = 8
PEER_NKEYS = 128
PEER_NEXPERTS = PEER_NKEYS * PEER_NKEYS
PEER_QDIM = 256
PEER_TOPK = 16
PEER_TOKEN_BLOCK = 128
PLE_DIM = 256
EPS = 1e-6

IN_SPLITS = (ML_QK_W, ML_QK_W, ML_W, ML_W, ML_HEADS, ML_HEADS,
             SSM_W, SSM_W, SSM_BC_W, SSM_BC_W, SSM_HEADS,
             FOX_W, FOX_W, FOX_W, FOX_HEADS,
             D_MODEL, D_MODEL, D_MODEL)
D_IN = sum(IN_SPLITS)

kernel_name = 'hybrid_mlstm_ssd_fox_peer'


def _split(a, sizes):
    offs = [int(o) for o in np.cumsum(sizes)[:-1]]
    return jnp.split(a, offs, axis=-1)


def rmsnorm(x, g):
    xf = x.astype(jnp.float32)
    y = xf * lax.rsqrt(jnp.mean(xf * xf, axis=-1, keepdims=True) + EPS)
    return (y * g.astype(jnp.float32)).astype(x.dtype)


def causal_conv(x, w, b):
    y = lax.conv_general_dilated(x, w[:, None, :].astype(x.dtype), window_strides=(1,),
                                 padding=[(CONV_WIDTH - 1, 0)],
                                 dimension_numbers=('NWC', 'WIO', 'NWC'),
                                 feature_group_count=x.shape[-1])
    return y + b.astype(x.dtype)


def to_chunks(a, size):
    bsz, s = a.shape[0], a.shape[1]
    return jnp.moveaxis(a.reshape((bsz, s // size, size) + a.shape[2:]), 1, 0)


def from_chunks(a):
    a = jnp.moveaxis(a, 0, 1)
    return a.reshape((a.shape[0], a.shape[1] * a.shape[2]) + a.shape[3:])


def mlstm(q, k, v, i_pre, f_pre):
    f32 = jnp.float32
    bsz, _, nh, dk = q.shape
    dv = v.shape[-1]
    L = ML_CHUNK
    causal = jnp.tril(jnp.ones((L, L), dtype=bool))
    qs = to_chunks(q.astype(f32) * (dk ** -0.5), L)
    ks = to_chunks(k.astype(f32), L)
    vs = to_chunks(v.astype(f32), L)
    is_ = to_chunks(i_pre, L)
    lfs = to_chunks(jax.nn.log_sigmoid(f_pre), L)

    def step(carry, inp):
        cmem, nvec, m = carry
        qb, kb, vb, ib, lfb = inp
        ib = jnp.swapaxes(ib, 1, 2)
        b = jnp.cumsum(jnp.swapaxes(lfb, 1, 2), axis=-1)
        g = b[..., -1]
        dmat = jnp.where(causal, b[..., :, None] - b[..., None, :] + ib[..., None, :], -jnp.inf)
        inter = b + m[..., None]
        m_t = jnp.maximum(inter, jnp.max(dmat, axis=-1))
        sc = jnp.einsum('bthd,bshd->bhts', qb, kb) * jnp.exp(dmat - m_t[..., None])
        w_inter = jnp.swapaxes(jnp.exp(inter - m_t), 1, 2)
        num = (jnp.einsum('bhts,bshv->bthv', sc, vb)
               + w_inter[..., None] * jnp.einsum('bthd,bhdv->bthv', qb, cmem))
        den = jnp.swapaxes(jnp.sum(sc, axis=-1), 1, 2) + w_inter * jnp.einsum('bthd,bhd->bth', qb, nvec)
        floor = jnp.swapaxes(jnp.exp(-m_t), 1, 2)
        h = num / jnp.maximum(jnp.abs(den), floor)[..., None]
        a = g[..., None] - b + ib
        m_new = jnp.maximum(g + m, jnp.max(a, axis=-1))
        wa = jnp.exp(a - m_new[..., None])
        decay = jnp.exp(g + m - m_new)
        c_new = decay[..., None, None] * cmem + jnp.einsum('bhs,bshd,bshv->bhdv', wa, kb, vb)
        n_new = decay[..., None] * nvec + jnp.einsum('bhs,bshd->bhd', wa, kb)
        return (c_new, n_new, m_new), h

    carry0 = (jnp.zeros((bsz, nh, dk, dv), f32), jnp.zeros((bsz, nh, dk), f32), jnp.zeros((bsz, nh), f32))
    _, hs = lax.scan(step, carry0, (qs, ks, vs, is_, lfs))
    return from_chunks(hs)


def ssd(x, dt, a, bmat, cmat):
    f32 = jnp.float32
    bsz, _, ng, hg, pdim = x.shape
    nst = bmat.shape[-1]
    L = SSM_CHUNK
    causal = jnp.tril(jnp.ones((L, L), dtype=bool))
    xs = to_chunks(x.astype(f32) * dt[..., None], L)
    das = to_chunks(dt * a, L)
    bs = to_chunks(bmat.astype(f32), L)
    cs = to_chunks(cmat.astype(f32), L)

    def step(state, inp):
        xb, dab, bb, cb = inp
        acum = jnp.cumsum(jnp.transpose(dab, (0, 2, 3, 1)), axis=-1)
        lmat = jnp.exp(jnp.where(causal, acum[..., :, None] - acum[..., None, :], -jnp.inf))
        cbt = jnp.einsum('btgn,bsgn->bgts', cb, bb)
        y = jnp.einsum('bghts,bsghp->btghp', cbt[:, :, None] * lmat, xb)
        y = y + (jnp.einsum('btgn,bghnp->btghp', cb, state)
                 * jnp.transpose(jnp.exp(acum), (0, 3, 1, 2))[..., None])
        decay = jnp.exp(acum[..., -1:] - acum)
        new_state = (jnp.exp(acum[..., -1])[..., None, None] * state
                     + jnp.einsum('bsgn,bghs,bsghp->bghnp', bb, decay, xb))
        return new_state, y

    state0 = jnp.zeros((bsz, ng, hg, nst, pdim), f32)
    _, ys = lax.scan(step, state0, (xs, das, bs, cs))
    return from_chunks(ys)


def forgetting_attention(q, k, v, f_pre):
    f32 = jnp.float32
    bsz, s, nh, d = q.shape
    nb = s // FOX_QBLOCK
    fcum = jnp.cumsum(jax.nn.log_sigmoid(f_pre.astype(f32)), axis=1)
    fk = jnp.swapaxes(fcum, 1, 2)
    kf = k.astype(f32) * (d ** -0.5)
    vf = v.astype(f32)
    kpos = jnp.arange(s)

    def block(args):
        qb, fqb, start = args
        logits = (jnp.einsum('bthd,bshd->bhts', qb, kf)
                  + jnp.swapaxes(fqb, 1, 2)[..., None] - fk[:, :, None, :])
        qpos = start + jnp.arange(FOX_QBLOCK)
        logits = jnp.where(kpos[None, :] <= qpos[:, None], logits, -jnp.inf)
        probs = jax.nn.softmax(logits, axis=-1)
        return jnp.einsum('bhts,bshd->bthd', probs, vf)

    out = lax.map(block, (to_chunks(q.astype(f32), FOX_QBLOCK), to_chunks(fcum, FOX_QBLOCK),
                          jnp.arange(nb) * FOX_QBLOCK))
    return from_chunks(out)


def peer(h, w_q, keys1, keys2, u, v):
    f32 = jnp.float32
    bsz, s, d = h.shape
    t = bsz * s
    hf = h.reshape(t, d)
    q = (hf @ w_q).astype(f32).reshape(t, PEER_HEADS, 2, PEER_QDIM // 2)
    s1 = jnp.einsum('thd,hkd->thk', q[:, :, 0], keys1.astype(f32))
    s2 = jnp.einsum('thd,hkd->thk', q[:, :, 1], keys2.astype(f32))
    v1, i1 = lax.top_k(s1, PEER_TOPK)
    v2, i2 = lax.top_k(s2, PEER_TOPK)
    cand_s = (v1[..., :, None] + v2[..., None, :]).reshape(t, PEER_HEADS, PEER_TOPK * PEER_TOPK)
    cand_i = (i1[..., :, None] * PEER_NKEYS + i2[..., None, :]).reshape(t, PEER_HEADS, PEER_TOPK * PEER_TOPK)
    top_s, pos = lax.top_k(cand_s, PEER_TOPK)
    idx = jnp.take_along_axis(cand_i, pos, axis=-1)
    gates = jax.nn.softmax(top_s, axis=-1).astype(h.dtype)
    nblk = t // PEER_TOKEN_BLOCK

    def block(args):
        xb, ib, gb = args
        act = jax.nn.gelu(jnp.einsum('td,thkd->thk', xb, u[ib]), approximate=False)
        return jnp.einsum('thk,thkd->td', act * gb, v[ib])

    out = lax.map(block, (hf.reshape(nblk, PEER_TOKEN_BLOCK, d),
                          idx.reshape(nblk, PEER_TOKEN_BLOCK, PEER_HEADS, PEER_TOPK),
                          gates.reshape(nblk, PEER_TOKEN_BLOCK, PEER_HEADS, PEER_TOPK)))
    return out.reshape(bsz, s, d)


def hybrid_mixer(h, w_in, ml_conv_w, ml_conv_b, ml_b_i, ml_b_f, ml_norm_g,
                 ssm_conv_w, ssm_conv_b, ssm_dt_bias, ssm_a_log, ssm_d, ssm_norm_g,
                 fox_q_norm_g, fox_k_norm_g, fox_b_f,
                 w_branch_ml, w_branch_ssm, w_branch_fox, w_out):
    f32 = jnp.float32
    bsz, s, _ = h.shape
    (ml_q, ml_k, ml_v, ml_o, ml_i, ml_f, s_z, s_x, s_b, s_c, s_dt,
     f_q, f_k, f_v, f_f, g_ml, g_ssm, g_fox) = _split(h @ w_in, IN_SPLITS)

    qk = jax.nn.silu(causal_conv(jnp.concatenate([ml_q, ml_k], axis=-1), ml_conv_w, ml_conv_b))
    q, k = jnp.split(qk, 2, axis=-1)
    hm = mlstm(q.reshape(bsz, s, ML_HEADS, ML_DQK), k.reshape(bsz, s, ML_HEADS, ML_DQK),
               ml_v.reshape(bsz, s, ML_HEADS, ML_DV),
               ml_i.astype(f32) + ml_b_i.astype(f32), ml_f.astype(f32) + ml_b_f.astype(f32))
    hm = rmsnorm(hm, ml_norm_g.reshape(ML_HEADS, ML_DV)).reshape(bsz, s, ML_W)
    y_ml = (jax.nn.sigmoid(ml_o.astype(f32)) * hm).astype(h.dtype)

    hg = SSM_HEADS // SSM_GROUPS
    xbc = jax.nn.silu(causal_conv(jnp.concatenate([s_x, s_b, s_c], axis=-1), ssm_conv_w, ssm_conv_b))
    sx, sb, sc = _split(xbc, (SSM_W, SSM_BC_W, SSM_BC_W))
    sx = sx.reshape(bsz, s, SSM_GROUPS, hg, SSM_HEADDIM)
    dt = jax.nn.softplus(s_dt.astype(f32) + ssm_dt_bias.astype(f32)).reshape(bsz, s, SSM_GROUPS, hg)
    a = -jnp.exp(ssm_a_log.astype(f32)).reshape(SSM_GROUPS, hg)
    ys = ssd(sx, dt, a, sb.reshape(bsz, s, SSM_GROUPS, SSM_STATE), sc.reshape(bsz, s, SSM_GROUPS, SSM_STATE))
    ys = ys + ssm_d.astype(f32).reshape(SSM_GROUPS, hg)[..., None] * sx.astype(f32)
    ys = ys.reshape(bsz, s, SSM_W) * jax.nn.silu(s_z.astype(f32))
    y_ssm = rmsnorm(ys.reshape(bsz, s, SSM_GROUPS, SSM_W // SSM_GROUPS),
                    ssm_norm_g.reshape(SSM_GROUPS, SSM_W // SSM_GROUPS)).reshape(bsz, s, SSM_W).astype(h.dtype)

    fq = rmsnorm(f_q.reshape(bsz, s, FOX_HEADS, FOX_HEADDIM), fox_q_norm_g)
    fk = rmsnorm(f_k.reshape(bsz, s, FOX_HEADS, FOX_HEADDIM), fox_k_norm_g)
    y_fox = forgetting_attention(fq, fk, f_v.reshape(bsz, s, FOX_HEADS, FOX_HEADDIM),
                                 f_f.astype(f32) + fox_b_f.astype(f32))
    y_fox = y_fox.reshape(bsz, s, FOX_W).astype(h.dtype)

    merged = (jax.nn.sigmoid(g_ml) * (y_ml @ w_branch_ml)
              + jax.nn.sigmoid(g_ssm) * (y_ssm @ w_branch_ssm)
              + jax.nn.sigmoid(g_fox) * (y_fox @ w_branch_fox))
    return merged @ w_out


def setup_inputs(seed: int = 0) -> dict:
    key = jax.random.key(seed)
    ks = iter(jax.random.split(key, 48))
    f32 = jnp.float32

    def nrm(shape, scale):
        return jax.random.normal(next(ks), shape, f32) * scale

    def gain(shape):
        return 1.0 + nrm(shape, 0.02)

    dt0 = jnp.exp(jax.random.uniform(next(ks), (DEPTH, SSM_HEADS), f32, math.log(1e-3), math.log(1e-1)))
    return {
        'x': nrm((BATCH, SEQ, D_MODEL), 1.0),
        'p': nrm((DEPTH, BATCH, SEQ, PLE_DIM), 1.0),
        'norm_mix_g': gain((DEPTH, D_MODEL)),
        'w_in': nrm((DEPTH, D_MODEL, D_IN), D_MODEL ** -0.5),
        'ml_conv_w': nrm((DEPTH, CONV_WIDTH, 2 * ML_QK_W), CONV_WIDTH ** -0.5),
        'ml_conv_b': nrm((DEPTH, 2 * ML_QK_W), 0.02),
        'ml_b_i': nrm((DEPTH, ML_HEADS), 0.1),
        'ml_b_f': jnp.linspace(3.0, 6.0, ML_HEADS, dtype=f32)[None, :] + nrm((DEPTH, ML_HEADS), 0.1),
        'ml_norm_g': gain((DEPTH, ML_W)),
        'ssm_conv_w': nrm((DEPTH, CONV_WIDTH, SSM_W + 2 * SSM_BC_W), CONV_WIDTH ** -0.5),
        'ssm_conv_b': nrm((DEPTH, SSM_W + 2 * SSM_BC_W), 0.02),
        'ssm_dt_bias': dt0 + jnp.log(-jnp.expm1(-dt0)),
        'ssm_a_log': jnp.log(jax.random.uniform(next(ks), (DEPTH, SSM_HEADS), f32, 1.0, 16.0)),
        'ssm_d': gain((DEPTH, SSM_HEADS)),
        'ssm_norm_g': gain((DEPTH, SSM_W)),
        'fox_q_norm_g': gain((DEPTH, FOX_HEADDIM)),
        'fox_k_norm_g': gain((DEPTH, FOX_HEADDIM)),
        'fox_b_f': jnp.linspace(1.0, 5.0, FOX_HEADS, dtype=f32)[None, :] + nrm((DEPTH, FOX_HEADS), 0.1),
        'w_branch_ml': nrm((DEPTH, ML_W, D_MODEL), ML_W ** -0.5),
        'w_branch_ssm': nrm((DEPTH, SSM_W, D_MODEL), SSM_W ** -0.5),
        'w_branch_fox': nrm((DEPTH, FOX_W, D_MODEL), FOX_W ** -0.5),
        'w_out': nrm((DEPTH, D_MODEL, D_MODEL), D_MODEL ** -0.5),
        'norm_ffn_g': gain((DEPTH, D_MODEL)),
        'peer_w_q': nrm((DEPTH, D_MODEL, PEER_HEADS * PEER_QDIM), D_MODEL ** -0.5),
        'peer_keys1': nrm((DEPTH, PEER_HEADS, PEER_NKEYS, PEER_QDIM // 2), (PEER_QDIM // 2) ** -0.5),
        'peer_keys2': nrm((DEPTH, PEER_HEADS, PEER_NKEYS, PEER_QDIM // 2), (PEER_QDIM // 2) ** -0.5),
        'peer_u': nrm((DEPTH, PEER_NEXPERTS, D_MODEL), D_MODEL ** -0.5),
        'peer_v': nrm((DEPTH, PEER_NEXPERTS, D_MODEL), (PEER_HEADS * PEER_TOPK) ** -0.5),
        'norm_ple_g': gain((DEPTH, D_MODEL)),
        'ple_w_gate': nrm((DEPTH, D_MODEL, D_MODEL), D_MODEL ** -0.5),
        'ple_w_proj': nrm((DEPTH, PLE_DIM, D_MODEL), PLE_DIM ** -0.5),
        'final_norm_g': gain((D_MODEL,)),
    }


def reference(x, p, norm_mix_g, w_in, ml_conv_w, ml_conv_b, ml_b_i, ml_b_f, ml_norm_g,
              ssm_conv_w, ssm_conv_b, ssm_dt_bias, ssm_a_log, ssm_d, ssm_norm_g,
              fox_q_norm_g, fox_k_norm_g, fox_b_f, w_branch_ml, w_branch_ssm, w_branch_fox, w_out,
              norm_ffn_g, peer_w_q, peer_keys1, peer_keys2, peer_u, peer_v,
              norm_ple_g, ple_w_gate, ple_w_proj, final_norm_g):
    for i in range(DEPTH):
        h = rmsnorm(x, norm_mix_g[i])
        x = x + hybrid_mixer(h, w_in[i], ml_conv_w[i], ml_conv_b[i], ml_b_i[i], ml_b_f[i], ml_norm_g[i],
                             ssm_conv_w[i], ssm_conv_b[i], ssm_dt_bias[i], ssm_a_log[i], ssm_d[i], ssm_norm_g[i],
                             fox_q_norm_g[i], fox_k_norm_g[i], fox_b_f[i],
                             w_branch_ml[i], w_branch_ssm[i], w_branch_fox[i], w_out[i])
        x = x + peer(rmsnorm(x, norm_ffn_g[i]), peer_w_q[i], peer_keys1[i], peer_keys2[i], peer_u[i], peer_v[i])
        x = x + jax.nn.sigmoid(rmsnorm(x, norm_ple_g[i]) @ ple_w_gate[i]) * (p[i] @ ple_w_proj[i])
    return rmsnorm(x, final_norm_g)
```

```python
import os
import contextlib
import numpy as np
import concourse.bass as bass
import concourse.mybir as mybir
from concourse.bass_utils import run_bass_kernel_spmd

F32 = mybir.dt.float32
BF16 = mybir.dt.bfloat16
U32 = mybir.dt.uint32
AF = mybir.ActivationFunctionType
ALU = mybir.AluOpType
AX = mybir.AxisListType


class Buf:
    def __init__(self, K, name, ap_fn):
        self.K = K
        self.name = name
        self.ap_fn = ap_fn
        self.writes = {}
        self.reads = {}
        self.dsem = None
        self.dcnt = 0
        self.is_dram = False
        self.multi = False

    def __getitem__(self, idx):
        return Ref(self, self.ap_fn[idx])


class Ref:
    def __init__(self, buf, ap):
        self.buf = buf
        self.ap = ap

    @property
    def shape(self):
        return self.ap.shape

    def __getitem__(self, idx):
        return Ref(self.buf, self.ap[idx])

    def bc(self, shape):
        return Ref(self.buf, self.ap.to_broadcast(list(shape)))

    def unsq(self, axis):
        return Ref(self.buf, self.ap.unsqueeze(axis))

    def re(self, pattern_, **kw):
        return Ref(self.buf, self.ap.rearrange(pattern_, **kw))

    def pbc(self, n):
        return Ref(self.buf, self.ap.partition_broadcast(n))

    def bitcast(self, dt):
        return Ref(self.buf, self.ap.bitcast(dt))


class K:
    ENG = ('pe', 'dve', 'act', 'pool', 'sp')

    def __init__(self, nc, same_engine_sync=True):
        self.nc = nc
        self.es = contextlib.ExitStack()
        self.engobj = {'pe': nc.tensor, 'dve': nc.vector, 'act': nc.scalar,
                       'pool': nc.gpsimd, 'sp': nc.sync}
        self.sems = {}
        self.cnt = {}
        for e in self.ENG:
            self.sems[e] = self.es.enter_context(nc.semaphore("s_" + e))
            self.cnt[e] = 0
        self.waited = {e: {} for e in self.ENG}
        self.prog = {e: [] for e in self.ENG}
        self.same = same_engine_sync
        self.nbuf = 0
        self.ninst = 0
        self.dmax = {}
        self.coll_inc = False
        self.live = []
        self.free_sems = []

    def sb(self, name, shape, dt):
        t = self.es.enter_context(self.nc.sbuf_tensor(name, list(shape), dt))
        return Buf(self, name, t)

    def ps(self, name, shape, dt):
        t = self.es.enter_context(self.nc.psum_tensor(name, list(shape), dt))
        return Buf(self, name, t)

    def dram(self, name, shape, dt, kind):
        t = self.nc.dram_tensor(name, list(shape), dt, kind=kind)
        b = Buf(self, name, t.ap())
        b.is_dram = True
        b.multi = True
        return b

    def init_arena(self, nbytes=188 * 1024):
        self.arena_t = self.es.enter_context(self.nc.sbuf_tensor("arena", [128, nbytes // 4], F32))
        self.arena_n = nbytes // 4
        self.arena_off = 0
        self.banks = [self.ps("bank%d" % i, [128, 512], F32) for i in range(8)]

    def al(self, name, shape, dt):
        esz = 2 if dt == BF16 else 4
        n = 1
        for s_ in shape[1:]:
            n *= s_
        words = (n * esz + 3) // 4
        words = (words + 15) // 16 * 16
        assert self.arena_off + words <= self.arena_n, ("arena overflow", name, self.arena_off, words)
        ap = self.arena_t[0:shape[0], self.arena_off:self.arena_off + words]
        off0 = self.arena_off
        self.arena_off += words
        if dt != F32:
            ap = ap.bitcast(dt)
        ap = ap[:, 0:n]
        if len(shape) > 2:
            names = " ".join("d%d" % i for i in range(1, len(shape)))
            kw = {"d%d" % i: shape[i] for i in range(1, len(shape))}
            ap = ap.rearrange("p (%s) -> p %s" % (names, names), **kw)
        b = Buf(self, name, ap)
        self.live.append((off0, b))
        return b

    def bank(self, i, shape, dt=None):
        b = self.banks[i]
        ap = b.ap_fn[:, :]
        if dt is not None and dt != F32:
            ap = ap.bitcast(dt)
        n = 1
        for s_ in shape[1:]:
            n *= s_
        ap = ap[0:shape[0], 0:n]
        if len(shape) > 2:
            names = " ".join("d%d" % i for i in range(1, len(shape)))
            kw = {"d%d" % i: shape[i] for i in range(1, len(shape))}
            ap = ap.rearrange("p (%s) -> p %s" % (names, names), **kw)
        return Ref(b, ap)

    def mark(self):
        return self.arena_off

    def release(self, m):
        self.barrier()
        keep = []
        for off, b in self.live:
            if off >= m:
                if b.dsem is not None:
                    self.free_sems.append(b.dsem)
                    b.dsem = None
            else:
                keep.append((off, b))
        self.live = keep
        self.arena_off = m

    def barrier(self):
        snap = dict(self.cnt)
        for key, sem in self.sems.items():
            if key not in snap:
                snap[key] = None
        dvals = {}
        for key in self.sems:
            if key not in self.cnt:
                dvals[key] = self.dmax.get(key, 0)
        for e in self.ENG:
            waits = []
            for key, v in list(snap.items()):
                v = self.cnt[key] if key in self.cnt else dvals[key]
                if key == e or v == 0:
                    continue
                if self.waited[e].get(key, 0) >= v:
                    continue
                self.waited[e][key] = v
                waits.append((key, v))
            self._emit_waits(e, waits)

    def _need(self, eng, reads, writes, skip_waw=None):
        need = {}
        for b in reads:
            for k, v in b.writes.items():
                if need.get(k, 0) < v:
                    need[k] = v
        for b in writes:
            if not b.multi:
                for k, v in b.writes.items():
                    if k == skip_waw:
                        continue
                    if need.get(k, 0) < v:
                        need[k] = v
            for k, v in b.reads.items():
                if need.get(k, 0) < v:
                    need[k] = v
        out = []
        for k, v in need.items():
            if k == eng and (eng == 'pe' or not self.same):
                continue
            if self.waited[eng].get(k, 0) >= v:
                continue
            self.waited[eng][k] = v
            out.append((k, v))
        return out

    def _emit_waits(self, eng, waits):
        for k, v in waits:
            sem = self.sems[k]
            self.prog[eng].append(lambda e, sem=sem, v=v: e.wait_ge(sem, v))

    def _mark(self, key, val, reads, writes):
        for b in writes:
            if b.multi:
                if b.writes.get(key, 0) < val:
                    b.writes[key] = val
            else:
                b.writes = {key: val}
                b.reads = {}
        for b in reads:
            if b.reads.get(key, 0) < val:
                b.reads[key] = val

    def op(self, eng, fn, reads=(), writes=(), inc=True):
        waits = self._need(eng, reads, writes)
        self._emit_waits(eng, waits)
        val = self.cnt[eng] + 1
        if inc:
            self.cnt[eng] = val
            sem = self.sems[eng]
            self.prog[eng].append(lambda e, fn=fn, sem=sem: fn(e).then_inc(sem, 1))
        else:
            self.prog[eng].append(lambda e, fn=fn: fn(e))
        self._mark(eng, val, reads, writes)
        self.ninst += 1

    WRITE_KW = ('out', 'accum_out', 'out_max', 'out_indices', 'ap')

    def ins(self, eng, method, inc=True, **kw):
        reads, writes, args = [], [], {}
        for n, v in kw.items():
            if isinstance(v, Ref):
                (writes if n in self.WRITE_KW else reads).append(v.buf)
                args[n] = v.ap
            else:
                args[n] = v
        self.op(eng, lambda e, m=method, a=args: getattr(e, m)(**a), reads, writes, inc=inc)

    def mm(self, out, lhsT, rhs, start=True, stop=True, inc=True):
        self.op('pe', lambda e, o=out.ap, l=lhsT.ap, r=rhs.ap: e.matmul(o, lhsT=l, rhs=r, start=start, stop=stop),
                [lhsT.buf, rhs.buf], [out.buf], inc=inc)

    def tr(self, out, in_, ident):
        self.op('pe', lambda e, o=out.ap, i=in_.ap, d=ident.ap: e.transpose(out=o, in_=i, identity=d),
                [in_.buf, ident.buf], [out.buf])

    def ld(self, q, out, in_, **kw):
        self.dma(q, out.buf, out.ap, in_.buf, in_.ap, **kw)

    def dma(self, q, out_buf, out_ap, in_buf, in_ap, **kw):
        own = in_buf if out_buf.is_dram else out_buf
        if own.dsem is None:
            if self.free_sems:
                key = self.free_sems.pop()
            else:
                key = "d%d" % self.nbuf
                self.nbuf += 1
                self.sems[key] = self.es.enter_context(self.nc.semaphore(key))
            own.dsem = key
            own.dcnt = self.dmax.get(key, 0)
        key = own.dsem
        waits = self._need(q, [in_buf], [out_buf], skip_waw=key)
        self._emit_waits(q, waits)
        own.dcnt += 16
        val = own.dcnt
        self.dmax[key] = val
        sem = self.sems[key]
        self.prog[q].append(lambda e, o=out_ap, i=in_ap, sem=sem, kw=kw: e.dma_start(out=o, in_=i, **kw).then_inc(sem, 16))
        self._mark(key, val, [in_buf], [out_buf])
        self.ninst += 1

    def coll(self, kind, out, in_, groups, op=None):
        key = "coll"
        if key not in self.sems:
            self.sems[key] = self.es.enter_context(self.nc.semaphore(key))
            self.dmax[key] = 0
        waits = self._need('pool', [in_.buf], [out.buf])
        self._emit_waits('pool', waits)
        self.dmax[key] += 1
        val = self.dmax[key]
        sem = self.sems[key]
        aop = ALU.bypass if op is None else op
        self.prog['pool'].append(lambda e, o=out.ap.opt(), i=in_.ap.opt(), sem=sem: e.collective_compute(
            kind, aop, replica_groups=groups, ins=[i], outs=[o]).then_inc(sem, 1))
        self._mark(key, val, [in_.buf], [out.buf])
        self.ninst += 1

    def final_wait(self, eng, bufs):
        waits = self._need(eng, bufs, bufs)
        self._emit_waits(eng, waits)

    def emit(self):
        with self.nc.Block() as block:
            names = {'pe': 'tensor', 'dve': 'vector', 'act': 'scalar', 'pool': 'gpsimd', 'sp': 'sync'}
            for e in self.ENG:
                lst = self.prog[e]
                if not lst:
                    continue

                def run(engine, lst=lst):
                    for th in lst:
                        th(engine)
                getattr(block, names[e])(run)
        self.es.close()


T = 8192
D = 1024
NCM = 896
NTM = 1540
EPS = 1e-6
NG = T // 512

OFF = {}
_names = ['ml_q', 'ml_k', 'ml_v', 'ml_o', 'ml_i', 'ml_f', 's_z', 's_x', 's_b', 's_c', 's_dt',
          'f_q', 'f_k', 'f_v', 'f_f', 'g_ml', 'g_ssm', 'g_fox']
_sizes = [512, 512, 1024, 1024, 8, 8, 1024, 1024, 256, 256, 16, 1024, 1024, 1024, 8, 1024, 1024, 1024]
_o = 0
for _n, _s in zip(_names, _sizes):
    OFF[_n] = _o
    _o += _s
D_IN = _o


def prep_ab(inp, layer, xfull):
    w = inp['w_in'][layer]
    maps = []
    for c in range(8):
        b, g = c // 4, c % 4
        sg = g // 2
        wcm = np.zeros((D, NCM), np.float32)
        wcm[:, 0:128] = w[:, OFF['ml_q'] + 128 * g: OFF['ml_q'] + 128 * g + 128]
        wcm[:, 128:256] = w[:, OFF['ml_k'] + 128 * g: OFF['ml_k'] + 128 * g + 128]
        wcm[:, 256:512] = w[:, OFF['s_x'] + 256 * g: OFF['s_x'] + 256 * g + 256]
        wcm[:, 512:640] = w[:, OFF['s_b'] + 128 * sg: OFF['s_b'] + 128 * sg + 128]
        wcm[:, 640:768] = w[:, OFF['s_c'] + 128 * sg: OFF['s_c'] + 128 * sg + 128]
        wcm[:, 768:770] = w[:, OFF['ml_i'] + 2 * g: OFF['ml_i'] + 2 * g + 2]
        wcm[:, 800:802] = w[:, OFF['ml_f'] + 2 * g: OFF['ml_f'] + 2 * g + 2]
        wcm[:, 832:834] = w[:, OFF['f_f'] + 2 * g: OFF['f_f'] + 2 * g + 2]
        wcm[:, 864:868] = w[:, OFF['s_dt'] + 4 * g: OFF['s_dt'] + 4 * g + 4]
        wtm = np.concatenate([
            w[:, OFF['ml_v'] + 256 * g: OFF['ml_v'] + 256 * g + 256],
            w[:, OFF['ml_o'] + 256 * g: OFF['ml_o'] + 256 * g + 256],
            w[:, OFF['s_z'] + 256 * g: OFF['s_z'] + 256 * g + 256],
            w[:, OFF['f_q'] + 256 * g: OFF['f_q'] + 256 * g + 256],
            w[:, OFF['f_k'] + 256 * g: OFF['f_k'] + 256 * g + 256],
            w[:, OFF['f_v'] + 256 * g: OFF['f_v'] + 256 * g + 256],
            w[:, OFF['s_dt'] + 4 * g: OFF['s_dt'] + 4 * g + 4]], axis=1)
        mcw, mcb = inp['ml_conv_w'][layer], inp['ml_conv_b'][layer]
        scw, scb = inp['ssm_conv_w'][layer], inp['ssm_conv_b'][layer]
        chans_w = np.concatenate([
            mcw[:, 128 * g:128 * g + 128], mcw[:, 512 + 128 * g: 512 + 128 * g + 128],
            scw[:, 256 * g:256 * g + 256], scw[:, 1024 + 128 * sg:1024 + 128 * sg + 128],
            scw[:, 1280 + 128 * sg:1280 + 128 * sg + 128]], axis=1)
        chans_b = np.concatenate([
            mcb[128 * g:128 * g + 128], mcb[512 + 128 * g: 512 + 128 * g + 128],
            scb[256 * g:256 * g + 256], scb[1024 + 128 * sg:1024 + 128 * sg + 128],
            scb[1280 + 128 * sg:1280 + 128 * sg + 128]])
        cw = np.ascontiguousarray(chans_w.T.reshape(6, 128, 4).transpose(1, 0, 2))
        cb = np.ascontiguousarray(chans_b.reshape(6, 128).T)
        m = {
            'x': np.ascontiguousarray(xfull[b]),
            'ng': np.ascontiguousarray(inp['norm_mix_g'][layer].reshape(1, D)),
            'wcm': wcm, 'wtm': np.ascontiguousarray(wtm), 'cw': cw, 'cb': cb,
            'sprm': np.ascontiguousarray(np.concatenate([inp['ssm_dt_bias'][layer][4 * g:4 * g + 4], inp['ssm_a_log'][layer][4 * g:4 * g + 4], inp['ssm_d'][layer][4 * g:4 * g + 4]]).reshape(1, 12).astype(np.float32)),
            'mprm': np.ascontiguousarray(np.concatenate([inp['ml_b_i'][layer][2 * g:2 * g + 2], inp['ml_b_f'][layer][2 * g:2 * g + 2]]).reshape(1, 4).astype(np.float32)),
            'mlg': np.ascontiguousarray(inp['ml_norm_g'][layer][256 * g:256 * g + 256].reshape(1, 256)),
            'fqg': np.ascontiguousarray(inp['fox_q_norm_g'][layer].reshape(1, 128)),
            'fkg': np.ascontiguousarray(inp['fox_k_norm_g'][layer].reshape(1, 128)),
            'fbf': np.ascontiguousarray(inp['fox_b_f'][layer][2 * g:2 * g + 2].reshape(1, 2)),
        }
        maps.append(m)
    return maps


def make_ident(k, name="ident"):
    idf = k.al(name + "_f", [128, 128], F32)
    idb = k.al(name + "_b", [128, 128], BF16)
    k.ins('pool', 'memset', ap=idf[:], constant=0.0)
    k.op('pool', lambda e: e.affine_select(out=idf[:].ap, in_=idf[:].ap, pattern=[[-1, 128]],
                                           compare_op=ALU.not_equal, fill=1.0, base=0, channel_multiplier=1),
         [idf], [idf])
    k.ins('dve', 'tensor_copy', out=idb[:], in_=idf[:])
    return idf, idb


def phase_p(k, x, ng, wcm, wtm, cw, cb, zcm, zg, ztm, idb, ngroups=NG):
    ngrep = k.al("ngrep", [128, D], F32)
    k.ld('sp', ngrep[:], ng[0:1, :].pbc(128))
    wcm_sb = k.al("wcm_sb", [128, 8, NCM], BF16)
    wtm_sb = k.al("wtm_sb", [128, 8, NTM], BF16)
    wst = [k.al("wst%d" % i, [128, NTM], F32) for i in range(2)]
    for kc in range(8):
        st = wst[kc % 2]
        k.ld('sp', st[:, 0:NCM], wcm[kc * 128:(kc + 1) * 128, :])
        k.ins('act', 'copy', out=wcm_sb[:, kc, :], in_=st[:, 0:NCM])
    for kc in range(8):
        st = wst[kc % 2]
        k.ld('sp', st[:, :], wtm[kc * 128:(kc + 1) * 128, :])
        k.ins('dve', 'tensor_copy', out=wtm_sb[:, kc, :], in_=st[:, :])
    cw_sb = k.al("cw_sb", [128, 6, 4], F32)
    cb_sb = k.al("cb_sb", [128, 6], F32)
    k.ld('sp', cw_sb[:], cw[:, :, :])
    k.ld('sp', cb_sb[:], cb[:, :])
    xb = [k.al("xb%d" % i, [128, D], F32) for i in range(4)]
    junk = k.al("junk", [128, D], BF16)
    ss = k.al("ss", [128, 8], F32)
    hn = [k.al("hn%d" % i, [128, D], BF16) for i in range(2)]
    hT = [k.al("hT%d" % i, [128, 8, 512], BF16) for i in range(2)]
    zc = [k.al("zc%d" % i, [128, 515], F32) for i in range(2)]
    halo = k.al("halo", [128, 6, 3], F32)
    k.ins('pool', 'memset', ap=halo[:], constant=0.0)
    acc = [k.al("acc%d" % i, [128, 512], F32) for i in range(2)]
    ob = [k.al("ob%d" % i, [128, 512], BF16) for i in range(2)]
    gsb = [k.al("gsb%d" % i, [128, 512], F32) for i in range(2)]
    tmsb = [k.al("tmsb%d" % i, [128, NTM], F32) for i in range(2)]
    rot = 0
    nt = 0
    for tg in range(ngroups):
        h_T = hT[tg % 2]
        for tt in range(4):
            r0 = tg * 512 + tt * 128
            xt = xb[tt]
            k.ld('sp', xt[:], (x(r0) if callable(x) else x[r0:r0 + 128, :]))
            k.ins('act', 'activation', out=junk[:], in_=xt[:], func=AF.Square, accum_out=ss[:, tt:tt + 1])
        k.ins('dve', 'tensor_scalar', out=ss[:, 0:4], in0=ss[:, 0:4], scalar1=1.0 / D, scalar2=EPS, op0=ALU.mult, op1=ALU.add)
        k.ins('dve', 'reciprocal', out=ss[:, 0:4], in_=ss[:, 0:4])
        k.ins('act', 'activation', out=ss[:, 4:8], in_=ss[:, 0:4], func=AF.Sqrt)
        for tt in range(4):
            xt = xb[tt]
            hb = hn[nt % 2]
            k.ins('dve', 'scalar_tensor_tensor', out=hb[:], in0=xt[:], scalar=ss[:, 4 + tt:5 + tt], in1=ngrep[:], op0=ALU.mult, op1=ALU.mult)
            p_T = k.bank(nt % 2, [128, 8, 128], BF16)
            for kc in range(8):
                k.tr(p_T[:, kc, :], hb[:, kc * 128:(kc + 1) * 128], idb[:])
            k.ins('act', 'copy', out=h_T[:, :, tt * 128:(tt + 1) * 128], in_=p_T)
            nt += 1
        c0 = tg * 512
        for cc in range(7):
            pc = k.bank(2 + rot % 4, [128, 512])
            rot += 1
            for kc in range(8):
                k.mm(pc[:, :], wcm_sb[:, kc, cc * 128:(cc + 1) * 128], h_T[:, kc, :], start=(kc == 0), stop=(kc == 7), inc=(kc == 7))
            if cc < 6:
                z = zc[cc % 2]
                k.ins('act', 'copy', out=z[:, 3:515], in_=pc[:, :])
                k.ins('dve', 'tensor_copy', out=z[:, 0:3], in_=halo[:, cc, :])
                k.ins('dve', 'tensor_copy', out=halo[:, cc, :], in_=z[:, 512:515])
                a = acc[cc % 2]
                k.ins('dve', 'tensor_scalar', out=a[:], in0=z[:, 0:512], scalar1=cw_sb[:, cc, 0:1], scalar2=cb_sb[:, cc:cc + 1], op0=ALU.mult, op1=ALU.add)
                for tap in (1, 2, 3):
                    k.ins('dve', 'scalar_tensor_tensor', out=a[:], in0=z[:, tap:tap + 512], scalar=cw_sb[:, cc, tap:tap + 1], in1=a[:], op0=ALU.mult, op1=ALU.add)
                o = ob[cc % 2]
                k.ins('act', 'activation', out=o[:], in_=a[:], func=AF.Silu)
                k.ld('sp', zcm[cc * 128:(cc + 1) * 128, c0:c0 + 512], o[:])
            else:
                gs = gsb[tg % 2]
                k.ins('act', 'copy', out=gs[:], in_=pc[:, :])
                k.ld('sp', zg[:, c0:c0 + 512], gs[:, :])
        for tt in range(4):
            tm = tmsb[tt % 2]
            for ci, (a0, a1) in enumerate(((0, 512), (512, 1024), (1024, 1536), (1536, 1540))):
                pc = k.bank(2 + rot % 4, [128, 512])
                rot += 1
                for kc in range(8):
                    k.mm(pc[:, 0:a1 - a0], h_T[:, kc, tt * 128:(tt + 1) * 128], wtm_sb[:, kc, a0:a1], start=(kc == 0), stop=(kc == 7), inc=(kc == 7))
                if ci % 2 == 0:
                    k.ins('act', 'copy', out=tm[:, a0:a1], in_=pc[:, 0:a1 - a0])
                else:
                    k.ins('dve', 'tensor_copy', out=tm[:, a0:a1], in_=pc[:, 0:a1 - a0])
            r0 = tg * 512 + tt * 128
            k.ld('sp', ztm[r0:r0 + 128, :], tm[:])


def build_ab(dbg=True, phases=('p',), ngroups=NG, nq=16, stop=9):
    nc = bass.Bass("TRN2", target_bir_lowering=False)
    k = K(nc)
    kS = "ExternalOutput" if dbg else "Internal"
    x = k.dram("x", [T, D], F32, "ExternalInput")
    ng = k.dram("ng", [1, D], F32, "ExternalInput")
    wcm = k.dram("wcm", [D, NCM], F32, "ExternalInput")
    wtm = k.dram("wtm", [D, NTM], F32, "ExternalInput")
    cw = k.dram("cw", [128, 6, 4], F32, "ExternalInput")
    cb = k.dram("cb", [128, 6], F32, "ExternalInput")
    fqg = k.dram("fqg", [1, 128], F32, "ExternalInput")
    fkg = k.dram("fkg", [1, 128], F32, "ExternalInput")
    fbf = k.dram("fbf", [1, 2], F32, "ExternalInput")
    yfox = k.dram("yfox", [T, 256], F32, "ExternalOutput")
    sprm = k.dram("sprm", [1, 12], F32, "ExternalInput")
    ysd = k.dram("ysd", [T, 256], F32, "ExternalOutput")
    mprm = k.dram("mprm", [1, 4], F32, "ExternalInput")
    mlg = k.dram("mlg", [1, 256], F32, "ExternalInput")
    yml = k.dram("yml", [T, 256], F32, "ExternalOutput")
    zcm = k.dram("zcm", [768, T], BF16, kS)
    zg = k.dram("zg", [128, T], F32, kS)
    ztm = k.dram("ztm", [T, NTM], F32, kS)
    k.init_arena()
    idf, idb = make_ident(k)
    outs = []
    if 'p' in phases:
        mp = k.mark()
        phase_p(k, x, ng, wcm, wtm, cw, cb, zcm, zg, ztm, idb, ngroups)
        outs += [zcm, zg, ztm]
        k.release(mp)
    if 'm' in phases:
        phase_m(k, zcm, zg, ztm, mprm, mlg, yml, idf, idb, nch=min(64, 4 * nq), stop=stop)
        outs += [yml]
    if 's' in phases:
        phase_s(k, zcm, zg, ztm, sprm, ysd, idf, idb, nch=min(64, 4 * nq))
        outs += [ysd]
    if 'f' in phases:
        phase_f(k, ztm, zg, fqg, fkg, fbf, yfox, idf, idb, nq, stop)
        outs += [yfox]
    k.final_wait('sp', outs)
    k.emit()
    print("instructions:", k.ninst, "sems:", len(k.sems))
    return nc


def phase_f(k, ztm, zg, fqg, fkg, fbf, yfox, idf, idb, nq=16, stop=9, ystore=None):
    m0 = k.mark()
    ntile = 4 * nq
    gain = k.al("fgain", [128, 4, 128], F32)
    k.ld('sp', gain[:, 0, :], fqg[0:1, :].pbc(128))
    k.ld('sp', gain[:, 1, :], fqg[0:1, :].pbc(128))
    k.ld('sp', gain[:, 2, :], fkg[0:1, :].pbc(128))
    k.ld('sp', gain[:, 3, :], fkg[0:1, :].pbc(128))
    k.ins('dve', 'tensor_scalar', out=gain[:, 2:4, :], in0=gain[:, 2:4, :], scalar1=128.0 ** -0.5, scalar2=None, op0=ALU.mult)
    negb = k.al("fnegb", [64, 2], F32)
    k.ld('sp', negb[:], fbf[0:1, :].pbc(64))
    k.ins('dve', 'tensor_scalar', out=negb[:], in0=negb[:], scalar1=-1.0, scalar2=None, op0=ALU.mult)
    U = k.al("fU", [64, 64], F32)
    k.ins('pool', 'memset', ap=U[:], constant=1.0)
    k.op('pool', lambda e: e.affine_select(out=U[:].ap, in_=U[:].ap, pattern=[[1, 64]], compare_op=ALU.is_gt,
                                           fill=0.0, base=0, channel_multiplier=-1), [U], [U])
    negid = k.al("fnegid", [64, 64], F32)
    k.ins('dve', 'tensor_scalar', out=negid[:], in0=idf[0:64, 0:64], scalar1=-1.0, scalar2=None, op0=ALU.mult)
    maskneg = k.al("fmask", [128, 128], F32)
    k.ins('pool', 'memset', ap=maskneg[:], constant=0.0)
    k.op('pool', lambda e: e.affine_select(out=maskneg[:].ap, in_=maskneg[:].ap, pattern=[[1, 128]], compare_op=ALU.is_ge,
                                           fill=-1e30, base=0, channel_multiplier=-1), [maskneg], [maskneg])
    ones64 = k.al("fones", [64, 128], F32)
    k.ins('pool', 'memset', ap=ones64[:], constant=1.0)
    if stop <= 1:
        k.release(m0)
        return
    gF = k.al("gF", [64, 2, 128], F32)
    if ntile < 64:
        k.ins('pool', 'memset', ap=gF[:], constant=0.0)
    for h in range(2):
        k.ld('sp', gF[0:ntile, h, :], zg[64 + h, 0:ntile * 128].re("(j s) -> j s", s=128))
    e = k.al("fe", [64, 2, 128], F32)
    for h in range(2):
        k.ins('act', 'activation', out=e[:, h, :], in_=gF[:, h, :], func=AF.Exp, scale=-1.0, bias=negb[:, h:h + 1])
    k.ins('act', 'activation', out=e[:], in_=e[:], func=AF.Ln, bias=1.0)
    cs = k.al("fcs", [64, 2, 128], F32)
    for h in range(2):
        k.ins('dve', 'tensor_tensor_scan', out=cs[:, h, :], data0=ones64[:, :], data1=e[:, h, :], initial=0.0, op0=ALU.mult, op1=ALU.add)
    pb = k.bank(0, [64, 2])
    k.mm(pb, U[:, :], cs[:, :, 127])
    offs = k.al("foffs", [64, 2], F32)
    k.ins('dve', 'tensor_copy', out=offs[:], in_=pb)
    Fneg = k.al("Fneg", [64, 2, 128], F32)
    for h in range(2):
        k.ins('dve', 'tensor_scalar', out=Fneg[:, h, :], in0=cs[:, h, :], scalar1=offs[:, h:h + 1], scalar2=None, op0=ALU.add)
    colF = k.al("colF", [128, 2, 64], F32)
    for h in range(2):
        pb = k.bank(1, [128, 64])
        k.tr(pb, Fneg[:, h, :], idf[0:64, 0:64])
        k.ins('dve', 'tensor_copy', out=colF[:, h, :], in_=pb)
    if stop <= 2:
        k.release(m0)
        return
    qT = [k.al("qT%d" % h, [128, T], BF16) for h in range(2)]
    kT = [k.al("kT%d" % h, [128, T], BF16) for h in range(2)]
    Va = [k.al("Va%d" % h, [128, 64, 130], BF16) for h in range(2)]
    for h in range(2):
        k.ins('dve', 'memset', ap=Va[h][:, :, 128:130], constant=1.0)
    if stop <= 2.1:
        k.release(m0)
        return
    qkv = [k.al("qkv%d" % i, [128, 768], F32) for i in range(2)]
    sq = k.al("fsq", [128, 512], F32)
    ssq = k.al("fssq", [128, 8], F32)
    qn = [k.al("qn%d" % i, [128, 512], BF16) for i in range(2)]
    for j in range(ntile):
        t_ = qkv[j % 2]
        k.ld('sp', t_[:], ztm[j * 128:(j + 1) * 128, 768:1536])
        k.ins('act', 'activation', out=sq[:], in_=t_[:, 0:512], func=AF.Square)
        k.ins('dve', 'tensor_reduce', out=ssq[:, 0:4], in_=sq[:].re("p (a b) -> p a b", a=4), axis=AX.X, op=ALU.add)
        k.ins('dve', 'tensor_scalar', out=ssq[:, 0:4], in0=ssq[:, 0:4], scalar1=1.0 / 128, scalar2=EPS, op0=ALU.mult, op1=ALU.add)
        k.ins('dve', 'reciprocal', out=ssq[:, 0:4], in_=ssq[:, 0:4])
        k.ins('act', 'activation', out=ssq[:, 4:8], in_=ssq[:, 0:4], func=AF.Sqrt)
        if stop <= 2.2:
            continue
        n_ = qn[j % 2]
        for seg in range(4):
            k.ins('dve', 'scalar_tensor_tensor', out=n_[:, seg * 128:(seg + 1) * 128], in0=t_[:, seg * 128:(seg + 1) * 128],
                  scalar=ssq[:, 4 + seg:5 + seg], in1=gain[:, seg, :], op0=ALU.mult, op1=ALU.mult)
        pt = k.bank(j % 2, [128, 4, 128], BF16)
        for seg in range(4):
            k.tr(pt[:, seg, :], n_[:, seg * 128:(seg + 1) * 128], idb[:])
        if stop <= 2.3:
            continue
        cols = slice(j * 128, (j + 1) * 128)
        k.ins('act', 'copy', out=qT[0][:, cols], in_=pt[:, 0, :])
        k.ins('act', 'copy', out=qT[1][:, cols], in_=pt[:, 1, :])
        k.ins('act', 'copy', out=kT[0][:, cols], in_=pt[:, 2, :])
        k.ins('act', 'copy', out=kT[1][:, cols], in_=pt[:, 3, :])
        if stop <= 2.4:
            continue
        for h in range(2):
            k.ins('act', 'copy', out=Va[h][:, j, 0:128], in_=t_[:, 512 + h * 128:512 + (h + 1) * 128])
    if stop <= 3:
        k.release(m0)
        return
    zs = [k.al("zs%d" % i, [128, 512], F32) for i in range(2)]
    ptb = [k.al("ptb%d" % i, [128, 512], BF16) for i in range(2)]
    frep = [k.al("frep%d" % i, [128, 512], F32) for i in range(2)]
    ysb = [k.al("ysb%d" % i, [128, 128], F32) for i in range(2)]
    rec = k.al("frec", [128, 4], F32)
    it = 0
    for h in range(2):
        for i in range(nq):
            fb = k.bank(0, [128, 512])
            for qq in range(4):
                jb = 4 * i + qq
                k.mm(fb[:, qq * 128:(qq + 1) * 128], negid[:, jb:jb + 1].bc([64, 128]), Fneg[:, h, :])
            fr = frep[i % 2]
            k.ins('act', 'copy', out=fr[:], in_=fb)
            O = [k.bank(4 + qq, [128, 129]) for qq in range(4)]
            for j in range(4 * i + 4):
                c0 = max(0, j - 4 * i) * 128
                n = 512 - c0
                ps = k.bank(2 + it % 2, [128, 512])
                k.mm(ps[:, 0:n], kT[h][:, j * 128:(j + 1) * 128], qT[h][:, i * 512 + c0:(i + 1) * 512])
                z = zs[it % 2]
                k.ins('dve', 'scalar_tensor_tensor', out=z[:, 0:n], in0=ps[:, 0:n], scalar=colF[:, h, j:j + 1],
                      in1=fr[:, c0:512], op0=ALU.add, op1=ALU.add)
                if j >= 4 * i:
                    k.ins('dve', 'tensor_tensor', out=z[:, 0:128], in0=z[:, 0:128], in1=maskneg[:], op=ALU.add)
                p = ptb[it % 2]
                k.ins('act', 'activation', out=p[:, 0:n], in_=z[:, 0:n], func=AF.Exp)
                for qq in range(max(j - 4 * i, 0), 4):
                    col = qq * 128 - c0
                    qb = 4 * i + qq
                    k.mm(O[qq], p[:, col:col + 128], Va[h][:, j, 0:129], start=(j == 0), stop=(j == qb))
                it += 1
            for qq in range(4):
                k.ins('dve', 'reciprocal', out=rec[:, qq:qq + 1], in_=O[qq][:, 128:129])
                y = ysb[qq % 2]
                k.ins('dve', 'tensor_scalar', out=y[:], in0=O[qq][:, 0:128], scalar1=rec[:, qq:qq + 1], scalar2=None, op0=ALU.mult)
                r0 = (4 * i + qq) * 128
                if ystore is None:
                    k.ld('sp', yfox[r0:r0 + 128, h * 128:(h + 1) * 128], y[:])
                else:
                    ystore(4 * i + qq, 512 + h * 128, 128, y[:])
    k.release(m0)


def phase_s(k, zcm, zg, ztm, sprm, ysd, idf, idb, nch=64, ystore=None):
    m0 = k.mark()
    prm = k.al("sprm", [128, 12], F32)
    k.ld('sp', prm[:], sprm[0:1, :].pbc(128))
    arep_ = k.al("sa", [128, 4], F32)
    k.ins('act', 'activation', out=arep_[:], in_=prm[:, 4:8], func=AF.Exp)
    k.ins('dve', 'tensor_scalar', out=arep_[:], in0=arep_[:], scalar1=-1.0, scalar2=None, op0=ALU.mult)
    maskneg = k.al("smask", [128, 128], F32)
    k.ins('pool', 'memset', ap=maskneg[:], constant=0.0)
    k.op('pool', lambda e: e.affine_select(out=maskneg[:].ap, in_=maskneg[:].ap, pattern=[[1, 128]], compare_op=ALU.is_ge,
                                           fill=-1e30, base=0, channel_multiplier=-1), [maskneg], [maskneg])
    ones64 = k.al("sones", [64, 128], F32)
    k.ins('pool', 'memset', ap=ones64[:], constant=1.0)
    dt = k.al("sdt", [64, 4, 128], F32)
    if nch < 64:
        k.ins('pool', 'memset', ap=dt[:], constant=0.0)
    for h in range(4):
        k.ld('sp', dt[0:nch, h, :], zg[96 + h, 0:nch * 128].re("(c t) -> c t", t=128))
    for h in range(4):
        k.ins('act', 'activation', out=dt[:, h, :], in_=dt[:, h, :], func=AF.Exp, bias=prm[0:64, h:h + 1])
    k.ins('act', 'activation', out=dt[:], in_=dt[:], func=AF.Ln, bias=1.0)
    da = k.al("sda", [64, 4, 128], F32)
    k.ins('dve', 'tensor_tensor', out=da[:], in0=dt[:], in1=arep_[0:64, :].unsq(2).bc([64, 4, 128]), op=ALU.mult)
    acum = k.al("sacum", [64, 4, 128], F32)
    for h in range(4):
        k.ins('dve', 'tensor_tensor_scan', out=acum[:, h, :], data0=ones64[:, :], data1=da[:, h, :], initial=0.0, op0=ALU.mult, op1=ALU.add)
    colA = k.al("scolA", [128, 4, 64], F32)
    colDT = k.al("scolDT", [128, 4, 64], F32)
    for h in range(4):
        pb = k.bank(h % 2, [128, 64])
        k.tr(pb, acum[:, h, :], idf[0:64, 0:64])
        k.ins('dve', 'tensor_copy', out=colA[:, h, :], in_=pb)
        pb2 = k.bank(2 + h % 2, [128, 64])
        k.tr(pb2, dt[:, h, :], idf[0:64, 0:64])
        k.ins('dve', 'tensor_copy', out=colDT[:, h, :], in_=pb2)
    Drep = k.al("sDrep", [128, 4, 64], F32)
    k.ins('dve', 'tensor_copy', out=Drep[:], in_=prm[:, 8:12].unsq(2).bc([128, 4, 64]))
    state = k.al("sstate", [128, 256], F32)
    state_bf = k.al("sstate_bf", [128, 256], BF16)
    k.ins('dve', 'memset', ap=state[:], constant=0.0)
    k.ins('dve', 'memset', ap=state_bf[:], constant=0.0)
    xT = [k.al("sxT%d" % i, [128, 2, 128], BF16) for i in range(2)]
    BT = [k.al("sBT%d" % i, [128, 128], BF16) for i in range(2)]
    CT = [k.al("sCT%d" % i, [128, 128], BF16) for i in range(2)]
    xbt = [k.al("sxbt%d" % i, [128, 384], BF16) for i in range(2)]
    arsb = [k.al("sarsb%d" % i, [128, 4, 128], F32) for i in range(2)]
    dm = [k.al("sdm%d" % i, [128, 4, 128], F32) for i in range(2)]
    mt = [k.al("smt%d" % i, [128, 4, 128], BF16) for i in range(2)]
    ea = [k.al("sea%d" % i, [128, 4, 128], F32) for i in range(2)]
    cts = [k.al("scts%d" % i, [128, 4, 128], BF16) for i in range(2)]
    xdt = [k.al("sxdt%d" % i, [128, 4, 64], BF16) for i in range(2)]
    xdd = [k.al("sxdd%d" % i, [128, 4, 64], BF16) for i in range(2)]
    decs = k.al("sdecs", [128, 8], F32)
    yd = k.al("syd", [128, 64, 256], F32)
    for c in range(nch):
        i2 = c % 2
        cols = slice(c * 128, (c + 1) * 128)
        k.ld('sp', xT[i2][:], zcm[256:512, cols].re("(a p) t -> p a t", p=128))
        k.ld('sp', BT[i2][:], zcm[512:640, cols])
        k.ld('sp', CT[i2][:], zcm[640:768, cols])
        ptr = k.bank(0, [128, 3, 128], BF16)
        k.tr(ptr[:, 0, :], xT[i2][:, 0, :], idb[:])
        k.tr(ptr[:, 1, :], xT[i2][:, 1, :], idb[:])
        k.tr(ptr[:, 2, :], BT[i2][:], idb[:])
        k.ins('act', 'copy', out=xbt[i2][:], in_=ptr)
        pg = k.bank(1, [128, 128])
        k.mm(pg, BT[i2][:], CT[i2][:])
        par = k.bank(2 + i2, [128, 512])
        k.mm(par, idf[0:64, c:c + 1].bc([64, 128]), acum[:].re("c h t -> c (h t)"))
        ar = arsb[i2]
        k.ins('act', 'copy', out=ar[:], in_=par)
        d_ = dm[i2]
        k.ins('dve', 'tensor_tensor', out=d_[:], in0=ar[:], in1=colA[:, :, c:c + 1].bc([128, 4, 128]), op=ALU.subtract)
        k.ins('dve', 'tensor_tensor', out=d_[:], in0=d_[:], in1=maskneg[:].unsq(1).bc([128, 4, 128]), op=ALU.add)
        k.ins('act', 'activation', out=d_[:], in_=d_[:], func=AF.Exp)
        k.ins('dve', 'tensor_tensor', out=mt[i2][:], in0=d_[:], in1=pg.unsq(1).bc([128, 4, 128]), op=ALU.mult)
        xv = xbt[i2][:, 0:256].re("p (h q) -> p h q", h=4)
        k.ins('dve', 'tensor_tensor', out=xdt[i2][:], in0=xv, in1=colDT[:, :, c:c + 1].bc([128, 4, 64]), op=ALU.mult)
        k.ins('act', 'activation', out=ea[i2][:], in_=ar[:], func=AF.Exp)
        k.ins('dve', 'tensor_tensor', out=cts[i2][:], in0=ea[i2][:], in1=CT[i2][:].unsq(1).bc([128, 4, 128]), op=ALU.mult)
        py = k.bank(4 + i2, [128, 256])
        for h in range(4):
            k.mm(py[:, h * 64:(h + 1) * 64], mt[i2][:, h, :], xdt[i2][:, h, :], start=True, stop=False)
            k.mm(py[:, h * 64:(h + 1) * 64], cts[i2][:, h, :], state_bf[:, h * 64:(h + 1) * 64], start=False, stop=True)
        ydc = yd[:, c, :]
        k.ins('dve', 'tensor_tensor', out=ydc, in0=xbt[i2][:, 0:256], in1=Drep[:].re("p h q -> p (h q)"), op=ALU.mult)
        k.ins('dve', 'tensor_tensor', out=ydc, in0=ydc, in1=py, op=ALU.add)
        k.ins('dve', 'tensor_tensor', out=decs[:, 0:4], in0=ar[:, :, 127], in1=colA[:, :, c], op=ALU.subtract)
        k.ins('act', 'activation', out=decs[:, 4:8], in_=decs[:, 0:4], func=AF.Exp)
        k.ins('dve', 'tensor_tensor', out=xdd[i2][:], in0=xdt[i2][:], in1=decs[:, 4:8].unsq(2).bc([128, 4, 64]), op=ALU.mult)
        pst = k.bank(6, [128, 256])
        k.mm(pst, xbt[i2][:, 256:384], xdd[i2][:].re("p h q -> p (h q)"))
        sv = state[:].re("p (h q) -> p h q", h=4)
        k.ins('dve', 'tensor_tensor', out=sv, in0=sv, in1=ea[i2][:, :, 127:128].bc([128, 4, 64]), op=ALU.mult)
        k.ins('dve', 'tensor_tensor', out=state[:], in0=state[:], in1=pst, op=ALU.add)
        k.ins('act', 'copy', out=state_bf[:], in_=state[:])
    zt = [k.al("szt%d" % i, [128, 256], F32) for i in range(2)]
    yo = [k.al("syo%d" % i, [128, 256], F32) for i in range(2)]
    for c in range(nch):
        i2 = c % 2
        rows = slice(c * 128, (c + 1) * 128)
        k.ld('sp', zt[i2][:], ztm[rows, 512:768])
        k.ins('act', 'activation', out=zt[i2][:], in_=zt[i2][:], func=AF.Silu)
        k.ins('dve', 'tensor_tensor', out=yo[i2][:], in0=zt[i2][:], in1=yd[:, c, :], op=ALU.mult)
        if ystore is None:
            k.ld('sp', ysd[rows, :], yo[i2][:])
        else:
            ystore(c, 256, 256, yo[i2][:])
    k.release(m0)


def phase_m(k, zcm, zg, ztm, mprm, mlg, yml, idf, idb, nch=64, stop=9, ystore=None):
    import math
    LN8 = math.log(0.125)
    m0 = k.mark()
    prm = k.al("mprm", [64, 4], F32)
    k.ld('sp', prm[:], mprm[0:1, :].pbc(64))
    nbf = k.al("mnbf", [64, 2], F32)
    k.ins('dve', 'tensor_scalar', out=nbf[:], in0=prm[:, 2:4], scalar1=-1.0, scalar2=None, op0=ALU.mult)
    grep = k.al("mgrep", [128, 256], F32)
    k.ld('sp', grep[:], mlg[0:1, :].pbc(128))
    maskneg = k.al("mmask", [128, 128], F32)
    k.ins('pool', 'memset', ap=maskneg[:], constant=0.0)
    k.op('pool', lambda e: e.affine_select(out=maskneg[:].ap, in_=maskneg[:].ap, pattern=[[1, 128]], compare_op=ALU.is_ge,
                                           fill=-1e30, base=0, channel_multiplier=-1), [maskneg], [maskneg])
    ones64 = k.al("mones", [64, 128], F32)
    k.ins('pool', 'memset', ap=ones64[:], constant=1.0)
    zeros64 = k.al("mzeros", [64, 128], F32)
    k.ins('pool', 'memset', ap=zeros64[:], constant=0.0)
    hmask = k.al("mhmask", [128, 2], F32)
    k.ins('pool', 'memset', ap=hmask[:], constant=1.0)
    k.op('pool', lambda e: e.affine_select(out=hmask[:, 0:1].ap, in_=hmask[:, 0:1].ap, pattern=[[0, 1]], compare_op=ALU.is_ge,
                                           fill=0.0, base=63, channel_multiplier=-1), [hmask], [hmask])
    k.op('pool', lambda e: e.affine_select(out=hmask[:, 1:2].ap, in_=hmask[:, 1:2].ap, pattern=[[0, 1]], compare_op=ALU.is_ge,
                                           fill=0.0, base=-64, channel_multiplier=1), [hmask], [hmask])
    ir = k.al("mir", [64, 2, 128], F32)
    fr = k.al("mfr", [64, 2, 128], F32)
    if nch < 64:
        k.ins('pool', 'memset', ap=ir[:], constant=0.0)
        k.ins('pool', 'memset', ap=fr[:], constant=0.0)
    for h in range(2):
        k.ld('sp', ir[0:nch, h, :], zg[h, 0:nch * 128].re("(c t) -> c t", t=128))
        k.ld('sp', fr[0:nch, h, :], zg[32 + h, 0:nch * 128].re("(c t) -> c t", t=128))
    for h in range(2):
        k.ins('act', 'activation', out=fr[:, h, :], in_=fr[:, h, :], func=AF.Exp, scale=-1.0, bias=nbf[:, h:h + 1])
    k.ins('act', 'activation', out=fr[:], in_=fr[:], func=AF.Ln, bias=1.0)
    bneg = k.al("mbneg", [64, 2, 128], F32)
    a_ = k.al("ma", [64, 2, 128], F32)
    cm = k.al("mcm", [64, 2, 128], F32)
    for h in range(2):
        k.ins('dve', 'tensor_tensor_scan', out=bneg[:, h, :], data0=ones64[:, :], data1=fr[:, h, :], initial=0.0, op0=ALU.mult, op1=ALU.add)
    for h in range(2):
        k.ins('dve', 'scalar_tensor_tensor', out=a_[:, h, :], in0=ir[:, h, :], scalar=prm[:, h:h + 1], in1=bneg[:, h, :], op0=ALU.add, op1=ALU.add)
    for h in range(2):
        k.ins('dve', 'tensor_tensor_scan', out=cm[:, h, :], data0=zeros64[:, :], data1=a_[:, h, :], initial=-1e30, op0=ALU.add, op1=ALU.max)
    gq = k.al("mgq", [64, 2], F32)
    cmq = k.al("mcmq", [64, 2], F32)
    k.ins('dve', 'tensor_scalar', out=gq[:], in0=bneg[:, :, 127], scalar1=-1.0, scalar2=None, op0=ALU.mult)
    k.ins('dve', 'tensor_copy', out=cmq[:], in_=cm[:, :, 127])
    if stop <= 1:
        k.release(m0)
        return
    gT = k.al("mgT", [2, 64], F32)
    cT = k.al("mcT", [2, 64], F32)
    pb = k.bank(0, [2, 64])
    k.tr(pb, gq[:], idf[0:64, 0:64])
    k.ins('dve', 'tensor_copy', out=gT[:], in_=pb)
    pb = k.bank(1, [2, 64])
    k.tr(pb, cmq[:], idf[0:64, 0:64])
    k.ins('dve', 'tensor_copy', out=cT[:], in_=pb)
    mnext = k.al("mmnext", [2, 64], F32)
    k.ins('dve', 'tensor_tensor_scan', out=mnext[:], data0=cT[:], data1=gT[:], initial=0.0, op0=ALU.max, op1=ALU.add)
    Mrow = k.al("mMrow", [2, 64], F32)
    k.ins('dve', 'memset', ap=Mrow[:], constant=0.0)
    k.ins('dve', 'tensor_copy', out=Mrow[:, 1:64], in_=mnext[:, 0:63])
    Mcol = k.al("mMcol", [64, 2], F32)
    mncol = k.al("mmncol", [64, 2], F32)
    pb = k.bank(2, [64, 2])
    k.tr(pb, Mrow[:], idf[0:2, 0:2])
    k.ins('dve', 'tensor_copy', out=Mcol[:], in_=pb)
    pb = k.bank(3, [64, 2])
    k.tr(pb, mnext[:], idf[0:2, 0:2])
    k.ins('dve', 'tensor_copy', out=mncol[:], in_=pb)
    if stop <= 2:
        k.release(m0)
        return
    mt_ = k.al("mmt", [64, 2, 128], F32)
    k.ins('dve', 'tensor_tensor', out=mt_[:], in0=cm[:], in1=Mcol[:].unsq(2).bc([64, 2, 128]), op=ALU.max)
    k.ins('dve', 'tensor_tensor', out=mt_[:], in0=mt_[:], in1=bneg[:], op=ALU.subtract)
    R = k.al("mR", [64, 258], F32)
    Rv = R[:, 0:256].re("p (h t) -> p h t", h=2)
    k.ins('dve', 'tensor_tensor', out=Rv, in0=bneg[:], in1=mt_[:], op=ALU.add)
    k.ins('dve', 'tensor_scalar', out=Rv, in0=Rv, scalar1=-1.0, scalar2=None, op0=ALU.mult)
    k.ins('dve', 'tensor_tensor', out=R[:, 256:258], in0=gq[:], in1=Mcol[:], op=ALU.add)
    k.ins('dve', 'tensor_tensor', out=R[:, 256:258], in0=R[:, 256:258], in1=mncol[:], op=ALU.subtract)
    Q = k.al("mQ", [64, 8, 128], F32)
    gm = k.al("mgm", [64, 2], F32)
    k.ins('dve', 'tensor_scalar', out=Q[:, 0:2, :], in0=a_[:], scalar1=LN8, scalar2=None, op0=ALU.add)
    k.ins('dve', 'tensor_tensor', out=gm[:], in0=gq[:], in1=mncol[:], op=ALU.subtract)
    k.ins('dve', 'tensor_tensor', out=Q[:, 2:4, :], in0=a_[:], in1=gm[:].unsq(2).bc([64, 2, 128]), op=ALU.add)
    k.ins('dve', 'tensor_tensor', out=Q[:, 4:6, :], in0=Rv, in1=Mcol[:].unsq(2).bc([64, 2, 128]), op=ALU.add)
    k.ins('dve', 'tensor_scalar', out=Q[:, 4:6, :], in0=Q[:, 4:6, :], scalar1=LN8, scalar2=None, op0=ALU.add)
    k.ins('dve', 'tensor_scalar', out=Q[:, 6:8, :], in0=mt_[:], scalar1=-1.0, scalar2=None, op0=ALU.mult)
    colQ = k.al("mcolQ", [128, 8, 64], F32)
    for q in range(8):
        pb = k.bank(q % 4, [128, 64])
        k.tr(pb, Q[:, q, :], idf[0:64, 0:64])
        k.ins('dve', 'tensor_copy', out=colQ[:, q, :], in_=pb)
    k.ins('act', 'activation', out=colQ[:, 2:8, :], in_=colQ[:, 2:8, :], func=AF.Exp)
    if stop <= 3:
        k.release(m0)
        return
    state = k.al("mstate", [128, 130], F32)
    state_bf = k.al("mstate_bf", [128, 2, 130], BF16)
    kz = [k.al("mkz%d" % i, [128, 2, 128], BF16) for i in range(2)]
    k.ins('dve', 'memset', ap=state[:], constant=0.0)
    k.ins('dve', 'memset', ap=state_bf[:], constant=0.0)
    qk = [k.al("mqk%d" % i, [128, 2, 128], BF16) for i in range(2)]
    vo = [k.al("mvo%d" % i, [128, 512], F32) for i in range(2)]
    Va = [k.al("mVa%d" % i, [128, 2, 130], BF16) for i in range(2)]
    for i in range(2):
        k.ins('dve', 'memset', ap=Va[i][:, :, 128:130], constant=1.0)
    kw = [k.al("mkw%d" % i, [128, 2, 64], BF16) for i in range(2)]
    rrsb = [k.al("mrr%d" % i, [128, 258], F32) for i in range(2)]
    dmx = [k.al("mdmx%d" % i, [128, 2, 128], F32) for i in range(2)]
    pm = [k.al("mpm%d" % i, [128, 2, 128], BF16) for i in range(2)]
    o1 = [k.al("mo1%d" % i, [128, 2, 129], F32) for i in range(2)]
    osb = [k.al("mos%d" % i, [128, 2, 129], F32) for i in range(2)]
    sm = k.al("msm", [128, 16], F32)
    hsb = [k.al("mhs%d" % i, [128, 2, 128], F32) for i in range(2)]
    junk = k.al("mjunk", [128, 128], F32)
    sg = [k.al("msg%d" % i, [128, 256], F32) for i in range(2)]
    yo = [k.al("myo%d" % i, [128, 256], F32) for i in range(2)]
    for c in range(nch):
        i2 = c % 2
        cols = slice(c * 128, (c + 1) * 128)
        rows = slice(c * 128, (c + 1) * 128)
        k.ld('sp', qk[i2][:], zcm[0:256, cols].re("(a p) t -> p a t", p=128))
        k.ld('sp', vo[i2][:], ztm[rows, 0:512])
        k.ins('act', 'copy', out=Va[i2][:, :, 0:128], in_=vo[i2][:, 0:256].re("p (h v) -> p h v", h=2))
        ptk = k.bank(0, [128, 128], BF16)
        k.tr(ptk, qk[i2][:, 1, :], idb[:])
        for h in range(2):
            k.ins('act', 'activation', out=kw[i2][:, h, :], in_=ptk[:, h * 64:(h + 1) * 64], func=AF.Copy, scale=colQ[:, 2 + h, c:c + 1])
        if stop <= 4:
            continue
        pS = k.bank(1, [128, 2, 128])
        for h in range(2):
            k.ins('dve', 'tensor_scalar', out=kz[i2][:, h, :], in0=qk[i2][:, 1, :], scalar1=hmask[:, h:h + 1], scalar2=None, op0=ALU.mult)
        for h in range(2):
            k.mm(pS[:, h, :], kz[i2][:, h, :], qk[i2][:, 0, :])
        if stop <= 4.2:
            continue
        prr = k.bank(2 + i2, [128, 258])
        k.mm(prr, idf[0:64, c:c + 1].bc([64, 128]), R[:, :])
        rr = rrsb[i2]
        k.ins('act', 'copy', out=rr[:], in_=prr)
        if stop <= 4.4:
            continue
        d_ = dmx[i2]
        k.ins('dve', 'tensor_tensor', out=d_[:], in0=rr[:, 0:256].re("p (h t) -> p h t", h=2), in1=colQ[:, 0:2, c:c + 1].bc([128, 2, 128]), op=ALU.add)
        k.ins('dve', 'tensor_tensor', out=d_[:], in0=d_[:], in1=maskneg[:].unsq(1).bc([128, 2, 128]), op=ALU.add)
        k.ins('act', 'activation', out=d_[:], in_=d_[:], func=AF.Exp)
        if stop <= 4.6:
            continue
        k.ins('dve', 'tensor_tensor', out=pm[i2][:], in0=d_[:], in1=pS, op=ALU.mult)
        if stop <= 5:
            continue
        pO1 = k.bank(4, [128, 2, 129])
        pO2 = k.bank(5, [128, 2, 129])
        for h in range(2):
            k.mm(pO1[:, h, :], pm[i2][:, h, :], Va[i2][:, h, 0:129])
        for h in range(2):
            k.mm(pO2[:, h, :], qk[i2][:, 0, :], state_bf[:, h, 0:129])
        k.ins('act', 'copy', out=o1[i2][:], in_=pO1)
        for h in range(2):
            k.ins('dve', 'scalar_tensor_tensor', out=osb[i2][:, h, :], in0=pO2[:, h, :], scalar=colQ[:, 4 + h, c:c + 1],
                  in1=o1[i2][:, h, :], op0=ALU.mult, op1=ALU.add)
        k.ins('dve', 'tensor_scalar', out=sm[:, 12:14], in0=osb[i2][:, :, 128], scalar1=-1.0, scalar2=None, op0=ALU.mult)
        k.ins('dve', 'tensor_tensor', out=sm[:, 0:2], in0=sm[:, 12:14], in1=osb[i2][:, :, 128], op=ALU.max)
        k.ins('dve', 'tensor_tensor', out=sm[:, 0:2], in0=sm[:, 0:2], in1=colQ[:, 6:8, c], op=ALU.max)
        k.ins('dve', 'reciprocal', out=sm[:, 2:4], in_=sm[:, 0:2])
        for h in range(2):
            k.ins('dve', 'tensor_scalar', out=hsb[i2][:, h, :], in0=osb[i2][:, h, 0:128], scalar1=sm[:, 2 + h:3 + h], scalar2=None, op0=ALU.mult)
        if stop <= 6:
            continue
        for h in range(2):
            k.ins('act', 'activation', out=junk[:], in_=hsb[i2][:, h, :], func=AF.Square, accum_out=sm[:, 4 + h:5 + h])
        k.ins('act', 'activation', out=sm[:, 6:8], in_=sm[:, 4:6], func=AF.Ln, scale=1.0 / 128, bias=EPS)
        k.ins('act', 'activation', out=sm[:, 8:10], in_=sm[:, 6:8], func=AF.Exp, scale=-0.5)
        k.ins('act', 'activation', out=sg[i2][:], in_=vo[i2][:, 256:512], func=AF.Exp, scale=-1.0)
        k.ins('dve', 'tensor_scalar', out=sg[i2][:], in0=sg[i2][:], scalar1=1.0, scalar2=None, op0=ALU.add)
        k.ins('dve', 'reciprocal', out=sg[i2][:], in_=sg[i2][:])
        for h in range(2):
            k.ins('dve', 'scalar_tensor_tensor', out=yo[i2][:, h * 128:(h + 1) * 128], in0=hsb[i2][:, h, :], scalar=sm[:, 8 + h:9 + h],
                  in1=grep[:, h * 128:(h + 1) * 128], op0=ALU.mult, op1=ALU.mult)
        k.ins('dve', 'tensor_tensor', out=yo[i2][:], in0=yo[i2][:], in1=sg[i2][:], op=ALU.mult)
        if ystore is None:
            k.ld('sp', yml[rows, :], yo[i2][:])
        else:
            ystore(c, 0, 256, yo[i2][:])
        if stop <= 7:
            continue
        pst = k.bank(6, [128, 260])
        k.mm(pst, kw[i2][:].re("p h d -> p (h d)"), Va[i2][:].re("p h v -> p (h v)"))
        k.ins('act', 'activation', out=sm[:, 10:12], in_=rr[:, 256:258], func=AF.Exp)
        for h in range(2):
            ps_ = slice(64 * h, 64 * h + 64)
            k.ins('dve', 'scalar_tensor_tensor', out=state[ps_, 0:129], in0=state[ps_, 0:129], scalar=sm[ps_, 10 + h:11 + h],
                  in1=pst[ps_, 130 * h:130 * h + 129], op0=ALU.mult, op1=ALU.add)
        for h in range(2):
            k.ins('act', 'activation', out=state_bf[:, h, :], in_=state[:], func=AF.Copy, scale=hmask[:, h:h + 1])
    k.release(m0)


TT = 2048
NT = TT // 128
D = 1024
NE = 16384


def load_w_bf16(k, dst, src, ncols, stage, row_scale=None):
    kcn = src[:, :].shape[0] // 128
    i = 0
    for kc in range(kcn):
        for c0 in range(0, ncols, 1024):
            c1 = min(ncols, c0 + 1024)
            st = stage[i % 2]
            i += 1
            k.ld('sp', st[:, 0:c1 - c0], src[kc * 128:(kc + 1) * 128, c0:c1])
            if row_scale is None:
                k.ins('act', 'copy', out=dst[:, kc, c0:c1], in_=st[:, 0:c1 - c0])
            else:
                k.ins('act', 'activation', out=dst[:, kc, c0:c1], in_=st[:, 0:c1 - c0], func=AF.Copy, scale=row_scale[:, kc:kc + 1])


def rms_tile(k, xt, grep, hb, junk, ss, idx):
    a, b, c = 3 * idx, 3 * idx + 1, 3 * idx + 2
    k.ins('act', 'activation', out=junk[:], in_=xt, func=AF.Square, accum_out=ss[:, a:a + 1])
    k.ins('dve', 'tensor_scalar', out=ss[:, b:b + 1], in0=ss[:, a:a + 1], scalar1=1.0 / D, scalar2=EPS, op0=ALU.mult, op1=ALU.add)
    k.ins('dve', 'reciprocal', out=ss[:, b:b + 1], in_=ss[:, b:b + 1])
    k.ins('act', 'activation', out=ss[:, c:c + 1], in_=ss[:, b:b + 1], func=AF.Sqrt)
    k.ins('dve', 'scalar_tensor_tensor', out=hb, in0=xt, scalar=ss[:, c:c + 1], in1=grep[:], op0=ALU.mult, op1=ALU.mult)


def transpose8(k, dstT, hb, idb, bank):
    p_T = k.bank(bank, [128, 8, 128], BF16)
    for kc in range(8):
        k.tr(p_T[:, kc, :], hb[:, kc * 128:(kc + 1) * 128], idb[:])
    k.ins('act', 'copy', out=dstT, in_=p_T)


def phase_c1(k, d, idf, idb, nt=NT, fused=False):
    m0 = k.mark()
    stage = [k.al("c1st%d" % i, [128, 1024], F32) for i in range(2)]
    grep = k.al("c1g", [128, D], F32)
    k.ld('sp', grep[:], d['ng_mix'][0:1, :].pbc(128))
    sng = k.al("c1sng", [128, 8], F32)
    k.ld('sp', sng[:], d['sng'][:, :])
    wg = k.al("c1wg", [128, 8, 3072], BF16)
    load_w_bf16(k, wg, d['wg'], 3072, stage)
    wb = {}
    for nm in ('w_ml', 'w_ssm', 'w_fox', 'w_out'):
        wb[nm] = k.al("c1" + nm, [128, 8, D], BF16)
        load_w_bf16(k, wb[nm], d[nm], D, stage, row_scale=(sng if nm == 'w_ssm' else None))
    xb = [k.al("c1x%d" % i, [128, D], F32) for i in range(2)]
    junk = k.al("c1junk", [128, D], BF16)
    ss = k.al("c1ss", [128, 12], F32)
    hb = k.al("c1hb", [128, D], BF16)
    hT = k.al("c1hT", [128, 8, 128], BF16)
    sg = k.al("c1sg", [128, 3072], F32)
    yst = [k.al("c1yst%d" % i, [128, 8, 128], F32) for i in range(2)] if not fused else None
    ybf = [k.al("c1ybf%d" % i, [128, 8, 128], BF16) for i in range(3)]
    ytm = k.al("c1ytm", [128, D], F32) if not fused else None
    mg = k.al("c1mg", [128, D], F32)
    tmp = k.al("c1tmp", [128, 512], F32)
    mgb = k.al("c1mgb", [128, D], BF16)
    mT = k.al("c1mT", [128, 8, 128], BF16)
    x1 = [k.al("c1x1%d" % i, [128, D], F32) for i in range(2)]
    if fused:
        bm = k.al("c1bm", [128, 4], F32)
        k.ld('sp', bm[:], d['bmask'][0:1, :].pbc(128))
        y3a = k.al("c1y3a", [128, 4, 768], F32)
        y3b = k.al("c1y3b", [128, 4, 768], F32)
        ytb = k.al("c1ytb", [128, 3, D], BF16)
    rot = 0
    for tt in range(nt):
        rows = slice(tt * 128, (tt + 1) * 128)
        xt = xb[tt % 2]
        k.ld('sp', xt[:], d['xtok'][rows, :])
        rms_tile(k, xt[:], grep, hb[:], junk, ss, 0)
        transpose8(k, hT[:], hb, idb, 0)
        for cc in range(6):
            pc = k.bank(2 + rot % 4, [128, 512])
            rot += 1
            for kc in range(8):
                k.mm(pc, hT[:, kc, :], wg[:, kc, cc * 512:(cc + 1) * 512], start=(kc == 0), stop=(kc == 7), inc=(kc == 7))
            k.ins('act', 'activation', out=sg[:, cc * 512:(cc + 1) * 512], in_=pc, func=AF.Sigmoid)
        if not fused:
            for bi, nm in enumerate(('ymlT', 'ysdT', 'yfoxT')):
                st = yst[bi % 2]
                k.ld('sp', st[:], d[nm][:, rows].re("(kc p) t -> p kc t", p=128))
                k.ins('act', 'copy', out=ybf[bi][:], in_=st[:])
            k.ld('sp', ytm[:], d['ysd_tm'][rows, :])
            for grp in range(2):
                k.ins('act', 'activation', out=junk[:, 0:512], in_=ytm[:, grp * 512:(grp + 1) * 512], func=AF.Square, accum_out=ss[:, 3 + grp:4 + grp])
        else:
            for rr in range(4):
                for q in range(4):
                    k.ld('sp', y3b[:, q, :], d['ydst'][q * 8 + tt // 2, rr, (tt % 2) * 128:(tt % 2) * 128 + 128, :])
                k.ins('dve', 'tensor_scalar', out=y3a[:, rr, :], in0=y3b[:, 0, :], scalar1=bm[:, 0:1], scalar2=None, op0=ALU.mult)
                for q in range(1, 4):
                    k.ins('dve', 'scalar_tensor_tensor', out=y3a[:, rr, :], in0=y3b[:, q, :], scalar=bm[:, q:q + 1], in1=y3a[:, rr, :], op0=ALU.mult, op1=ALU.add)
            for bi in range(3):
                k.ins('act', 'copy', out=ytb[:, bi, :].re("p (r c) -> p r c", r=4), in_=y3a[:, :, bi * 256:(bi + 1) * 256])
            for bi in range(3):
                transpose8(k, ybf[bi][:], ytb[:, bi, :], idb, bi % 2)
            for grp in range(2):
                k.ins('act', 'activation', out=junk[:, 0:512].re("p (r c) -> p r c", r=2), in_=y3a[:, 2 * grp:2 * grp + 2, 256:512], func=AF.Square, accum_out=ss[:, 3 + grp:4 + grp])
        k.ins('dve', 'tensor_scalar', out=ss[:, 5:7], in0=ss[:, 3:5], scalar1=1.0 / 512, scalar2=EPS, op0=ALU.mult, op1=ALU.add)
        k.ins('dve', 'reciprocal', out=ss[:, 5:7], in_=ss[:, 5:7])
        k.ins('act', 'activation', out=ss[:, 7:9], in_=ss[:, 5:7], func=AF.Sqrt)
        for half in range(2):
            hs = slice(half * 512, (half + 1) * 512)
            pc = k.bank(2 + rot % 4, [128, 512])
            rot += 1
            for kc in range(8):
                k.mm(pc, ybf[0][:, kc, :], wb['w_ml'][:, kc, hs], start=(kc == 0), stop=(kc == 7), inc=(kc == 7))
            k.ins('dve', 'tensor_tensor', out=mg[:, hs], in0=sg[:, hs], in1=pc, op=ALU.mult)
            pc = k.bank(2 + rot % 4, [128, 512])
            rot += 1
            for kc in range(8):
                k.mm(pc, ybf[2][:, kc, :], wb['w_fox'][:, kc, hs], start=(kc == 0), stop=(kc == 7), inc=(kc == 7))
            k.ins('dve', 'tensor_tensor', out=tmp[:], in0=sg[:, 2048 + half * 512:2048 + (half + 1) * 512], in1=pc, op=ALU.mult)
            k.ins('dve', 'tensor_tensor', out=mg[:, hs], in0=mg[:, hs], in1=tmp[:], op=ALU.add)
            pg = []
            for grp in range(2):
                pc = k.bank(2 + rot % 4, [128, 512])
                rot += 1
                for q in range(4):
                    kc = grp * 4 + q
                    k.mm(pc, ybf[1][:, kc, :], wb['w_ssm'][:, kc, hs], start=(q == 0), stop=(q == 3), inc=(q == 3))
                pg.append(pc)
            k.ins('dve', 'tensor_scalar', out=tmp[:], in0=pg[0], scalar1=ss[:, 7:8], scalar2=None, op0=ALU.mult)
            k.ins('dve', 'scalar_tensor_tensor', out=tmp[:], in0=pg[1], scalar=ss[:, 8:9], in1=tmp[:], op0=ALU.mult, op1=ALU.add)
            k.ins('dve', 'tensor_tensor', out=tmp[:], in0=tmp[:], in1=sg[:, 1024 + half * 512:1024 + (half + 1) * 512], op=ALU.mult)
            k.ins('dve', 'tensor_tensor', out=mg[:, hs], in0=mg[:, hs], in1=tmp[:], op=ALU.add)
        k.ins('act', 'copy', out=mgb[:], in_=mg[:])
        transpose8(k, mT[:], mgb, idb, 1)
        xo = x1[tt % 2]
        for half in range(2):
            hs = slice(half * 512, (half + 1) * 512)
            pc = k.bank(2 + rot % 4, [128, 512])
            rot += 1
            for kc in range(8):
                k.mm(pc, mT[:, kc, :], wb['w_out'][:, kc, hs], start=(kc == 0), stop=(kc == 7), inc=(kc == 7))
            k.ins('dve', 'tensor_tensor', out=xo[:, hs], in0=xt[:, hs], in1=pc, op=ALU.add)
        k.ld('sp', d['x1s'][rows, :], xo[:])
    k.release(m0)


def phase_c2a(k, d, idb, nblk=32):
    m0 = k.mark()
    ust = [k.al("c2ust%d" % i, [128, 4, D], F32) for i in range(2)]
    ub = [k.al("c2ub%d" % i, [128, 4, D], BF16) for i in range(2)]
    uts = [k.al("c2uts%d" % i, [128, 8, 512], BF16) for i in range(2)]
    vst = [k.al("c2vst%d" % i, [128, 4, D], F32) for i in range(2)]
    vbb = [k.al("c2vbb%d" % i, [128, 4, D], BF16) for i in range(2)]
    UTv = d['UT'][:, :, :].re("kc p e -> p kc e")
    for blk in range(nblk):
        i2 = blk % 2
        e0 = blk * 512
        k.ld('sp', ust[i2][:], d['peer_u'][e0:e0 + 512, :].re("(a p) d -> p a d", p=128))
        k.ins('act', 'copy', out=ub[i2][:], in_=ust[i2][:])
        for kc in range(8):
            pT = k.bank(kc % 4, [128, 4, 128], BF16)
            for a in range(4):
                k.tr(pT[:, a, :], ub[i2][:, a, kc * 128:(kc + 1) * 128], idb[:])
            k.ins('act', 'copy', out=uts[i2][:, kc, :], in_=pT)
        k.ld('sp', UTv[:, :, e0:e0 + 512], uts[i2][:])
        k.ld('sp', vst[i2][:], d['peer_v'][e0:e0 + 512, :].re("(a p) d -> p a d", p=128))
        k.ins('dve', 'tensor_copy', out=vbb[i2][:], in_=vst[i2][:])
        k.ld('sp', d['Vb'][e0:e0 + 512, :].re("(a p) d -> p a d", p=128), vbb[i2][:])
    k.release(m0)


def phase_c2b(k, d, xnT, idb, nt=NT):
    m0 = k.mark()
    stage = [k.al("c2st%d" % i, [128, 1024], F32) for i in range(2)]
    grep = k.al("c2g", [128, D], F32)
    k.ld('sp', grep[:], d['ng_ffn'][0:1, :].pbc(128))
    wq = k.al("c2wq", [128, 8, 2048], BF16)
    load_w_bf16(k, wq, d['w_q'], 2048, stage)
    kst = k.al("c2kst", [128, 16, 128], F32)
    k.ld('sp', kst[:], d['keysT'][:, :, :].re("j p i -> p j i"))
    keys = k.al("c2keys", [128, 16, 128], BF16)
    k.ins('act', 'copy', out=keys[:], in_=kst[:])
    xb = [k.al("c2x%d" % i, [128, D], F32) for i in range(2)]
    junk = k.al("c2junk", [128, D], BF16)
    ss = k.al("c2ss", [128, 12], F32)
    hb = k.al("c2hb", [128, D], BF16)
    qTs = k.al("c2qTs", [128, 16, 512], BF16)
    sc = k.al("c2sc", [128, 16, 128], F32)
    wk = k.al("c2wk", [128, 256], F32)
    M1 = k.al("c2M1", [128, 8, 16], F32)
    M2 = k.al("c2M2", [128, 8, 16], F32)
    C16 = k.al("c2C16", [128, 8, 16], F32)
    cand = k.al("c2cand", [128, 16, 16], F32)
    e16 = k.al("c2e16", [128, 8, 16], F32)
    st = k.al("c2stt", [128, 6, 8], F32)
    gp = [k.al("c2gp%d" % i, [128, 8, 260], F32) for i in range(2)]
    for i in range(2):
        k.ins('dve', 'memset', ap=gp[i][:], constant=0.0)
    ngrp = (nt + 3) // 4
    rot = 0
    for tg in range(ngrp):
        ntile = min(4, nt - tg * 4)
        ncol = ntile * 128
        for tt in range(ntile):
            t_ = tg * 4 + tt
            rows = slice(t_ * 128, (t_ + 1) * 128)
            xt = xb[t_ % 2]
            k.ld('sp', xt[:], d['x1s'][rows, :])
            rms_tile(k, xt[:], grep, hb[:], junk, ss, 0)
            transpose8(k, xnT[:, :, t_ * 128:(t_ + 1) * 128], hb, idb, t_ % 2)
        g0 = tg * 512
        for j in range(16):
            pc = k.bank(2 + rot % 4, [128, 512])
            rot += 1
            for kc in range(8):
                k.mm(pc[:, 0:ncol], wq[:, kc, j * 128:(j + 1) * 128], xnT[:, kc, g0:g0 + ncol], start=(kc == 0), stop=(kc == 7), inc=(kc == 7))
            k.ins('act', 'copy', out=qTs[:, j, 0:ncol], in_=pc[:, 0:ncol])
        for tt in range(ntile):
            t_ = tg * 4 + tt
            rows = slice(t_ * 128, (t_ + 1) * 128)
            for jb in range(4):
                pc = k.bank(2 + rot % 4, [128, 4, 128])
                rot += 1
                for q in range(4):
                    j = jb * 4 + q
                    k.mm(pc[:, q, :], qTs[:, j, tt * 128:(tt + 1) * 128], keys[:, j, :])
                k.ins('act', 'copy', out=sc[:, jb * 4:(jb + 1) * 4, :], in_=pc)
            g = gp[t_ % 2]
            for h in range(8):
                for half, MM in ((0, M1), (1, M2)):
                    s_ = sc[:, 2 * h + half, :]
                    k.ins('dve', 'max', out=MM[:, h, 0:8], in_=s_)
                    k.ins('dve', 'match_replace', out=wk[:, 0:128], in_to_replace=MM[:, h, 0:8], in_values=s_, imm_value=-1e30)
                    k.ins('dve', 'max', out=MM[:, h, 8:16], in_=wk[:, 0:128])
                k.ins('dve', 'tensor_tensor', out=cand[:], in0=M1[:, h, :].unsq(2).bc([128, 16, 16]), in1=M2[:, h, :].unsq(1).bc([128, 16, 16]), op=ALU.add)
                cf = cand[:].re("p a b -> p (a b)")
                k.ins('dve', 'max', out=C16[:, h, 0:8], in_=cf)
                k.ins('dve', 'match_replace', out=wk[:, :], in_to_replace=C16[:, h, 0:8], in_values=cf, imm_value=-1e30)
                k.ins('dve', 'max', out=C16[:, h, 8:16], in_=wk[:, :])
            k.ins('dve', 'tensor_tensor', out=e16[:], in0=C16[:], in1=C16[:, :, 0:1].bc([128, 8, 16]), op=ALU.subtract)
            k.ins('act', 'activation', out=e16[:], in_=e16[:], func=AF.Exp)
            k.ins('dve', 'tensor_reduce', out=st[:, 0, :], in_=e16[:], axis=AX.X, op=ALU.add)
            k.ins('act', 'activation', out=st[:, 1, :], in_=st[:, 0, :], func=AF.Ln)
            k.ins('dve', 'tensor_tensor', out=st[:, 2, :], in0=M1[:, :, 0], in1=st[:, 1, :], op=ALU.add)
            k.ins('dve', 'tensor_scalar', out=st[:, 2, :], in0=st[:, 2, :], scalar1=-1.0, scalar2=None, op0=ALU.mult)
            k.ins('dve', 'tensor_scalar', out=st[:, 3, :], in0=M2[:, :, 0], scalar1=-1.0, scalar2=None, op0=ALU.mult)
            for h in range(8):
                k.ins('act', 'activation', out=M1[:, h, :], in_=M1[:, h, :], func=AF.Exp, bias=st[:, 2, h:h + 1])
                k.ins('act', 'activation', out=M2[:, h, :], in_=M2[:, h, :], func=AF.Exp, bias=st[:, 3, h:h + 1])
            for h in range(8):
                k.ins('dve', 'tensor_tensor', out=cand[:], in0=M1[:, h, :].unsq(2).bc([128, 16, 16]), in1=M2[:, h, :].unsq(1).bc([128, 16, 16]), op=ALU.mult)
                cf = cand[:].re("p a b -> p (a b)")
                k.ins('dve', 'max', out=C16[:, h, 0:8], in_=cf)
                k.ins('dve', 'match_replace', out=wk[:, :], in_to_replace=C16[:, h, 0:8], in_values=cf, imm_value=-1e30)
                k.ins('dve', 'max', out=C16[:, h, 8:16], in_=wk[:, :])
            k.ins('dve', 'tensor_copy', out=g[:, :, 256], in_=C16[:, :, 15])
            for h in range(8):
                k.ins('act', 'activation', out=g[:, h, 0:128], in_=sc[:, 2 * h, :], func=AF.Exp, bias=st[:, 2, h:h + 1])
                k.ins('act', 'activation', out=g[:, h, 128:256], in_=sc[:, 2 * h + 1, :], func=AF.Exp, bias=st[:, 3, h:h + 1])
            k.ld('sp', d['gps'][rows, :], g[:].re("p h c -> p (h c)"))
    k.release(m0)


def phase_c2c(k, d, xnT, idb, nt=NT, nblk=32):
    m0 = k.mark()
    gp = [k.al("c3gp%d" % i, [128, 8, 260], F32) for i in range(2)]
    utb = [k.al("c3utb%d" % i, [128, 8, 512], BF16) for i in range(2)]
    vb = [k.al("c3vb%d" % i, [128, 4, D], BF16) for i in range(2)]
    A = [k.al("c3A%d" % i, [128, 512], F32) for i in range(2)]
    P = [k.al("c3P%d" % i, [128, 8, 4, 128], F32) for i in range(2)]
    W = [k.al("c3W%d" % i, [128, 512], F32) for i in range(2)]
    AW = [k.al("c3AW%d" % i, [128, 512], BF16) for i in range(2)]
    awt = [k.al("c3awt%d" % i, [128, 4, 128], BF16) for i in range(2)]
    x1 = [k.al("c3x1%d" % i, [128, D], F32) for i in range(2)]
    UTv = d['UT'][:, :, :].re("kc p e -> p kc e")
    cnt = 0
    for pr in range((nt + 1) // 2):
        tiles = [t for t in (2 * pr, 2 * pr + 1) if t < nt]
        for ti, t_ in enumerate(tiles):
            k.ld('sp', gp[ti][:].re("p h c -> p (h c)"), d['gps'][t_ * 128:(t_ + 1) * 128, :])
        for eb in range(nblk):
            e0 = eb * 512
            u_ = utb[eb % 2]
            v_ = vb[eb % 2]
            k.ld('sp', u_[:], UTv[:, :, e0:e0 + 512])
            k.ld('sp', v_[:], d['Vb'][e0:e0 + 512, :].re("(a p) d -> p a d", p=128))
            for ti, t_ in enumerate(tiles):
                c2 = cnt % 2
                cnt += 1
                g = gp[ti]
                pA = k.bank(4 + c2, [128, 512])
                for kc in range(8):
                    k.mm(pA, xnT[:, kc, t_ * 128:(t_ + 1) * 128], u_[:, kc, :], start=(kc == 0), stop=(kc == 7), inc=(kc == 7))
                k.ins('act', 'activation', out=A[c2][:], in_=pA, func=AF.Gelu)
                Pt = P[c2]
                k.ins('dve', 'tensor_tensor', out=Pt[:], in0=g[:, :, 4 * eb:4 * eb + 4].unsq(3).bc([128, 8, 4, 128]),
                      in1=g[:, :, 128:256].unsq(2).bc([128, 8, 4, 128]), op=ALU.mult)
                for h in range(8):
                    k.ins('dve', 'scalar_tensor_tensor', out=Pt[:, h, :, :], in0=Pt[:, h, :, :], scalar=g[:, h, 256:257], in1=Pt[:, h, :, :],
                          op0=ALU.is_ge, op1=ALU.mult)
                k.ins('dve', 'tensor_reduce', out=W[c2][:], in_=Pt[:].re("p h a i -> p (a i) h"), axis=AX.X, op=ALU.add)
                k.ins('dve', 'tensor_tensor', out=AW[c2][:], in0=A[c2][:], in1=W[c2][:], op=ALU.mult)
                pT = k.bank(6 + c2, [128, 4, 128], BF16)
                for a in range(4):
                    k.tr(pT[:, a, :], AW[c2][:, a * 128:(a + 1) * 128], idb[:])
                k.ins('act', 'copy', out=awt[c2][:], in_=pT)
                for a in range(4):
                    for half in range(2):
                        last = (eb == nblk - 1 and a == 3)
                        k.mm(k.bank(2 * ti + half, [128, 512]), awt[c2][:, a, :], v_[:, a, half * 512:(half + 1) * 512],
                             start=(eb == 0 and a == 0), stop=last, inc=(last or (a == 3 and half == 1)))
        for ti, t_ in enumerate(tiles):
            rows = slice(t_ * 128, (t_ + 1) * 128)
            xo = x1[ti]
            k.ld('sp', xo[:], d['x1s'][rows, :])
            for half in range(2):
                hs = slice(half * 512, (half + 1) * 512)
                k.ins('dve', 'tensor_tensor', out=xo[:, hs], in0=xo[:, hs], in1=k.bank(2 * ti + half, [128, 512]), op=ALU.add)
            k.ld('sp', d['x2s'][rows, :], xo[:])
    k.release(m0)


def phase_c3(k, d, idb, nt=NT, final=False, outname='xout'):
    m0 = k.mark()
    stage = [k.al("c4st%d" % i, [128, 1024], F32) for i in range(2)]
    grep = k.al("c4g", [128, D], F32)
    k.ld('sp', grep[:], d['ng_ple'][0:1, :].pbc(128))
    fg = k.al("c4fg", [128, D], F32)
    if final:
        k.ld('sp', fg[:], d['final_g'][0:1, :].pbc(128))
    wgt = k.al("c4wg", [128, 8, D], BF16)
    load_w_bf16(k, wgt, d['w_gate'], D, stage)
    wpj = k.al("c4wp", [128, 2, D], BF16)
    load_w_bf16(k, wpj, d['w_proj'], D, stage)
    xb = [k.al("c4x%d" % i, [128, D], F32) for i in range(2)]
    junk = k.al("c4junk", [128, D], BF16)
    junkf = k.al("c4junkf", [128, D], F32)
    ss = k.al("c4ss", [128, 12], F32)
    hb = k.al("c4hb", [128, D], BF16)
    hT = k.al("c4hT", [128, 8, 128], BF16)
    sgp = k.al("c4sg", [128, D], F32)
    pst = [k.al("c4pst%d" % i, [128, 2, 128], F32) for i in range(2)]
    pbf = k.al("c4pbf", [128, 2, 128], BF16)
    xo = [k.al("c4xo%d" % i, [128, D], F32) for i in range(2)]
    rot = 0
    for tt in range(nt):
        rows = slice(tt * 128, (tt + 1) * 128)
        xt = xb[tt % 2]
        k.ld('sp', xt[:], d['x2s'][rows, :])
        rms_tile(k, xt[:], grep, hb[:], junk, ss, 0)
        transpose8(k, hT[:], hb, idb, tt % 2)
        k.ld('sp', pst[tt % 2][:], d['pT'][:, rows].re("(kc p) t -> p kc t", p=128))
        k.ins('act', 'copy', out=pbf[:], in_=pst[tt % 2][:])
        for half in range(2):
            hs = slice(half * 512, (half + 1) * 512)
            pc = k.bank(2 + rot % 4, [128, 512])
            rot += 1
            for kc in range(8):
                k.mm(pc, hT[:, kc, :], wgt[:, kc, hs], start=(kc == 0), stop=(kc == 7), inc=(kc == 7))
            k.ins('act', 'activation', out=sgp[:, hs], in_=pc, func=AF.Sigmoid)
            pc2 = k.bank(2 + rot % 4, [128, 512])
            rot += 1
            for kc in range(2):
                k.mm(pc2, pbf[:, kc, :], wpj[:, kc, hs], start=(kc == 0), stop=(kc == 1), inc=(kc == 1))
            k.ins('dve', 'tensor_tensor', out=sgp[:, hs], in0=sgp[:, hs], in1=pc2, op=ALU.mult)
        o = xo[tt % 2]
        k.ins('dve', 'tensor_tensor', out=o[:], in0=xt[:], in1=sgp[:], op=ALU.add)
        if final:
            k.ins('act', 'activation', out=junkf[:], in_=o[:], func=AF.Square, accum_out=ss[:, 3:4])
            k.ins('dve', 'tensor_scalar', out=ss[:, 4:5], in0=ss[:, 3:4], scalar1=1.0 / D, scalar2=EPS, op0=ALU.mult, op1=ALU.add)
            k.ins('dve', 'reciprocal', out=ss[:, 4:5], in_=ss[:, 4:5])
            k.ins('act', 'activation', out=ss[:, 5:6], in_=ss[:, 4:5], func=AF.Sqrt)
            k.ins('dve', 'scalar_tensor_tensor', out=o[:], in0=o[:], scalar=ss[:, 5:6], in1=fg[:], op0=ALU.mult, op1=ALU.mult)
        k.ld('sp', d[outname][rows, :], o[:])
    k.release(m0)


CIN = [("xtok", [TT, D]), ("ng_mix", [1, D]), ("wg", [D, 3072]), ("ymlT", [D, TT]), ("ysdT", [D, TT]), ("yfoxT", [D, TT]),
       ("ysd_tm", [TT, D]), ("sng", [128, 8]), ("w_ml", [D, D]), ("w_ssm", [D, D]), ("w_fox", [D, D]), ("w_out", [D, D]),
       ("ng_ffn", [1, D]), ("w_q", [D, 2048]), ("keysT", [16, 128, 128]), ("peer_u", [NE, D]), ("peer_v", [NE, D]),
       ("ng_ple", [1, D]), ("w_gate", [D, D]), ("w_proj", [256, D]), ("pT", [256, TT]), ("final_g", [1, D])]


def build_c(dbg=True, phases=('1', 'a', 'b', 'c', '3'), nt=NT, nblk=32, final=False):
    nc = bass.Bass("TRN2", target_bir_lowering=False)
    k = K(nc)
    kS = "ExternalOutput" if dbg else "Internal"
    d = {}
    for nm, shp in CIN:
        d[nm] = k.dram(nm, shp, F32, "ExternalInput")
    d['x1s'] = k.dram("x1s", [TT, D], F32, kS)
    d['x2s'] = k.dram("x2s", [TT, D], F32, kS)
    d['gps'] = k.dram("gps", [TT, 8 * 260], F32, kS)
    d['UT'] = k.dram("UT", [8, 128, NE], BF16, "Internal")
    d['Vb'] = k.dram("Vb", [NE, D], BF16, "Internal")
    d['xout'] = k.dram("xout", [TT, D], F32, "ExternalOutput")
    k.init_arena(200 * 1024)
    idf, idb = make_ident(k)
    outs = [d['xout']]
    if '1' in phases:
        phase_c1(k, d, idf, idb, nt)
        outs.append(d['x1s'])
    if 'a' in phases:
        phase_c2a(k, d, idb, nblk)
    xnT = k.al("xnT", [128, 8, TT], BF16)
    if 'b' in phases:
        phase_c2b(k, d, xnT, idb, nt)
        outs.append(d['gps'])
    if 'c' in phases:
        phase_c2c(k, d, xnT, idb, nt, nblk)
        outs.append(d['x2s'])
    if '3' in phases:
        phase_c3(k, d, idb, nt, final)
    k.final_wait('sp', outs)
    k.emit()
    print("instructions:", k.ninst, "sems:", len(k.sems))
    return nc


def prep_c(inp, layer, xfull, yml, ysd, yfox):
    w = inp['w_in'][layer]
    O = OFF
    wg = np.ascontiguousarray(w[:, O['g_ml']:O['g_ml'] + 3072])
    sng = np.ascontiguousarray(inp['ssm_norm_g'][layer].reshape(8, 128).T)
    keysT = np.empty((16, 128, 128), np.float32)
    for h in range(8):
        keysT[2 * h] = inp['peer_keys1'][layer][h].T
        keysT[2 * h + 1] = inp['peer_keys2'][layer][h].T
    maps = []
    xf = xfull.reshape(16384, D)
    ymlf, ysdf, yfoxf = yml.reshape(16384, D), ysd.reshape(16384, D), yfox.reshape(16384, D)
    pf = inp['p'][layer].reshape(16384, 256)
    for c in range(8):
        rows = slice(c * TT, (c + 1) * TT)
        m = {
            'xtok': np.ascontiguousarray(xf[rows]), 'ng_mix': inp['norm_mix_g'][layer].reshape(1, D), 'wg': wg,
            'ymlT': np.ascontiguousarray(ymlf[rows].T), 'ysdT': np.ascontiguousarray(ysdf[rows].T), 'yfoxT': np.ascontiguousarray(yfoxf[rows].T),
            'ysd_tm': np.ascontiguousarray(ysdf[rows]), 'sng': sng,
            'w_ml': inp['w_branch_ml'][layer], 'w_ssm': inp['w_branch_ssm'][layer], 'w_fox': inp['w_branch_fox'][layer], 'w_out': inp['w_out'][layer],
            'ng_ffn': inp['norm_ffn_g'][layer].reshape(1, D), 'w_q': inp['peer_w_q'][layer], 'keysT': keysT,
            'peer_u': inp['peer_u'][layer], 'peer_v': inp['peer_v'][layer],
            'ng_ple': inp['norm_ple_g'][layer].reshape(1, D), 'w_gate': inp['ple_w_gate'][layer], 'w_proj': inp['ple_w_proj'][layer],
            'pT': np.ascontiguousarray(pf[rows].T), 'final_g': inp['final_norm_g'].reshape(1, D),
        }
        maps.append({kk: np.ascontiguousarray(v, dtype=np.float32) for kk, v in m.items()})
    return maps


AB_IN = [("wcm", [D, NCM]), ("wtm", [D, NTM]), ("cw", [128, 6, 4]), ("cb", [128, 6]), ("ng", [1, D]), ("fqg", [1, 128]),
         ("fkg", [1, 128]), ("fbf", [1, 2]), ("sprm", [1, 12]), ("mprm", [1, 4]), ("mlg", [1, 256])]
C_SKIP = ("xtok", "ymlT", "ysdT", "yfoxT", "ysd_tm", "final_g")


def build_fused(depth=2):
    nc = bass.Bass("TRN2", target_bir_lowering=False)
    k = K(nc)
    g = {
        'x': k.dram("x", [T, D], F32, "ExternalInput"),
        'xtok': k.dram("xtok", [TT, D], F32, "ExternalInput"),
        'bmask': k.dram("bmask", [1, 4], F32, "ExternalInput"),
        'final_g': k.dram("final_g", [1, D], F32, "ExternalInput"),
    }
    L = []
    for l in range(depth):
        dl = {}
        for nm, shp in AB_IN:
            dl[nm] = k.dram("%s_%d" % (nm, l), shp, F32, "ExternalInput")
        for nm, shp in CIN:
            if nm not in C_SKIP:
                dl[nm] = k.dram("%s_%d" % (nm, l), shp, F32, "ExternalInput")
        L.append(dl)
    zcm = k.dram("zcm", [768, T], BF16, "Internal")
    zg = k.dram("zg", [128, T], F32, "Internal")
    ztm = k.dram("ztm", [T, NTM], F32, "Internal")
    ysrc = k.dram("ysrc", [T, 768], F32, "Internal")
    ydst = k.dram("ydst", [32, 4, 256, 768], F32, "Internal")
    xcur = k.dram("xcur", [TT, D], F32, "Internal")
    xg = k.dram("xg", [8, 4, 256, D], F32, "Internal")
    sc = {
        'x1s': k.dram("x1s", [TT, D], F32, "Internal"), 'x2s': k.dram("x2s", [TT, D], F32, "Internal"),
        'gps': k.dram("gps", [TT, 8 * 260], F32, "Internal"), 'UT': k.dram("UT", [8, 128, NE], BF16, "Internal"),
        'Vb': k.dram("Vb", [NE, D], BF16, "Internal"), 'xout': k.dram("xout", [TT, D], F32, "ExternalOutput"),
        'xcur': xcur, 'ydst': ydst, 'bmask': g['bmask'], 'final_g': g['final_g'],
    }
    k.init_arena(206 * 1024)
    idf, idb = make_ident(k)

    def ystore(c, col0, ncol, ref):
        k.ld('sp', ysrc[c * 128:(c + 1) * 128, col0:col0 + ncol], ref)

    for l in range(depth):
        dl = L[l]
        xsrc = g['x'] if l == 0 else (lambda r0: xg[(r0 % 2048) // 256, r0 // 2048, (r0 % 256):(r0 % 256) + 128, :])
        mp = k.mark()
        phase_p(k, xsrc, dl['ng'], dl['wcm'], dl['wtm'], dl['cw'], dl['cb'], zcm, zg, ztm, idb, NG)
        k.release(mp)
        phase_m(k, zcm, zg, ztm, dl['mprm'], dl['mlg'], None, idf, idb, nch=64, ystore=ystore)
        phase_s(k, zcm, zg, ztm, dl['sprm'], None, idf, idb, nch=64, ystore=ystore)
        phase_f(k, ztm, zg, dl['fqg'], dl['fkg'], dl['fbf'], None, idf, idb, nq=16, ystore=ystore)
        for cch in range(32):
            k.coll("AllGather", ydst[cch, :, :, :], ysrc[cch * 256:(cch + 1) * 256, :], [[0, 1, 2, 3], [4, 5, 6, 7]])
        d = dict(sc)
        d.update(dl)
        d['xtok'] = g['xtok'] if l == 0 else xcur
        final = (l == depth - 1)
        phase_c1(k, d, idf, idb, NT, fused=True)
        phase_c2a(k, d, idb, 32)
        mx = k.mark()
        xnT = k.al("xnT", [128, 8, TT], BF16)
        phase_c2b(k, d, xnT, idb, NT)
        phase_c2c(k, d, xnT, idb, NT, 32)
        k.release(mx)
        phase_c3(k, d, idb, NT, final, outname=('xout' if final else 'xcur'))
        if not final:
            for cch in range(8):
                k.coll("AllGather", xg[cch, :, :, :], xcur[cch * 256:(cch + 1) * 256, :], [[0, 1, 2, 3], [4, 5, 6, 7]])
    k.final_wait('sp', [sc['xout']])
    k.emit()
    return nc


def kernel(**inputs):
    inp = {k_: np.asarray(v) for k_, v in inputs.items()}
    x = np.ascontiguousarray(inp['x'], dtype=np.float32)
    depth = inp['w_in'].shape[0]
    zeros = np.zeros((2, T, 1024), np.float32)
    maps = [dict() for _ in range(8)]
    xf = x.reshape(16384, D)
    for l in range(depth):
        mab = prep_ab(inp, l, x)
        mc = prep_c(inp, l, x, zeros, zeros, zeros)
        for c in range(8):
            for nm, _ in AB_IN:
                maps[c]["%s_%d" % (nm, l)] = mab[c][nm]
            for nm, _ in CIN:
                if nm not in C_SKIP:
                    maps[c]["%s_%d" % (nm, l)] = mc[c][nm]
    for c in range(8):
        b = c // 4
        maps[c]['x'] = np.ascontiguousarray(x[b])
        maps[c]['xtok'] = np.ascontiguousarray(xf[c * TT:(c + 1) * TT])
        bmk = np.zeros((1, 4), np.float32)
        bmk[0, c % 4] = 1.0
        maps[c]['bmask'] = bmk
        maps[c]['final_g'] = np.ascontiguousarray(inp['final_norm_g'].reshape(1, D), dtype=np.float32)
    nc = build_fused(depth)
    res = run_bass_kernel_spmd(nc, maps, core_ids=list(range(8)))
    out = np.empty((16384, D), np.float32)
    for c in range(8):
        out[c * TT:(c + 1) * TT] = np.asarray(res.results[c]['xout'])
    return np.ascontiguousarray(out.reshape(2, T, D), dtype=np.float32)
```

```python
import os
import contextlib
import numpy as np
import concourse.bass as bass
import concourse.mybir as mybir
from concourse.bass_utils import run_bass_kernel_spmd

F32 = mybir.dt.float32
BF16 = mybir.dt.bfloat16
U32 = mybir.dt.uint32
AF = mybir.ActivationFunctionType
ALU = mybir.AluOpType
AX = mybir.AxisListType


class Buf:
    def __init__(self, K, name, ap_fn):
        self.K = K
        self.name = name
        self.ap_fn = ap_fn
        self.writes = {}
        self.reads = {}
        self.dsem = None
        self.dcnt = 0
        self.is_dram = False
        self.multi = False

    def __getitem__(self, idx):
        return Ref(self, self.ap_fn[idx])


class Ref:
    def __init__(self, buf, ap):
        self.buf = buf
        self.ap = ap

    @property
    def shape(self):
        return self.ap.shape

    def __getitem__(self, idx):
        return Ref(self.buf, self.ap[idx])

    def bc(self, shape):
        return Ref(self.buf, self.ap.to_broadcast(list(shape)))

    def unsq(self, axis):
        return Ref(self.buf, self.ap.unsqueeze(axis))

    def re(self, pattern_, **kw):
        return Ref(self.buf, self.ap.rearrange(pattern_, **kw))

    def pbc(self, n):
        return Ref(self.buf, self.ap.partition_broadcast(n))

    def bitcast(self, dt):
        return Ref(self.buf, self.ap.bitcast(dt))


class K:
    ENG = ('pe', 'dve', 'act', 'pool', 'sp')

    def __init__(self, nc, same_engine_sync=True):
        self.nc = nc
        self.es = contextlib.ExitStack()
        self.engobj = {'pe': nc.tensor, 'dve': nc.vector, 'act': nc.scalar,
                       'pool': nc.gpsimd, 'sp': nc.sync}
        self.sems = {}
        self.cnt = {}
        for e in self.ENG:
            self.sems[e] = self.es.enter_context(nc.semaphore("s_" + e))
            self.cnt[e] = 0
        self.waited = {e: {} for e in self.ENG}
        self.prog = {e: [] for e in self.ENG}
        self.same = same_engine_sync
        self.nbuf = 0
        self.ninst = 0
        self.dmax = {}
        self.coll_inc = False
        self.live = []
        self.free_sems = []

    def sb(self, name, shape, dt):
        t = self.es.enter_context(self.nc.sbuf_tensor(name, list(shape), dt))
        return Buf(self, name, t)

    def ps(self, name, shape, dt):
        t = self.es.enter_context(self.nc.psum_tensor(name, list(shape), dt))
        return Buf(self, name, t)

    def dram(self, name, shape, dt, kind):
        t = self.nc.dram_tensor(name, list(shape), dt, kind=kind)
        b = Buf(self, name, t.ap())
        b.is_dram = True
        b.multi = True
        return b

    def init_arena(self, nbytes=188 * 1024):
        self.arena_t = self.es.enter_context(self.nc.sbuf_tensor("arena", [128, nbytes // 4], F32))
        self.arena_n = nbytes // 4
        self.arena_off = 0
        self.banks = [self.ps("bank%d" % i, [128, 512], F32) for i in range(8)]

    def al(self, name, shape, dt):
        esz = 2 if dt == BF16 else 4
        n = 1
        for s_ in shape[1:]:
            n *= s_
        words = (n * esz + 3) // 4
        words = (words + 15) // 16 * 16
        assert self.arena_off + words <= self.arena_n, ("arena overflow", name, self.arena_off, words)
        ap = self.arena_t[0:shape[0], self.arena_off:self.arena_off + words]
        off0 = self.arena_off
        self.arena_off += words
        if dt != F32:
            ap = ap.bitcast(dt)
        ap = ap[:, 0:n]
        if len(shape) > 2:
            names = " ".join("d%d" % i for i in range(1, len(shape)))
            kw = {"d%d" % i: shape[i] for i in range(1, len(shape))}
            ap = ap.rearrange("p (%s) -> p %s" % (names, names), **kw)
        b = Buf(self, name, ap)
        self.live.append((off0, b))
        return b

    def bank(self, i, shape, dt=None):
        b = self.banks[i]
        ap = b.ap_fn[:, :]
        if dt is not None and dt != F32:
            ap = ap.bitcast(dt)
        n = 1
        for s_ in shape[1:]:
            n *= s_
        ap = ap[0:shape[0], 0:n]
        if len(shape) > 2:
            names = " ".join("d%d" % i for i in range(1, len(shape)))
            kw = {"d%d" % i: shape[i] for i in range(1, len(shape))}
            ap = ap.rearrange("p (%s) -> p %s" % (names, names), **kw)
        return Ref(b, ap)

    def mark(self):
        return self.arena_off

    def release(self, m):
        self.barrier()
        keep = []
        for off, b in self.live:
            if off >= m:
                if b.dsem is not None:
                    self.free_sems.append(b.dsem)
                    b.dsem = None
            else:
                keep.append((off, b))
        self.live = keep
        self.arena_off = m

    def barrier(self):
        snap = dict(self.cnt)
        for key, sem in self.sems.items():
            if key not in snap:
                snap[key] = None
        dvals = {}
        for key in self.sems:
            if key not in self.cnt:
                dvals[key] = self.dmax.get(key, 0)
        for e in self.ENG:
            waits = []
            for key, v in list(snap.items()):
                v = self.cnt[key] if key in self.cnt else dvals[key]
                if key == e or v == 0:
                    continue
                if self.waited[e].get(key, 0) >= v:
                    continue
                self.waited[e][key] = v
                waits.append((key, v))
            self._emit_waits(e, waits)

    def _need(self, eng, reads, writes, skip_waw=None):
        need = {}
        for b in reads:
            for k, v in b.writes.items():
                if need.get(k, 0) < v:
                    need[k] = v
        for b in writes:
            if not b.multi:
                for k, v in b.writes.items():
                    if k == skip_waw:
                        continue
                    if need.get(k, 0) < v:
                        need[k] = v
            for k, v in b.reads.items():
                if need.get(k, 0) < v:
                    need[k] = v
        out = []
        for k, v in need.items():
            if k == eng and (eng == 'pe' or not self.same):
                continue
            if self.waited[eng].get(k, 0) >= v:
                continue
            self.waited[eng][k] = v
            out.append((k, v))
        return out

    def _emit_waits(self, eng, waits):
        for k, v in waits:
            sem = self.sems[k]
            self.prog[eng].append(lambda e, sem=sem, v=v: e.wait_ge(sem, v))

    def _mark(self, key, val, reads, writes):
        for b in writes:
            if b.multi:
                if b.writes.get(key, 0) < val:
                    b.writes[key] = val
            else:
                b.writes = {key: val}
                b.reads = {}
        for b in reads:
            if b.reads.get(key, 0) < val:
                b.reads[key] = val

    def op(self, eng, fn, reads=(), writes=(), inc=True):
        waits = self._need(eng, reads, writes)
        self._emit_waits(eng, waits)
        val = self.cnt[eng] + 1
        if inc:
            self.cnt[eng] = val
            sem = self.sems[eng]
            self.prog[eng].append(lambda e, fn=fn, sem=sem: fn(e).then_inc(sem, 1))
        else:
            self.prog[eng].append(lambda e, fn=fn: fn(e))
        self._mark(eng, val, reads, writes)
        self.ninst += 1

    WRITE_KW = ('out', 'accum_out', 'out_max', 'out_indices', 'ap')

    def ins(self, eng, method, inc=True, **kw):
        reads, writes, args = [], [], {}
        for n, v in kw.items():
            if isinstance(v, Ref):
                (writes if n in self.WRITE_KW else reads).append(v.buf)
                args[n] = v.ap
            else:
                args[n] = v
        self.op(eng, lambda e, m=method, a=args: getattr(e, m)(**a), reads, writes, inc=inc)

    def mm(self, out, lhsT, rhs, start=True, stop=True, inc=True):
        self.op('pe', lambda e, o=out.ap, l=lhsT.ap, r=rhs.ap: e.matmul(o, lhsT=l, rhs=r, start=start, stop=stop),
                [lhsT.buf, rhs.buf], [out.buf], inc=inc)

    def tr(self, out, in_, ident):
        self.op('pe', lambda e, o=out.ap, i=in_.ap, d=ident.ap: e.transpose(out=o, in_=i, identity=d),
                [in_.buf, ident.buf], [out.buf])

    def ld(self, q, out, in_, **kw):
        self.dma(q, out.buf, out.ap, in_.buf, in_.ap, **kw)

    def dma(self, q, out_buf, out_ap, in_buf, in_ap, **kw):
        own = in_buf if out_buf.is_dram else out_buf
        if own.dsem is None:
            if self.free_sems:
                key = self.free_sems.pop()
            else:
                key = "d%d" % self.nbuf
                self.nbuf += 1
                self.sems[key] = self.es.enter_context(self.nc.semaphore(key))
            own.dsem = key
            own.dcnt = self.dmax.get(key, 0)
        key = own.dsem
        waits = self._need(q, [in_buf], [out_buf], skip_waw=key)
        self._emit_waits(q, waits)
        own.dcnt += 16
        val = own.dcnt
        self.dmax[key] = val
        sem = self.sems[key]
        self.prog[q].append(lambda e, o=out_ap, i=in_ap, sem=sem, kw=kw: e.dma_start(out=o, in_=i, **kw).then_inc(sem, 16))
        self._mark(key, val, [in_buf], [out_buf])
        self.ninst += 1

    def coll(self, kind, out, in_, groups, op=None):
        key = "coll"
        if key not in self.sems:
            self.sems[key] = self.es.enter_context(self.nc.semaphore(key))
            self.dmax[key] = 0
        waits = self._need('pool', [in_.buf], [out.buf])
        self._emit_waits('pool', waits)
        self.dmax[key] += 1
        val = self.dmax[key]
        sem = self.sems[key]
        aop = ALU.bypass if op is None else op
        self.prog['pool'].append(lambda e, o=out.ap.opt(), i=in_.ap.opt(), sem=sem: e.collective_compute(
            kind, aop, replica_groups=groups, ins=[i], outs=[o]).then_inc(sem, 1))
        self._mark(key, val, [in_.buf], [out.buf])
        self.ninst += 1

    def final_wait(self, eng, bufs):
        waits = self._need(eng, bufs, bufs)
        self._emit_waits(eng, waits)

    def emit(self):
        with self.nc.Block() as block:
            names = {'pe': 'tensor', 'dve': 'vector', 'act': 'scalar', 'pool': 'gpsimd', 'sp': 'sync'}
            for e in self.ENG:
                lst = self.prog[e]
                if not lst:
                    continue

                def run(engine, lst=lst):
                    for th in lst:
                        th(engine)
                getattr(block, names[e])(run)
        self.es.close()


T = 8192
D = 1024
NCM = 896
NTM = 1540
EPS = 1e-6
NG = T // 512

OFF = {}
_names = ['ml_q', 'ml_k', 'ml_v', 'ml_o', 'ml_i', 'ml_f', 's_z', 's_x', 's_b', 's_c', 's_dt',
          'f_q', 'f_k', 'f_v', 'f_f', 'g_ml', 'g_ssm', 'g_fox']
_sizes = [512, 512, 1024, 1024, 8, 8, 1024, 1024, 256, 256, 16, 1024, 1024, 1024, 8, 1024, 1024, 1024]
_o = 0
for _n, _s in zip(_names, _sizes):
    OFF[_n] = _o
    _o += _s
D_IN = _o


def prep_ab(inp, layer, xfull):
    w = inp['w_in'][layer]
    maps = []
    for c in range(8):
        b, g = c // 4, c % 4
        sg = g // 2
        wcm = np.zeros((D, NCM), np.float32)
        wcm[:, 0:128] = w[:, OFF['ml_q'] + 128 * g: OFF['ml_q'] + 128 * g + 128]
        wcm[:, 128:256] = w[:, OFF['ml_k'] + 128 * g: OFF['ml_k'] + 128 * g + 128]
        wcm[:, 256:512] = w[:, OFF['s_x'] + 256 * g: OFF['s_x'] + 256 * g + 256]
        wcm[:, 512:640] = w[:, OFF['s_b'] + 128 * sg: OFF['s_b'] + 128 * sg + 128]
        wcm[:, 640:768] = w[:, OFF['s_c'] + 128 * sg: OFF['s_c'] + 128 * sg + 128]
        wcm[:, 768:770] = w[:, OFF['ml_i'] + 2 * g: OFF['ml_i'] + 2 * g + 2]
        wcm[:, 800:802] = w[:, OFF['ml_f'] + 2 * g: OFF['ml_f'] + 2 * g + 2]
        wcm[:, 832:834] = w[:, OFF['f_f'] + 2 * g: OFF['f_f'] + 2 * g + 2]
        wcm[:, 864:868] = w[:, OFF['s_dt'] + 4 * g: OFF['s_dt'] + 4 * g + 4]
        wtm = np.concatenate([
            w[:, OFF['ml_v'] + 256 * g: OFF['ml_v'] + 256 * g + 256],
            w[:, OFF['ml_o'] + 256 * g: OFF['ml_o'] + 256 * g + 256],
            w[:, OFF['s_z'] + 256 * g: OFF['s_z'] + 256 * g + 256],
            w[:, OFF['f_q'] + 256 * g: OFF['f_q'] + 256 * g + 256],
            w[:, OFF['f_k'] + 256 * g: OFF['f_k'] + 256 * g + 256],
            w[:, OFF['f_v'] + 256 * g: OFF['f_v'] + 256 * g + 256],
            w[:, OFF['s_dt'] + 4 * g: OFF['s_dt'] + 4 * g + 4]], axis=1)
        mcw, mcb = inp['ml_conv_w'][layer], inp['ml_conv_b'][layer]
        scw, scb = inp['ssm_conv_w'][layer], inp['ssm_conv_b'][layer]
        chans_w = np.concatenate([
            mcw[:, 128 * g:128 * g + 128], mcw[:, 512 + 128 * g: 512 + 128 * g + 128],
            scw[:, 256 * g:256 * g + 256], scw[:, 1024 + 128 * sg:1024 + 128 * sg + 128],
            scw[:, 1280 + 128 * sg:1280 + 128 * sg + 128]], axis=1)
        chans_b = np.concatenate([
            mcb[128 * g:128 * g + 128], mcb[512 + 128 * g: 512 + 128 * g + 128],
            scb[256 * g:256 * g + 256], scb[1024 + 128 * sg:1024 + 128 * sg + 128],
            scb[1280 + 128 * sg:1280 + 128 * sg + 128]])
        cw = np.ascontiguousarray(chans_w.T.reshape(6, 128, 4).transpose(1, 0, 2))
        cb = np.ascontiguousarray(chans_b.reshape(6, 128).T)
        m = {
            'x': np.ascontiguousarray(xfull[b]),
            'ng': np.ascontiguousarray(inp['norm_mix_g'][layer].reshape(1, D)),
            'wcm': wcm, 'wtm': np.ascontiguousarray(wtm), 'cw': cw, 'cb': cb,
            'sprm': np.ascontiguousarray(np.concatenate([inp['ssm_dt_bias'][layer][4 * g:4 * g + 4], inp['ssm_a_log'][layer][4 * g:4 * g + 4], inp['ssm_d'][layer][4 * g:4 * g + 4]]).reshape(1, 12).astype(np.float32)),
            'mprm': np.ascontiguousarray(np.concatenate([inp['ml_b_i'][layer][2 * g:2 * g + 2], inp['ml_b_f'][layer][2 * g:2 * g + 2]]).reshape(1, 4).astype(np.float32)),
            'mlg': np.ascontiguousarray(inp['ml_norm_g'][layer][256 * g:256 * g + 256].reshape(1, 256)),
            'fqg': np.ascontiguousarray(inp['fox_q_norm_g'][layer].reshape(1, 128)),
            'fkg': np.ascontiguousarray(inp['fox_k_norm_g'][layer].reshape(1, 128)),
            'fbf': np.ascontiguousarray(inp['fox_b_f'][layer][2 * g:2 * g + 2].reshape(1, 2)),
        }
        maps.append(m)
    return maps


def make_ident(k, name="ident"):
    idf = k.al(name + "_f", [128, 128], F32)
    idb = k.al(name + "_b", [128, 128], BF16)
    k.ins('pool', 'memset', ap=idf[:], constant=0.0)
    k.op('pool', lambda e: e.affine_select(out=idf[:].ap, in_=idf[:].ap, pattern=[[-1, 128]],
                                           compare_op=ALU.not_equal, fill=1.0, base=0, channel_multiplier=1),
         [idf], [idf])
    k.ins('dve', 'tensor_copy', out=idb[:], in_=idf[:])
    return idf, idb


def phase_p(k, x, ng, wcm, wtm, cw, cb, zcm, zg, ztm, idb, ngroups=NG):
    ngrep = k.al("ngrep", [128, D], F32)
    k.ld('sp', ngrep[:], ng[0:1, :].pbc(128))
    wcm_sb = k.al("wcm_sb", [128, 8, NCM], BF16)
    wtm_sb = k.al("wtm_sb", [128, 8, NTM], BF16)
    wst = [k.al("wst%d" % i, [128, NTM], F32) for i in range(2)]
    for kc in range(8):
        st = wst[kc % 2]
        k.ld('sp', st[:, 0:NCM], wcm[kc * 128:(kc + 1) * 128, :])
        k.ins('act', 'copy', out=wcm_sb[:, kc, :], in_=st[:, 0:NCM])
    for kc in range(8):
        st = wst[kc % 2]
        k.ld('sp', st[:, :], wtm[kc * 128:(kc + 1) * 128, :])
        k.ins('dve', 'tensor_copy', out=wtm_sb[:, kc, :], in_=st[:, :])
    cw_sb = k.al("cw_sb", [128, 6, 4], F32)
    cb_sb = k.al("cb_sb", [128, 6], F32)
    k.ld('sp', cw_sb[:], cw[:, :, :])
    k.ld('sp', cb_sb[:], cb[:, :])
    xb = [k.al("xb%d" % i, [128, D], F32) for i in range(4)]
    junk = k.al("junk", [128, D], BF16)
    ss = k.al("ss", [128, 8], F32)
    hn = [k.al("hn%d" % i, [128, D], BF16) for i in range(2)]
    hT = [k.al("hT%d" % i, [128, 8, 512], BF16) for i in range(2)]
    zc = [k.al("zc%d" % i, [128, 515], F32) for i in range(2)]
    halo = k.al("halo", [128, 6, 3], F32)
    k.ins('pool', 'memset', ap=halo[:], constant=0.0)
    acc = [k.al("acc%d" % i, [128, 512], F32) for i in range(2)]
    ob = [k.al("ob%d" % i, [128, 512], BF16) for i in range(2)]
    gsb = [k.al("gsb%d" % i, [128, 512], F32) for i in range(2)]
    tmsb = [k.al("tmsb%d" % i, [128, NTM], F32) for i in range(2)]
    rot = 0
    nt = 0
    for tg in range(ngroups):
        h_T = hT[tg % 2]
        for tt in range(4):
            r0 = tg * 512 + tt * 128
            xt = xb[tt]
            k.ld('sp', xt[:], (x(r0) if callable(x) else x[r0:r0 + 128, :]))
            k.ins('act', 'activation', out=junk[:], in_=xt[:], func=AF.Square, accum_out=ss[:, tt:tt + 1])
        k.ins('dve', 'tensor_scalar', out=ss[:, 0:4], in0=ss[:, 0:4], scalar1=1.0 / D, scalar2=EPS, op0=ALU.mult, op1=ALU.add)
        k.ins('dve', 'reciprocal', out=ss[:, 0:4], in_=ss[:, 0:4])
        k.ins('act', 'activation', out=ss[:, 4:8], in_=ss[:, 0:4], func=AF.Sqrt)
        for tt in range(4):
            xt = xb[tt]
            hb = hn[nt % 2]
            k.ins('dve', 'scalar_tensor_tensor', out=hb[:], in0=xt[:], scalar=ss[:, 4 + tt:5 + tt], in1=ngrep[:], op0=ALU.mult, op1=ALU.mult)
            p_T = k.bank(nt % 2, [128, 8, 128], BF16)
            for kc in range(8):
                k.tr(p_T[:, kc, :], hb[:, kc * 128:(kc + 1) * 128], idb[:])
            k.ins('act', 'copy', out=h_T[:, :, tt * 128:(tt + 1) * 128], in_=p_T)
            nt += 1
        c0 = tg * 512
        for cc in range(7):
            pc = k.bank(2 + rot % 4, [128, 512])
            rot += 1
            for kc in range(8):
                k.mm(pc[:, :], wcm_sb[:, kc, cc * 128:(cc + 1) * 128], h_T[:, kc, :], start=(kc == 0), stop=(kc == 7), inc=(kc == 7))
            if cc < 6:
                z = zc[cc % 2]
                k.ins('act', 'copy', out=z[:, 3:515], in_=pc[:, :])
                k.ins('dve', 'tensor_copy', out=z[:, 0:3], in_=halo[:, cc, :])
                k.ins('dve', 'tensor_copy', out=halo[:, cc, :], in_=z[:, 512:515])
                a = acc[cc % 2]
                k.ins('dve', 'tensor_scalar', out=a[:], in0=z[:, 0:512], scalar1=cw_sb[:, cc, 0:1], scalar2=cb_sb[:, cc:cc + 1], op0=ALU.mult, op1=ALU.add)
                for tap in (1, 2, 3):
                    k.ins('dve', 'scalar_tensor_tensor', out=a[:], in0=z[:, tap:tap + 512], scalar=cw_sb[:, cc, tap:tap + 1], in1=a[:], op0=ALU.mult, op1=ALU.add)
                o = ob[cc % 2]
                k.ins('act', 'activation', out=o[:], in_=a[:], func=AF.Silu)
                k.ld('sp', zcm[cc * 128:(cc + 1) * 128, c0:c0 + 512], o[:])
            else:
                gs = gsb[tg % 2]
                k.ins('act', 'copy', out=gs[:], in_=pc[:, :])
                k.ld('sp', zg[:, c0:c0 + 512], gs[:, :])
        for tt in range(4):
            tm = tmsb[tt % 2]
            for ci, (a0, a1) in enumerate(((0, 512), (512, 1024), (1024, 1536), (1536, 1540))):
                pc = k.bank(2 + rot % 4, [128, 512])
                rot += 1
                for kc in range(8):
                    k.mm(pc[:, 0:a1 - a0], h_T[:, kc, tt * 128:(tt + 1) * 128], wtm_sb[:, kc, a0:a1], start=(kc == 0), stop=(kc == 7), inc=(kc == 7))
                if ci % 2 == 0:
                    k.ins('act', 'copy', out=tm[:, a0:a1], in_=pc[:, 0:a1 - a0])
                else:
                    k.ins('dve', 'tensor_copy', out=tm[:, a0:a1], in_=pc[:, 0:a1 - a0])
            r0 = tg * 512 + tt * 128
            k.ld('sp', ztm[r0:r0 + 128, :], tm[:])


def build_ab(dbg=True, phases=('p',), ngroups=NG, nq=16, stop=9):
    nc = bass.Bass("TRN2", target_bir_lowering=False)
    k = K(nc)
    kS = "ExternalOutput" if dbg else "Internal"
    x = k.dram("x", [T, D], F32, "ExternalInput")
    ng = k.dram("ng", [1, D], F32, "ExternalInput")
    wcm = k.dram("wcm", [D, NCM], F32, "ExternalInput")
    wtm = k.dram("wtm", [D, NTM], F32, "ExternalInput")
    cw = k.dram("cw", [128, 6, 4], F32, "ExternalInput")
    cb = k.dram("cb", [128, 6], F32, "ExternalInput")
    fqg = k.dram("fqg", [1, 128], F32, "ExternalInput")
    fkg = k.dram("fkg", [1, 128], F32, "ExternalInput")
    fbf = k.dram("fbf", [1, 2], F32, "ExternalInput")
    yfox = k.dram("yfox", [T, 256], F32, "ExternalOutput")
    sprm = k.dram("sprm", [1, 12], F32, "ExternalInput")
    ysd = k.dram("ysd", [T, 256], F32, "ExternalOutput")
    mprm = k.dram("mprm", [1, 4], F32, "ExternalInput")
    mlg = k.dram("mlg", [1, 256], F32, "ExternalInput")
    yml = k.dram("yml", [T, 256], F32, "ExternalOutput")
    zcm = k.dram("zcm", [768, T], BF16, kS)
    zg = k.dram("zg", [128, T], F32, kS)
    ztm = k.dram("ztm", [T, NTM], F32, kS)
    k.init_arena()
    idf, idb = make_ident(k)
    outs = []
    if 'p' in phases:
        mp = k.mark()
        phase_p(k, x, ng, wcm, wtm, cw, cb, zcm, zg, ztm, idb, ngroups)
        outs += [zcm, zg, ztm]
        k.release(mp)
    if 'm' in phases:
        phase_m(k, zcm, zg, ztm, mprm, mlg, yml, idf, idb, nch=min(64, 4 * nq), stop=stop)
        outs += [yml]
    if 's' in phases:
        phase_s(k, zcm, zg, ztm, sprm, ysd, idf, idb, nch=min(64, 4 * nq))
        outs += [ysd]
    if 'f' in phases:
        phase_f(k, ztm, zg, fqg, fkg, fbf, yfox, idf, idb, nq, stop)
        outs += [yfox]
    k.final_wait('sp', outs)
    k.emit()
    print("instructions:", k.ninst, "sems:", len(k.sems))
    return nc


def phase_f(k, ztm, zg, fqg, fkg, fbf, yfox, idf, idb, nq=16, stop=9, ystore=None):
    m0 = k.mark()
    ntile = 4 * nq
    gain = k.al("fgain", [128, 4, 128], F32)
    k.ld('sp', gain[:, 0, :], fqg[0:1, :].pbc(128))
    k.ld('sp', gain[:, 1, :], fqg[0:1, :].pbc(128))
    k.ld('sp', gain[:, 2, :], fkg[0:1, :].pbc(128))
    k.ld('sp', gain[:, 3, :], fkg[0:1, :].pbc(128))
    k.ins('dve', 'tensor_scalar', out=gain[:, 2:4, :], in0=gain[:, 2:4, :], scalar1=128.0 ** -0.5, scalar2=None, op0=ALU.mult)
    negb = k.al("fnegb", [64, 2], F32)
    k.ld('sp', negb[:], fbf[0:1, :].pbc(64))
    k.ins('dve', 'tensor_scalar', out=negb[:], in0=negb[:], scalar1=-1.0, scalar2=None, op0=ALU.mult)
    U = k.al("fU", [64, 64], F32)
    k.ins('pool', 'memset', ap=U[:], constant=1.0)
    k.op('pool', lambda e: e.affine_select(out=U[:].ap, in_=U[:].ap, pattern=[[1, 64]], compare_op=ALU.is_gt,
                                           fill=0.0, base=0, channel_multiplier=-1), [U], [U])
    negid = k.al("fnegid", [64, 64], F32)
    k.ins('dve', 'tensor_scalar', out=negid[:], in0=idf[0:64, 0:64], scalar1=-1.0, scalar2=None, op0=ALU.mult)
    maskneg = k.al("fmask", [128, 128], F32)
    k.ins('pool', 'memset', ap=maskneg[:], constant=0.0)
    k.op('pool', lambda e: e.affine_select(out=maskneg[:].ap, in_=maskneg[:].ap, pattern=[[1, 128]], compare_op=ALU.is_ge,
                                           fill=-1e30, base=0, channel_multiplier=-1), [maskneg], [maskneg])
    ones64 = k.al("fones", [64, 128], F32)
    k.ins('pool', 'memset', ap=ones64[:], constant=1.0)
    if stop <= 1:
        k.release(m0)
        return
    gF = k.al("gF", [64, 2, 128], F32)
    if ntile < 64:
        k.ins('pool', 'memset', ap=gF[:], constant=0.0)
    for h in range(2):
        k.ld('sp', gF[0:ntile, h, :], zg[64 + h, 0:ntile * 128].re("(j s) -> j s", s=128))
    e = k.al("fe", [64, 2, 128], F32)
    for h in range(2):
        k.ins('act', 'activation', out=e[:, h, :], in_=gF[:, h, :], func=AF.Exp, scale=-1.0, bias=negb[:, h:h + 1])
    k.ins('act', 'activation', out=e[:], in_=e[:], func=AF.Ln, bias=1.0)
    cs = k.al("fcs", [64, 2, 128], F32)
    for h in range(2):
        k.ins('dve', 'tensor_tensor_scan', out=cs[:, h, :], data0=ones64[:, :], data1=e[:, h, :], initial=0.0, op0=ALU.mult, op1=ALU.add)
    pb = k.bank(0, [64, 2])
    k.mm(pb, U[:, :], cs[:, :, 127])
    offs = k.al("foffs", [64, 2], F32)
    k.ins('dve', 'tensor_copy', out=offs[:], in_=pb)
    Fneg = k.al("Fneg", [64, 2, 128], F32)
    for h in range(2):
        k.ins('dve', 'tensor_scalar', out=Fneg[:, h, :], in0=cs[:, h, :], scalar1=offs[:, h:h + 1], scalar2=None, op0=ALU.add)
    colF = k.al("colF", [128, 2, 64], F32)
    for h in range(2):
        pb = k.bank(1, [128, 64])
        k.tr(pb, Fneg[:, h, :], idf[0:64, 0:64])
        k.ins('dve', 'tensor_copy', out=colF[:, h, :], in_=pb)
    if stop <= 2:
        k.release(m0)
        return
    qT = [k.al("qT%d" % h, [128, T], BF16) for h in range(2)]
    kT = [k.al("kT%d" % h, [128, T], BF16) for h in range(2)]
    Va = [k.al("Va%d" % h, [128, 64, 130], BF16) for h in range(2)]
    for h in range(2):
        k.ins('dve', 'memset', ap=Va[h][:, :, 128:130], constant=1.0)
    if stop <= 2.1:
        k.release(m0)
        return
    qkv = [k.al("qkv%d" % i, [128, 768], F32) for i in range(2)]
    sq = k.al("fsq", [128, 512], F32)
    ssq = k.al("fssq", [128, 8], F32)
    qn = [k.al("qn%d" % i, [128, 512], BF16) for i in range(2)]
    for j in range(ntile):
        t_ = qkv[j % 2]
        k.ld('sp', t_[:], ztm[j * 128:(j + 1) * 128, 768:1536])
        k.ins('act', 'activation', out=sq[:], in_=t_[:, 0:512], func=AF.Square)
        k.ins('dve', 'tensor_reduce', out=ssq[:, 0:4], in_=sq[:].re("p (a b) -> p a b", a=4), axis=AX.X, op=ALU.add)
        k.ins('dve', 'tensor_scalar', out=ssq[:, 0:4], in0=ssq[:, 0:4], scalar1=1.0 / 128, scalar2=EPS, op0=ALU.mult, op1=ALU.add)
        k.ins('dve', 'reciprocal', out=ssq[:, 0:4], in_=ssq[:, 0:4])
        k.ins('act', 'activation', out=ssq[:, 4:8], in_=ssq[:, 0:4], func=AF.Sqrt)
        if stop <= 2.2:
            continue
        n_ = qn[j % 2]
        for seg in range(4):
            k.ins('dve', 'scalar_tensor_tensor', out=n_[:, seg * 128:(seg + 1) * 128], in0=t_[:, seg * 128:(seg + 1) * 128],
                  scalar=ssq[:, 4 + seg:5 + seg], in1=gain[:, seg, :], op0=ALU.mult, op1=ALU.mult)
        pt = k.bank(j % 2, [128, 4, 128], BF16)
        for seg in range(4):
            k.tr(pt[:, seg, :], n_[:, seg * 128:(seg + 1) * 128], idb[:])
        if stop <= 2.3:
            continue
        cols = slice(j * 128, (j + 1) * 128)
        k.ins('act', 'copy', out=qT[0][:, cols], in_=pt[:, 0, :])
        k.ins('act', 'copy', out=qT[1][:, cols], in_=pt[:, 1, :])
        k.ins('act', 'copy', out=kT[0][:, cols], in_=pt[:, 2, :])
        k.ins('act', 'copy', out=kT[1][:, cols], in_=pt[:, 3, :])
        if stop <= 2.4:
            continue
        for h in range(2):
            k.ins('act', 'copy', out=Va[h][:, j, 0:128], in_=t_[:, 512 + h * 128:512 + (h + 1) * 128])
    if stop <= 3:
        k.release(m0)
        return
    zs = [k.al("zs%d" % i, [128, 512], F32) for i in range(2)]
    ptb = [k.al("ptb%d" % i, [128, 512], BF16) for i in range(2)]
    frep = [k.al("frep%d" % i, [128, 512], F32) for i in range(2)]
    ysb = [k.al("ysb%d" % i, [128, 128], F32) for i in range(2)]
    rec = k.al("frec", [128, 4], F32)
    it = 0
    for h in range(2):
        for i in range(nq):
            fb = k.bank(0, [128, 512])
            for qq in range(4):
                jb = 4 * i + qq
                k.mm(fb[:, qq * 128:(qq + 1) * 128], negid[:, jb:jb + 1].bc([64, 128]), Fneg[:, h, :])
            fr = frep[i % 2]
            k.ins('act', 'copy', out=fr[:], in_=fb)
            O = [k.bank(4 + qq, [128, 129]) for qq in range(4)]
            for j in range(4 * i + 4):
                c0 = max(0, j - 4 * i) * 128
                n = 512 - c0
                ps = k.bank(2 + it % 2, [128, 512])
                k.mm(ps[:, 0:n], kT[h][:, j * 128:(j + 1) * 128], qT[h][:, i * 512 + c0:(i + 1) * 512])
                z = zs[it % 2]
                k.ins('dve', 'scalar_tensor_tensor', out=z[:, 0:n], in0=ps[:, 0:n], scalar=colF[:, h, j:j + 1],
                      in1=fr[:, c0:512], op0=ALU.add, op1=ALU.add)
                if j >= 4 * i:
                    k.ins('dve', 'tensor_tensor', out=z[:, 0:128], in0=z[:, 0:128], in1=maskneg[:], op=ALU.add)
                p = ptb[it % 2]
                k.ins('act', 'activation', out=p[:, 0:n], in_=z[:, 0:n], func=AF.Exp)
                for qq in range(max(j - 4 * i, 0), 4):
                    col = qq * 128 - c0
                    qb = 4 * i + qq
                    k.mm(O[qq], p[:, col:col + 128], Va[h][:, j, 0:129], start=(j == 0), stop=(j == qb))
                it += 1
            for qq in range(4):
                k.ins('dve', 'reciprocal', out=rec[:, qq:qq + 1], in_=O[qq][:, 128:129])
                y = ysb[qq % 2]
                k.ins('dve', 'tensor_scalar', out=y[:], in0=O[qq][:, 0:128], scalar1=rec[:, qq:qq + 1], scalar2=None, op0=ALU.mult)
                r0 = (4 * i + qq) * 128
                if ystore is None:
                    k.ld('sp', yfox[r0:r0 + 128, h * 128:(h + 1) * 128], y[:])
                else:
                    ystore(4 * i + qq, 512 + h * 128, 128, y[:])
    k.release(m0)


def phase_s(k, zcm, zg, ztm, sprm, ysd, idf, idb, nch=64, ystore=None):
    m0 = k.mark()
    prm = k.al("sprm", [128, 12], F32)
    k.ld('sp', prm[:], sprm[0:1, :].pbc(128))
    arep_ = k.al("sa", [128, 4], F32)
    k.ins('act', 'activation', out=arep_[:], in_=prm[:, 4:8], func=AF.Exp)
    k.ins('dve', 'tensor_scalar', out=arep_[:], in0=arep_[:], scalar1=-1.0, scalar2=None, op0=ALU.mult)
    maskneg = k.al("smask", [128, 128], F32)
    k.ins('pool', 'memset', ap=maskneg[:], constant=0.0)
    k.op('pool', lambda e: e.affine_select(out=maskneg[:].ap, in_=maskneg[:].ap, pattern=[[1, 128]], compare_op=ALU.is_ge,
                                           fill=-1e30, base=0, channel_multiplier=-1), [maskneg], [maskneg])
    ones64 = k.al("sones", [64, 128], F32)
    k.ins('pool', 'memset', ap=ones64[:], constant=1.0)
    dt = k.al("sdt", [64, 4, 128], F32)
    if nch < 64:
        k.ins('pool', 'memset', ap=dt[:], constant=0.0)
    for h in range(4):
        k.ld('sp', dt[0:nch, h, :], zg[96 + h, 0:nch * 128].re("(c t) -> c t", t=128))
    for h in range(4):
        k.ins('act', 'activation', out=dt[:, h, :], in_=dt[:, h, :], func=AF.Exp, bias=prm[0:64, h:h + 1])
    k.ins('act', 'activation', out=dt[:], in_=dt[:], func=AF.Ln, bias=1.0)
    da = k.al("sda", [64, 4, 128], F32)
    k.ins('dve', 'tensor_tensor', out=da[:], in0=dt[:], in1=arep_[0:64, :].unsq(2).bc([64, 4, 128]), op=ALU.mult)
    acum = k.al("sacum", [64, 4, 128], F32)
    for h in range(4):
        k.ins('dve', 'tensor_tensor_scan', out=acum[:, h, :], data0=ones64[:, :], data1=da[:, h, :], initial=0.0, op0=ALU.mult, op1=ALU.add)
    colA = k.al("scolA", [128, 4, 64], F32)
    colDT = k.al("scolDT", [128, 4, 64], F32)
    for h in range(4):
        pb = k.bank(h % 2, [128, 64])
        k.tr(pb, acum[:, h, :], idf[0:64, 0:64])
        k.ins('dve', 'tensor_copy', out=colA[:, h, :], in_=pb)
        pb2 = k.bank(2 + h % 2, [128, 64])
        k.tr(pb2, dt[:, h, :], idf[0:64, 0:64])
        k.ins('dve', 'tensor_copy', out=colDT[:, h, :], in_=pb2)
    Drep = k.al("sDrep", [128, 4, 64], F32)
    k.ins('dve', 'tensor_copy', out=Drep[:], in_=prm[:, 8:12].unsq(2).bc([128, 4, 64]))
    state = k.al("sstate", [128, 256], F32)
    state_bf = k.al("sstate_bf", [128, 256], BF16)
    k.ins('dve', 'memset', ap=state[:], constant=0.0)
    k.ins('dve', 'memset', ap=state_bf[:], constant=0.0)
    xT = [k.al("sxT%d" % i, [128, 2, 128], BF16) for i in range(2)]
    BT = [k.al("sBT%d" % i, [128, 128], BF16) for i in range(2)]
    CT = [k.al("sCT%d" % i, [128, 128], BF16) for i in range(2)]
    xbt = [k.al("sxbt%d" % i, [128, 384], BF16) for i in range(2)]
    arsb = [k.al("sarsb%d" % i, [128, 4, 128], F32) for i in range(2)]
    dm = [k.al("sdm%d" % i, [128, 4, 128], F32) for i in range(2)]
    mt = [k.al("smt%d" % i, [128, 4, 128], BF16) for i in range(2)]
    ea = [k.al("sea%d" % i, [128, 4, 128], F32) for i in range(2)]
    cts = [k.al("scts%d" % i, [128, 4, 128], BF16) for i in range(2)]
    xdt = [k.al("sxdt%d" % i, [128, 4, 64], BF16) for i in range(2)]
    xdd = [k.al("sxdd%d" % i, [128, 4, 64], BF16) for i in range(2)]
    decs = k.al("sdecs", [128, 8], F32)
    yd = k.al("syd", [128, 64, 256], F32)
    for c in range(nch):
        i2 = c % 2
        cols = slice(c * 128, (c + 1) * 128)
        k.ld('sp', xT[i2][:], zcm[256:512, cols].re("(a p) t -> p a t", p=128))
        k.ld('sp', BT[i2][:], zcm[512:640, cols])
        k.ld('sp', CT[i2][:], zcm[640:768, cols])
        ptr = k.bank(0, [128, 3, 128], BF16)
        k.tr(ptr[:, 0, :], xT[i2][:, 0, :], idb[:])
        k.tr(ptr[:, 1, :], xT[i2][:, 1, :], idb[:])
        k.tr(ptr[:, 2, :], BT[i2][:], idb[:])
        k.ins('act', 'copy', out=xbt[i2][:], in_=ptr)
        pg = k.bank(1, [128, 128])
        k.mm(pg, BT[i2][:], CT[i2][:])
        par = k.bank(2 + i2, [128, 512])
        k.mm(par, idf[0:64, c:c + 1].bc([64, 128]), acum[:].re("c h t -> c (h t)"))
        ar = arsb[i2]
        k.ins('act', 'copy', out=ar[:], in_=par)
        d_ = dm[i2]
        k.ins('dve', 'tensor_tensor', out=d_[:], in0=ar[:], in1=colA[:, :, c:c + 1].bc([128, 4, 128]), op=ALU.subtract)
        k.ins('dve', 'tensor_tensor', out=d_[:], in0=d_[:], in1=maskneg[:].unsq(1).bc([128, 4, 128]), op=ALU.add)
        k.ins('act', 'activation', out=d_[:], in_=d_[:], func=AF.Exp)
        k.ins('dve', 'tensor_tensor', out=mt[i2][:], in0=d_[:], in1=pg.unsq(1).bc([128, 4, 128]), op=ALU.mult)
        xv = xbt[i2][:, 0:256].re("p (h q) -> p h q", h=4)
        k.ins('dve', 'tensor_tensor', out=xdt[i2][:], in0=xv, in1=colDT[:, :, c:c + 1].bc([128, 4, 64]), op=ALU.mult)
        k.ins('act', 'activation', out=ea[i2][:], in_=ar[:], func=AF.Exp)
        k.ins('dve', 'tensor_tensor', out=cts[i2][:], in0=ea[i2][:], in1=CT[i2][:].unsq(1).bc([128, 4, 128]), op=ALU.mult)
        py = k.bank(4 + i2, [128, 256])
        for h in range(4):
            k.mm(py[:, h * 64:(h + 1) * 64], mt[i2][:, h, :], xdt[i2][:, h, :], start=True, stop=False)
            k.mm(py[:, h * 64:(h + 1) * 64], cts[i2][:, h, :], state_bf[:, h * 64:(h + 1) * 64], start=False, stop=True)
        ydc = yd[:, c, :]
        k.ins('dve', 'tensor_tensor', out=ydc, in0=xbt[i2][:, 0:256], in1=Drep[:].re("p h q -> p (h q)"), op=ALU.mult)
        k.ins('dve', 'tensor_tensor', out=ydc, in0=ydc, in1=py, op=ALU.add)
        k.ins('dve', 'tensor_tensor', out=decs[:, 0:4], in0=ar[:, :, 127], in1=colA[:, :, c], op=ALU.subtract)
        k.ins('act', 'activation', out=decs[:, 4:8], in_=decs[:, 0:4], func=AF.Exp)
        k.ins('dve', 'tensor_tensor', out=xdd[i2][:], in0=xdt[i2][:], in1=decs[:, 4:8].unsq(2).bc([128, 4, 64]), op=ALU.mult)
        pst = k.bank(6, [128, 256])
        k.mm(pst, xbt[i2][:, 256:384], xdd[i2][:].re("p h q -> p (h q)"))
        sv = state[:].re("p (h q) -> p h q", h=4)
        k.ins('dve', 'tensor_tensor', out=sv, in0=sv, in1=ea[i2][:, :, 127:128].bc([128, 4, 64]), op=ALU.mult)
        k.ins('dve', 'tensor_tensor', out=state[:], in0=state[:], in1=pst, op=ALU.add)
        k.ins('act', 'copy', out=state_bf[:], in_=state[:])
    zt = [k.al("szt%d" % i, [128, 256], F32) for i in range(2)]
    yo = [k.al("syo%d" % i, [128, 256], F32) for i in range(2)]
    for c in range(nch):
        i2 = c % 2
        rows = slice(c * 128, (c + 1) * 128)
        k.ld('sp', zt[i2][:], ztm[rows, 512:768])
        k.ins('act', 'activation', out=zt[i2][:], in_=zt[i2][:], func=AF.Silu)
        k.ins('dve', 'tensor_tensor', out=yo[i2][:], in0=zt[i2][:], in1=yd[:, c, :], op=ALU.mult)
        if ystore is None:
            k.ld('sp', ysd[rows, :], yo[i2][:])
        else:
            ystore(c, 256, 256, yo[i2][:])
    k.release(m0)


def phase_m(k, zcm, zg, ztm, mprm, mlg, yml, idf, idb, nch=64, stop=9, ystore=None):
    import math
    LN8 = math.log(0.125)
    m0 = k.mark()
    prm = k.al("mprm", [64, 4], F32)
    k.ld('sp', prm[:], mprm[0:1, :].pbc(64))
    nbf = k.al("mnbf", [64, 2], F32)
    k.ins('dve', 'tensor_scalar', out=nbf[:], in0=prm[:, 2:4], scalar1=-1.0, scalar2=None, op0=ALU.mult)
    grep = k.al("mgrep", [128, 256], F32)
    k.ld('sp', grep[:], mlg[0:1, :].pbc(128))
    maskneg = k.al("mmask", [128, 128], F32)
    k.ins('pool', 'memset', ap=maskneg[:], constant=0.0)
    k.op('pool', lambda e: e.affine_select(out=maskneg[:].ap, in_=maskneg[:].ap, pattern=[[1, 128]], compare_op=ALU.is_ge,
                                           fill=-1e30, base=0, channel_multiplier=-1), [maskneg], [maskneg])
    ones64 = k.al("mones", [64, 128], F32)
    k.ins('pool', 'memset', ap=ones64[:], constant=1.0)
    zeros64 = k.al("mzeros", [64, 128], F32)
    k.ins('pool', 'memset', ap=zeros64[:], constant=0.0)
    hmask = k.al("mhmask", [128, 2], F32)
    k.ins('pool', 'memset', ap=hmask[:], constant=1.0)
    k.op('pool', lambda e: e.affine_select(out=hmask[:, 0:1].ap, in_=hmask[:, 0:1].ap, pattern=[[0, 1]], compare_op=ALU.is_ge,
                                           fill=0.0, base=63, channel_multiplier=-1), [hmask], [hmask])
    k.op('pool', lambda e: e.affine_select(out=hmask[:, 1:2].ap, in_=hmask[:, 1:2].ap, pattern=[[0, 1]], compare_op=ALU.is_ge,
                                           fill=0.0, base=-64, channel_multiplier=1), [hmask], [hmask])
    ir = k.al("mir", [64, 2, 128], F32)
    fr = k.al("mfr", [64, 2, 128], F32)
    if nch < 64:
        k.ins('pool', 'memset', ap=ir[:], constant=0.0)
        k.ins('pool', 'memset', ap=fr[:], constant=0.0)
    for h in range(2):
        k.ld('sp', ir[0:nch, h, :], zg[h, 0:nch * 128].re("(c t) -> c t", t=128))
        k.ld('sp', fr[0:nch, h, :], zg[32 + h, 0:nch * 128].re("(c t) -> c t", t=128))
    for h in range(2):
        k.ins('act', 'activation', out=fr[:, h, :], in_=fr[:, h, :], func=AF.Exp, scale=-1.0, bias=nbf[:, h:h + 1])
    k.ins('act', 'activation', out=fr[:], in_=fr[:], func=AF.Ln, bias=1.0)
    bneg = k.al("mbneg", [64, 2, 128], F32)
    a_ = k.al("ma", [64, 2, 128], F32)
    cm = k.al("mcm", [64, 2, 128], F32)
    for h in range(2):
        k.ins('dve', 'tensor_tensor_scan', out=bneg[:, h, :], data0=ones64[:, :], data1=fr[:, h, :], initial=0.0, op0=ALU.mult, op1=ALU.add)
    for h in range(2):
        k.ins('dve', 'scalar_tensor_tensor', out=a_[:, h, :], in0=ir[:, h, :], scalar=prm[:, h:h + 1], in1=bneg[:, h, :], op0=ALU.add, op1=ALU.add)
    for h in range(2):
        k.ins('dve', 'tensor_tensor_scan', out=cm[:, h, :], data0=zeros64[:, :], data1=a_[:, h, :], initial=-1e30, op0=ALU.add, op1=ALU.max)
    gq = k.al("mgq", [64, 2], F32)
    cmq = k.al("mcmq", [64, 2], F32)
    k.ins('dve', 'tensor_scalar', out=gq[:], in0=bneg[:, :, 127], scalar1=-1.0, scalar2=None, op0=ALU.mult)
    k.ins('dve', 'tensor_copy', out=cmq[:], in_=cm[:, :, 127])
    if stop <= 1:
        k.release(m0)
        return
    gT = k.al("mgT", [2, 64], F32)
    cT = k.al("mcT", [2, 64], F32)
    pb = k.bank(0, [2, 64])
    k.tr(pb, gq[:], idf[0:64, 0:64])
    k.ins('dve', 'tensor_copy', out=gT[:], in_=pb)
    pb = k.bank(1, [2, 64])
    k.tr(pb, cmq[:], idf[0:64, 0:64])
    k.ins('dve', 'tensor_copy', out=cT[:], in_=pb)
    mnext = k.al("mmnext", [2, 64], F32)
    k.ins('dve', 'tensor_tensor_scan', out=mnext[:], data0=cT[:], data1=gT[:], initial=0.0, op0=ALU.max, op1=ALU.add)
    Mrow = k.al("mMrow", [2, 64], F32)
    k.ins('dve', 'memset', ap=Mrow[:], constant=0.0)
    k.ins('dve', 'tensor_copy', out=Mrow[:, 1:64], in_=mnext[:, 0:63])
    Mcol = k.al("mMcol", [64, 2], F32)
    mncol = k.al("mmncol", [64, 2], F32)
    pb = k.bank(2, [64, 2])
    k.tr(pb, Mrow[:], idf[0:2, 0:2])
    k.ins('dve', 'tensor_copy', out=Mcol[:], in_=pb)
    pb = k.bank(3, [64, 2])
    k.tr(pb, mnext[:], idf[0:2, 0:2])
    k.ins('dve', 'tensor_copy', out=mncol[:], in_=pb)
    if stop <= 2:
        k.release(m0)
        return
    mt_ = k.al("mmt", [64, 2, 128], F32)
    k.ins('dve', 'tensor_tensor', out=mt_[:], in0=cm[:], in1=Mcol[:].unsq(2).bc([64, 2, 128]), op=ALU.max)
    k.ins('dve', 'tensor_tensor', out=mt_[:], in0=mt_[:], in1=bneg[:], op=ALU.subtract)
    R = k.al("mR", [64, 258], F32)
    Rv = R[:, 0:256].re("p (h t) -> p h t", h=2)
    k.ins('dve', 'tensor_tensor', out=Rv, in0=bneg[:], in1=mt_[:], op=ALU.add)
    k.ins('dve', 'tensor_scalar', out=Rv, in0=Rv, scalar1=-1.0, scalar2=None, op0=ALU.mult)
    k.ins('dve', 'tensor_tensor', out=R[:, 256:258], in0=gq[:], in1=Mcol[:], op=ALU.add)
    k.ins('dve', 'tensor_tensor', out=R[:, 256:258], in0=R[:, 256:258], in1=mncol[:], op=ALU.subtract)
    Q = k.al("mQ", [64, 8, 128], F32)
    gm = k.al("mgm", [64, 2], F32)
    k.ins('dve', 'tensor_scalar', out=Q[:, 0:2, :], in0=a_[:], scalar1=LN8, scalar2=None, op0=ALU.add)
    k.ins('dve', 'tensor_tensor', out=gm[:], in0=gq[:], in1=mncol[:], op=ALU.subtract)
    k.ins('dve', 'tensor_tensor', out=Q[:, 2:4, :], in0=a_[:], in1=gm[:].unsq(2).bc([64, 2, 128]), op=ALU.add)
    k.ins('dve', 'tensor_tensor', out=Q[:, 4:6, :], in0=Rv, in1=Mcol[:].unsq(2).bc([64, 2, 128]), op=ALU.add)
    k.ins('dve', 'tensor_scalar', out=Q[:, 4:6, :], in0=Q[:, 4:6, :], scalar1=LN8, scalar2=None, op0=ALU.add)
    k.ins('dve', 'tensor_scalar', out=Q[:, 6:8, :], in0=mt_[:], scalar1=-1.0, scalar2=None, op0=ALU.mult)
    colQ = k.al("mcolQ", [128, 8, 64], F32)
    for q in range(8):
        pb = k.bank(q % 4, [128, 64])
        k.tr(pb, Q[:, q, :], idf[0:64, 0:64])
        k.ins('dve', 'tensor_copy', out=colQ[:, q, :], in_=pb)
    k.ins('act', 'activation', out=colQ[:, 2:8, :], in_=colQ[:, 2:8, :], func=AF.Exp)
    if stop <= 3:
        k.release(m0)
        return
    state = k.al("mstate", [128, 130], F32)
    state_bf = k.al("mstate_bf", [128, 2, 130], BF16)
    kz = [k.al("mkz%d" % i, [128, 2, 128], BF16) for i in range(2)]
    k.ins('dve', 'memset', ap=state[:], constant=0.0)
    k.ins('dve', 'memset', ap=state_bf[:], constant=0.0)
    qk = [k.al("mqk%d" % i, [128, 2, 128], BF16) for i in range(2)]
    vo = [k.al("mvo%d" % i, [128, 512], F32) for i in range(2)]
    Va = [k.al("mVa%d" % i, [128, 2, 130], BF16) for i in range(2)]
    for i in range(2):
        k.ins('dve', 'memset', ap=Va[i][:, :, 128:130], constant=1.0)
    kw = [k.al("mkw%d" % i, [128, 2, 64], BF16) for i in range(2)]
    rrsb = [k.al("mrr%d" % i, [128, 258], F32) for i in range(2)]
    dmx = [k.al("mdmx%d" % i, [128, 2, 128], F32) for i in range(2)]
    pm = [k.al("mpm%d" % i, [128, 2, 128], BF16) for i in range(2)]
    o1 = [k.al("mo1%d" % i, [128, 2, 129], F32) for i in range(2)]
    osb = [k.al("mos%d" % i, [128, 2, 129], F32) for i in range(2)]
    sm = k.al("msm", [128, 16], F32)
    hsb = [k.al("mhs%d" % i, [128, 2, 128], F32) for i in range(2)]
    junk = k.al("mjunk", [128, 128], F32)
    sg = [k.al("msg%d" % i, [128, 256], F32) for i in range(2)]
    yo = [k.al("myo%d" % i, [128, 256], F32) for i in range(2)]
    for c in range(nch):
        i2 = c % 2
        cols = slice(c * 128, (c + 1) * 128)
        rows = slice(c * 128, (c + 1) * 128)
        k.ld('sp', qk[i2][:], zcm[0:256, cols].re("(a p) t -> p a t", p=128))
        k.ld('sp', vo[i2][:], ztm[rows, 0:512])
        k.ins('act', 'copy', out=Va[i2][:, :, 0:128], in_=vo[i2][:, 0:256].re("p (h v) -> p h v", h=2))
        ptk = k.bank(0, [128, 128], BF16)
        k.tr(ptk, qk[i2][:, 1, :], idb[:])
        for h in range(2):
            k.ins('act', 'activation', out=kw[i2][:, h, :], in_=ptk[:, h * 64:(h + 1) * 64], func=AF.Copy, scale=colQ[:, 2 + h, c:c + 1])
        if stop <= 4:
            continue
        pS = k.bank(1, [128, 2, 128])
        for h in range(2):
            k.ins('dve', 'tensor_scalar', out=kz[i2][:, h, :], in0=qk[i2][:, 1, :], scalar1=hmask[:, h:h + 1], scalar2=None, op0=ALU.mult)
        for h in range(2):
            k.mm(pS[:, h, :], kz[i2][:, h, :], qk[i2][:, 0, :])
        if stop <= 4.2:
            continue
        prr = k.bank(2 + i2, [128, 258])
        k.mm(prr, idf[0:64, c:c + 1].bc([64, 128]), R[:, :])
        rr = rrsb[i2]
        k.ins('act', 'copy', out=rr[:], in_=prr)
        if stop <= 4.4:
            continue
        d_ = dmx[i2]
        k.ins('dve', 'tensor_tensor', out=d_[:], in0=rr[:, 0:256].re("p (h t) -> p h t", h=2), in1=colQ[:, 0:2, c:c + 1].bc([128, 2, 128]), op=ALU.add)
        k.ins('dve', 'tensor_tensor', out=d_[:], in0=d_[:], in1=maskneg[:].unsq(1).bc([128, 2, 128]), op=ALU.add)
        k.ins('act', 'activation', out=d_[:], in_=d_[:], func=AF.Exp)
        if stop <= 4.6:
            continue
        k.ins('dve', 'tensor_tensor', out=pm[i2][:], in0=d_[:], in1=pS, op=ALU.mult)
        if stop <= 5:
            continue
        pO1 = k.bank(4, [128, 2, 129])
        pO2 = k.bank(5, [128, 2, 129])
        for h in range(2):
            k.mm(pO1[:, h, :], pm[i2][:, h, :], Va[i2][:, h, 0:129])
        for h in range(2):
            k.mm(pO2[:, h, :], qk[i2][:, 0, :], state_bf[:, h, 0:129])
        k.ins('act', 'copy', out=o1[i2][:], in_=pO1)
        for h in range(2):
            k.ins('dve', 'scalar_tensor_tensor', out=osb[i2][:, h, :], in0=pO2[:, h, :], scalar=colQ[:, 4 + h, c:c + 1],
                  in1=o1[i2][:, h, :], op0=ALU.mult, op1=ALU.add)
        k.ins('dve', 'tensor_scalar', out=sm[:, 12:14], in0=osb[i2][:, :, 128], scalar1=-1.0, scalar2=None, op0=ALU.mult)
        k.ins('dve', 'tensor_tensor', out=sm[:, 0:2], in0=sm[:, 12:14], in1=osb[i2][:, :, 128], op=ALU.max)
        k.ins('dve', 'tensor_tensor', out=sm[:, 0:2], in0=sm[:, 0:2], in1=colQ[:, 6:8, c], op=ALU.max)
        k.ins('dve', 'reciprocal', out=sm[:, 2:4], in_=sm[:, 0:2])
        for h in range(2):
            k.ins('dve', 'tensor_scalar', out=hsb[i2][:, h, :], in0=osb[i2][:, h, 0:128], scalar1=sm[:, 2 + h:3 + h], scalar2=None, op0=ALU.mult)
        if stop <= 6:
            continue
        for h in range(2):
            k.ins('act', 'activation', out=junk[:], in_=hsb[i2][:, h, :], func=AF.Square, accum_out=sm[:, 4 + h:5 + h])
        k.ins('act', 'activation', out=sm[:, 6:8], in_=sm[:, 4:6], func=AF.Ln, scale=1.0 / 128, bias=EPS)
        k.ins('act', 'activation', out=sm[:, 8:10], in_=sm[:, 6:8], func=AF.Exp, scale=-0.5)
        k.ins('act', 'activation', out=sg[i2][:], in_=vo[i2][:, 256:512], func=AF.Exp, scale=-1.0)
        k.ins('dve', 'tensor_scalar', out=sg[i2][:], in0=sg[i2][:], scalar1=1.0, scalar2=None, op0=ALU.add)
        k.ins('dve', 'reciprocal', out=sg[i2][:], in_=sg[i2][:])
        for h in range(2):
            k.ins('dve', 'scalar_tensor_tensor', out=yo[i2][:, h * 128:(h + 1) * 128], in0=hsb[i2][:, h, :], scalar=sm[:, 8 + h:9 + h],
                  in1=grep[:, h * 128:(h + 1) * 128], op0=ALU.mult, op1=ALU.mult)
        k.ins('dve', 'tensor_tensor', out=yo[i2][:], in0=yo[i2][:], in1=sg[i2][:], op=ALU.mult)
        if ystore is None:
            k.ld('sp', yml[rows, :], yo[i2][:])
        else:
            ystore(c, 0, 256, yo[i2][:])
        if stop <= 7:
            continue
        pst = k.bank(6, [128, 260])
        k.mm(pst, kw[i2][:].re("p h d -> p (h d)"), Va[i2][:].re("p h v -> p (h v)"))
        k.ins('act', 'activation', out=sm[:, 10:12], in_=rr[:, 256:258], func=AF.Exp)
        for h in range(2):
            ps_ = slice(64 * h, 64 * h + 64)
            k.ins('dve', 'scalar_tensor_tensor', out=state[ps_, 0:129], in0=state[ps_, 0:129], scalar=sm[ps_, 10 + h:11 + h],
                  in1=pst[ps_, 130 * h:130 * h + 129], op0=ALU.mult, op1=ALU.add)
        for h in range(2):
            k.ins('act', 'activation', out=state_bf[:, h, :], in_=state[:], func=AF.Copy, scale=hmask[:, h:h + 1])
    k.release(m0)


TT = 2048
NT = TT // 128
D = 1024
NE = 16384


def load_w_bf16(k, dst, src, ncols, stage, row_scale=None):
    kcn = src[:, :].shape[0] // 128
    i = 0
    for kc in range(kcn):
        for c0 in range(0, ncols, 1024):
            c1 = min(ncols, c0 + 1024)
            st = stage[i % 2]
            i += 1
            k.ld('sp', st[:, 0:c1 - c0], src[kc * 128:(kc + 1) * 128, c0:c1])
            if row_scale is None:
                k.ins('act', 'copy', out=dst[:, kc, c0:c1], in_=st[:, 0:c1 - c0])
            else:
                k.ins('act', 'activation', out=dst[:, kc, c0:c1], in_=st[:, 0:c1 - c0], func=AF.Copy, scale=row_scale[:, kc:kc + 1])


def rms_tile(k, xt, grep, hb, junk, ss, idx):
    a, b, c = 3 * idx, 3 * idx + 1, 3 * idx + 2
    k.ins('act', 'activation', out=junk[:], in_=xt, func=AF.Square, accum_out=ss[:, a:a + 1])
    k.ins('dve', 'tensor_scalar', out=ss[:, b:b + 1], in0=ss[:, a:a + 1], scalar1=1.0 / D, scalar2=EPS, op0=ALU.mult, op1=ALU.add)
    k.ins('dve', 'reciprocal', out=ss[:, b:b + 1], in_=ss[:, b:b + 1])
    k.ins('act', 'activation', out=ss[:, c:c + 1], in_=ss[:, b:b + 1], func=AF.Sqrt)
    k.ins('dve', 'scalar_tensor_tensor', out=hb, in0=xt, scalar=ss[:, c:c + 1], in1=grep[:], op0=ALU.mult, op1=ALU.mult)


def transpose8(k, dstT, hb, idb, bank):
    p_T = k.bank(bank, [128, 8, 128], BF16)
    for kc in range(8):
        k.tr(p_T[:, kc, :], hb[:, kc * 128:(kc + 1) * 128], idb[:])
    k.ins('act', 'copy', out=dstT, in_=p_T)


def phase_c1(k, d, idf, idb, nt=NT, fused=False):
    m0 = k.mark()
    stage = [k.al("c1st%d" % i, [128, 1024], F32) for i in range(2)]
    grep = k.al("c1g", [128, D], F32)
    k.ld('sp', grep[:], d['ng_mix'][0:1, :].pbc(128))
    sng = k.al("c1sng", [128, 8], F32)
    k.ld('sp', sng[:], d['sng'][:, :])
    wg = k.al("c1wg", [128, 8, 3072], BF16)
    load_w_bf16(k, wg, d['wg'], 3072, stage)
    wb = {}
    for nm in ('w_ml', 'w_ssm', 'w_fox', 'w_out'):
        wb[nm] = k.al("c1" + nm, [128, 8, D], BF16)
        load_w_bf16(k, wb[nm], d[nm], D, stage, row_scale=(sng if nm == 'w_ssm' else None))
    xb = [k.al("c1x%d" % i, [128, D], F32) for i in range(2)]
    junk = k.al("c1junk", [128, D], BF16)
    ss = k.al("c1ss", [128, 12], F32)
    hb = k.al("c1hb", [128, D], BF16)
    hT = k.al("c1hT", [128, 8, 128], BF16)
    sg = k.al("c1sg", [128, 3072], F32)
    yst = [k.al("c1yst%d" % i, [128, 8, 128], F32) for i in range(2)] if not fused else None
    ybf = [k.al("c1ybf%d" % i, [128, 8, 128], BF16) for i in range(3)]
    ytm = k.al("c1ytm", [128, D], F32) if not fused else None
    mg = k.al("c1mg", [128, D], F32)
    tmp = k.al("c1tmp", [128, 512], F32)
    mgb = k.al("c1mgb", [128, D], BF16)
    mT = k.al("c1mT", [128, 8, 128], BF16)
    x1 = [k.al("c1x1%d" % i, [128, D], F32) for i in range(2)]
    if fused:
        bm = k.al("c1bm", [128, 4], F32)
        k.ld('sp', bm[:], d['bmask'][0:1, :].pbc(128))
        y3a = k.al("c1y3a", [128, 4, 768], F32)
        y3b = k.al("c1y3b", [128, 4, 768], F32)
        ytb = k.al("c1ytb", [128, 3, D], BF16)
    rot = 0
    for tt in range(nt):
        rows = slice(tt * 128, (tt + 1) * 128)
        xt = xb[tt % 2]
        k.ld('sp', xt[:], d['xtok'][rows, :])
        rms_tile(k, xt[:], grep, hb[:], junk, ss, 0)
        transpose8(k, hT[:], hb, idb, 0)
        for cc in range(6):
            pc = k.bank(2 + rot % 4, [128, 512])
            rot += 1
            for kc in range(8):
                k.mm(pc, hT[:, kc, :], wg[:, kc, cc * 512:(cc + 1) * 512], start=(kc == 0), stop=(kc == 7), inc=(kc == 7))
            k.ins('act', 'activation', out=sg[:, cc * 512:(cc + 1) * 512], in_=pc, func=AF.Sigmoid)
        if not fused:
            for bi, nm in enumerate(('ymlT', 'ysdT', 'yfoxT')):
                st = yst[bi % 2]
                k.ld('sp', st[:], d[nm][:, rows].re("(kc p) t -> p kc t", p=128))
                k.ins('act', 'copy', out=ybf[bi][:], in_=st[:])
            k.ld('sp', ytm[:], d['ysd_tm'][rows, :])
            for grp in range(2):
                k.ins('act', 'activation', out=junk[:, 0:512], in_=ytm[:, grp * 512:(grp + 1) * 512], func=AF.Square, accum_out=ss[:, 3 + grp:4 + grp])
        else:
            for rr in range(4):
                for q in range(4):
                    k.ld('sp', y3b[:, q, :], d['ydst'][q * 8 + tt // 2, rr, (tt % 2) * 128:(tt % 2) * 128 + 128, :])
                k.ins('dve', 'tensor_scalar', out=y3a[:, rr, :], in0=y3b[:, 0, :], scalar1=bm[:, 0:1], scalar2=None, op0=ALU.mult)
                for q in range(1, 4):
                    k.ins('dve', 'scalar_tensor_tensor', out=y3a[:, rr, :], in0=y3b[:, q, :], scalar=bm[:, q:q + 1], in1=y3a[:, rr, :], op0=ALU.mult, op1=ALU.add)
            for bi in range(3):
                k.ins('act', 'copy', out=ytb[:, bi, :].re("p (r c) -> p r c", r=4), in_=y3a[:, :, bi * 256:(bi + 1) * 256])
            for bi in range(3):
                transpose8(k, ybf[bi][:], ytb[:, bi, :], idb, bi % 2)
            for grp in range(2):
                k.ins('act', 'activation', out=junk[:, 0:512].re("p (r c) -> p r c", r=2), in_=y3a[:, 2 * grp:2 * grp + 2, 256:512], func=AF.Square, accum_out=ss[:, 3 + grp:4 + grp])
        k.ins('dve', 'tensor_scalar', out=ss[:, 5:7], in0=ss[:, 3:5], scalar1=1.0 / 512, scalar2=EPS, op0=ALU.mult, op1=ALU.add)
        k.ins('dve', 'reciprocal', out=ss[:, 5:7], in_=ss[:, 5:7])
        k.ins('act', 'activation', out=ss[:, 7:9], in_=ss[:, 5:7], func=AF.Sqrt)
        for half in range(2):
            hs = slice(half * 512, (half + 1) * 512)
            pc = k.bank(2 + rot % 4, [128, 512])
            rot += 1
            for kc in range(8):
                k.mm(pc, ybf[0][:, kc, :], wb['w_ml'][:, kc, hs], start=(kc == 0), stop=(kc == 7), inc=(kc == 7))
            k.ins('dve', 'tensor_tensor', out=mg[:, hs], in0=sg[:, hs], in1=pc, op=ALU.mult)
            pc = k.bank(2 + rot % 4, [128, 512])
            rot += 1
            for kc in range(8):
                k.mm(pc, ybf[2][:, kc, :], wb['w_fox'][:, kc, hs], start=(kc == 0), stop=(kc == 7), inc=(kc == 7))
            k.ins('dve', 'tensor_tensor', out=tmp[:], in0=sg[:, 2048 + half * 512:2048 + (half + 1) * 512], in1=pc, op=ALU.mult)
            k.ins('dve', 'tensor_tensor', out=mg[:, hs], in0=mg[:, hs], in1=tmp[:], op=ALU.add)
            pg = []
            for grp in range(2):
                pc = k.bank(2 + rot % 4, [128, 512])
                rot += 1
                for q in range(4):
                    kc = grp * 4 + q
                    k.mm(pc, ybf[1][:, kc, :], wb['w_ssm'][:, kc, hs], start=(q == 0), stop=(q == 3), inc=(q == 3))
                pg.append(pc)
            k.ins('dve', 'tensor_scalar', out=tmp[:], in0=pg[0], scalar1=ss[:, 7:8], scalar2=None, op0=ALU.mult)
            k.ins('dve', 'scalar_tensor_tensor', out=tmp[:], in0=pg[1], scalar=ss[:, 8:9], in1=tmp[:], op0=ALU.mult, op1=ALU.add)
            k.ins('dve', 'tensor_tensor', out=tmp[:], in0=tmp[:], in1=sg[:, 1024 + half * 512:1024 + (half + 1) * 512], op=ALU.mult)
            k.ins('dve', 'tensor_tensor', out=mg[:, hs], in0=mg[:, hs], in1=tmp[:], op=ALU.add)
        k.ins('act', 'copy', out=mgb[:], in_=mg[:])
        transpose8(k, mT[:], mgb, idb, 1)
        xo = x1[tt % 2]
        for half in range(2):
            hs = slice(half * 512, (half + 1) * 512)
            pc = k.bank(2 + rot % 4, [128, 512])
            rot += 1
            for kc in range(8):
                k.mm(pc, mT[:, kc, :], wb['w_out'][:, kc, hs], start=(kc == 0), stop=(kc == 7), inc=(kc == 7))
            k.ins('dve', 'tensor_tensor', out=xo[:, hs], in0=xt[:, hs], in1=pc, op=ALU.add)
        k.ld('sp', d['x1s'][rows, :], xo[:])
    k.release(m0)


def phase_c2a(k, d, idb, nblk=32):
    m0 = k.mark()
    ust = [k.al("c2ust%d" % i, [128, 4, D], F32) for i in range(2)]
    ub = [k.al("c2ub%d" % i, [128, 4, D], BF16) for i in range(2)]
    uts = [k.al("c2uts%d" % i, [128, 8, 512], BF16) for i in range(2)]
    vst = [k.al("c2vst%d" % i, [128, 4, D], F32) for i in range(2)]
    vbb = [k.al("c2vbb%d" % i, [128, 4, D], BF16) for i in range(2)]
    UTv = d['UT'][:, :, :].re("kc p e -> p kc e")
    for blk in range(nblk):
        i2 = blk % 2
        e0 = blk * 512
        k.ld('sp', ust[i2][:], d['peer_u'][e0:e0 + 512, :].re("(a p) d -> p a d", p=128))
        k.ins('act', 'copy', out=ub[i2][:], in_=ust[i2][:])
        for kc in range(8):
            pT = k.bank(kc % 4, [128, 4, 128], BF16)
            for a in range(4):
                k.tr(pT[:, a, :], ub[i2][:, a, kc * 128:(kc + 1) * 128], idb[:])
            k.ins('act', 'copy', out=uts[i2][:, kc, :], in_=pT)
        k.ld('act', UTv[:, :, e0:e0 + 512], uts[i2][:])
        k.ld('sp', vst[i2][:], d['peer_v'][e0:e0 + 512, :].re("(a p) d -> p a d", p=128))
        k.ins('dve', 'tensor_copy', out=vbb[i2][:], in_=vst[i2][:])
        k.ld('act', d['Vb'][e0:e0 + 512, :].re("(a p) d -> p a d", p=128), vbb[i2][:])
    k.release(m0)


def phase_c2b(k, d, xnT, idb, nt=NT):
    m0 = k.mark()
    stage = [k.al("c2st%d" % i, [128, 1024], F32) for i in range(2)]
    grep = k.al("c2g", [128, D], F32)
    k.ld('sp', grep[:], d['ng_ffn'][0:1, :].pbc(128))
    wq = k.al("c2wq", [128, 8, 2048], BF16)
    load_w_bf16(k, wq, d['w_q'], 2048, stage)
    kst = k.al("c2kst", [128, 16, 128], F32)
    k.ld('sp', kst[:], d['keysT'][:, :, :].re("j p i -> p j i"))
    keys = k.al("c2keys", [128, 16, 128], BF16)
    k.ins('act', 'copy', out=keys[:], in_=kst[:])
    xb = [k.al("c2x%d" % i, [128, D], F32) for i in range(2)]
    junk = k.al("c2junk", [128, D], BF16)
    ss = k.al("c2ss", [128, 12], F32)
    hb = k.al("c2hb", [128, D], BF16)
    qTs = k.al("c2qTs", [128, 16, 512], BF16)
    sc = k.al("c2sc", [128, 16, 128], F32)
    wk = k.al("c2wk", [128, 256], F32)
    M1 = k.al("c2M1", [128, 8, 16], F32)
    M2 = k.al("c2M2", [128, 8, 16], F32)
    C16 = k.al("c2C16", [128, 8, 16], F32)
    cand = k.al("c2cand", [128, 16, 16], F32)
    e16 = k.al("c2e16", [128, 8, 16], F32)
    st = k.al("c2stt", [128, 6, 8], F32)
    gp = [k.al("c2gp%d" % i, [128, 8, 260], F32) for i in range(2)]
    for i in range(2):
        k.ins('dve', 'memset', ap=gp[i][:], constant=0.0)
    ngrp = (nt + 3) // 4
    rot = 0
    for tg in range(ngrp):
        ntile = min(4, nt - tg * 4)
        ncol = ntile * 128
        for tt in range(ntile):
            t_ = tg * 4 + tt
            rows = slice(t_ * 128, (t_ + 1) * 128)
            xt = xb[t_ % 2]
            k.ld('sp', xt[:], d['x1s'][rows, :])
            rms_tile(k, xt[:], grep, hb[:], junk, ss, 0)
            transpose8(k, xnT[:, :, t_ * 128:(t_ + 1) * 128], hb, idb, t_ % 2)
        g0 = tg * 512
        for j in range(16):
            pc = k.bank(2 + rot % 4, [128, 512])
            rot += 1
            for kc in range(8):
                k.mm(pc[:, 0:ncol], wq[:, kc, j * 128:(j + 1) * 128], xnT[:, kc, g0:g0 + ncol], start=(kc == 0), stop=(kc == 7), inc=(kc == 7))
            k.ins('act', 'copy', out=qTs[:, j, 0:ncol], in_=pc[:, 0:ncol])
        for tt in range(ntile):
            t_ = tg * 4 + tt
            rows = slice(t_ * 128, (t_ + 1) * 128)
            for jb in range(4):
                pc = k.bank(2 + rot % 4, [128, 4, 128])
                rot += 1
                for q in range(4):
                    j = jb * 4 + q
                    k.mm(pc[:, q, :], qTs[:, j, tt * 128:(tt + 1) * 128], keys[:, j, :])
                k.ins('act', 'copy', out=sc[:, jb * 4:(jb + 1) * 4, :], in_=pc)
            g = gp[t_ % 2]
            for h in range(8):
                for half, MM in ((0, M1), (1, M2)):
                    s_ = sc[:, 2 * h + half, :]
                    k.ins('dve', 'max', out=MM[:, h, 0:8], in_=s_)
                    k.ins('dve', 'match_replace', out=wk[:, 0:128], in_to_replace=MM[:, h, 0:8], in_values=s_, imm_value=-1e30)
                    k.ins('dve', 'max', out=MM[:, h, 8:16], in_=wk[:, 0:128])
                k.ins('dve', 'tensor_tensor', out=cand[:], in0=M1[:, h, :].unsq(2).bc([128, 16, 16]), in1=M2[:, h, :].unsq(1).bc([128, 16, 16]), op=ALU.add)
                cf = cand[:].re("p a b -> p (a b)")
                k.ins('dve', 'max', out=C16[:, h, 0:8], in_=cf)
                k.ins('dve', 'match_replace', out=wk[:, :], in_to_replace=C16[:, h, 0:8], in_values=cf, imm_value=-1e30)
                k.ins('dve', 'max', out=C16[:, h, 8:16], in_=wk[:, :])
            k.ins('dve', 'tensor_tensor', out=e16[:], in0=C16[:], in1=C16[:, :, 0:1].bc([128, 8, 16]), op=ALU.subtract)
            k.ins('act', 'activation', out=e16[:], in_=e16[:], func=AF.Exp)
            k.ins('dve', 'tensor_reduce', out=st[:, 0, :], in_=e16[:], axis=AX.X, op=ALU.add)
            k.ins('act', 'activation', out=st[:, 1, :], in_=st[:, 0, :], func=AF.Ln)
            k.ins('dve', 'tensor_tensor', out=st[:, 2, :], in0=M1[:, :, 0], in1=st[:, 1, :], op=ALU.add)
            k.ins('dve', 'tensor_scalar', out=st[:, 2, :], in0=st[:, 2, :], scalar1=-1.0, scalar2=None, op0=ALU.mult)
            k.ins('dve', 'tensor_scalar', out=st[:, 3, :], in0=M2[:, :, 0], scalar1=-1.0, scalar2=None, op0=ALU.mult)
            for h in range(8):
                k.ins('act', 'activation', out=M1[:, h, :], in_=M1[:, h, :], func=AF.Exp, bias=st[:, 2, h:h + 1])
                k.ins('act', 'activation', out=M2[:, h, :], in_=M2[:, h, :], func=AF.Exp, bias=st[:, 3, h:h + 1])
            for h in range(8):
                k.ins('dve', 'tensor_tensor', out=cand[:], in0=M1[:, h, :].unsq(2).bc([128, 16, 16]), in1=M2[:, h, :].unsq(1).bc([128, 16, 16]), op=ALU.mult)
                cf = cand[:].re("p a b -> p (a b)")
                k.ins('dve', 'max', out=C16[:, h, 0:8], in_=cf)
                k.ins('dve', 'match_replace', out=wk[:, :], in_to_replace=C16[:, h, 0:8], in_values=cf, imm_value=-1e30)
                k.ins('dve', 'max', out=C16[:, h, 8:16], in_=wk[:, :])
            k.ins('dve', 'tensor_copy', out=g[:, :, 256], in_=C16[:, :, 15])
            for h in range(8):
                k.ins('act', 'activation', out=g[:, h, 0:128], in_=sc[:, 2 * h, :], func=AF.Exp, bias=st[:, 2, h:h + 1])
                k.ins('act', 'activation', out=g[:, h, 128:256], in_=sc[:, 2 * h + 1, :], func=AF.Exp, bias=st[:, 3, h:h + 1])
            k.ld('sp', d['gps'][rows, :], g[:].re("p h c -> p (h c)"))
    k.release(m0)


def phase_c2c(k, d, xnT, idb, nt=NT, nblk=32):
    m0 = k.mark()
    gp = [k.al("c3gp%d" % i, [128, 8, 260], F32) for i in range(2)]
    utb = [k.al("c3utb%d" % i, [128, 8, 512], BF16) for i in range(2)]
    vb = [k.al("c3vb%d" % i, [128, 4, D], BF16) for i in range(2)]
    A = [k.al("c3A%d" % i, [128, 512], F32) for i in range(2)]
    P = [k.al("c3P%d" % i, [128, 8, 4, 128], F32) for i in range(2)]
    Mb = [k.al("c3Mb%d" % i, [128, 8, 512], BF16) for i in range(2)]
    AW = [k.al("c3AW%d" % i, [128, 512], BF16) for i in range(2)]
    awt = [k.al("c3awt%d" % i, [128, 4, 128], BF16) for i in range(2)]
    x1 = [k.al("c3x1%d" % i, [128, D], F32) for i in range(2)]
    UTv = d['UT'][:, :, :].re("kc p e -> p kc e")
    cnt = 0
    for pr in range((nt + 1) // 2):
        tiles = [t for t in (2 * pr, 2 * pr + 1) if t < nt]
        for ti, t_ in enumerate(tiles):
            k.ld('sp', gp[ti][:].re("p h c -> p (h c)"), d['gps'][t_ * 128:(t_ + 1) * 128, :])
        for eb in range(nblk):
            e0 = eb * 512
            u_ = utb[eb % 2]
            v_ = vb[eb % 2]
            k.ld('sp', u_[:], UTv[:, :, e0:e0 + 512])
            k.ld('act', v_[:], d['Vb'][e0:e0 + 512, :].re("(a p) d -> p a d", p=128))
            for ti, t_ in enumerate(tiles):
                c2 = cnt % 2
                cnt += 1
                g = gp[ti]
                pA = k.bank(4 + c2, [128, 512])
                for kc in range(8):
                    k.mm(pA, xnT[:, kc, t_ * 128:(t_ + 1) * 128], u_[:, kc, :], start=(kc == 0), stop=(kc == 7), inc=(kc == 7))
                k.ins('act', 'activation', out=A[c2][:], in_=pA, func=AF.Gelu)
                Pt = P[c2]
                k.ins('dve', 'tensor_tensor', out=Pt[:], in0=g[:, :, 4 * eb:4 * eb + 4].unsq(3).bc([128, 8, 4, 128]),
                      in1=g[:, :, 128:256].unsq(2).bc([128, 8, 4, 128]), op=ALU.mult)
                Mt = Mb[c2]
                for h in range(8):
                    k.ins('dve', 'scalar_tensor_tensor', out=Mt[:, h, :], in0=Pt[:, h, :, :].re("p a i -> p (a i)"), scalar=g[:, h, 256:257],
                          in1=Pt[:, h, :, :].re("p a i -> p (a i)"), op0=ALU.is_ge, op1=ALU.mult)
                k.ins('dve', 'tensor_tensor', out=Mt[:, 0:4, :], in0=Mt[:, 0:4, :], in1=Mt[:, 4:8, :], op=ALU.add)
                k.ins('dve', 'tensor_tensor', out=Mt[:, 0:2, :], in0=Mt[:, 0:2, :], in1=Mt[:, 2:4, :], op=ALU.add)
                k.ins('dve', 'tensor_tensor', out=Mt[:, 0, :], in0=Mt[:, 0, :], in1=Mt[:, 1, :], op=ALU.add)
                k.ins('dve', 'tensor_tensor', out=AW[c2][:], in0=A[c2][:], in1=Mt[:, 0, :], op=ALU.mult)
                pT = k.bank(6 + c2, [128, 4, 128], BF16)
                for a in range(4):
                    k.tr(pT[:, a, :], AW[c2][:, a * 128:(a + 1) * 128], idb[:])
                k.ins('act', 'copy', out=awt[c2][:], in_=pT)
                for a in range(4):
                    for half in range(2):
                        last = (eb == nblk - 1 and a == 3)
                        k.mm(k.bank(2 * ti + half, [128, 512]), awt[c2][:, a, :], v_[:, a, half * 512:(half + 1) * 512],
                             start=(eb == 0 and a == 0), stop=last, inc=(last or (a == 3 and half == 1)))
        for ti, t_ in enumerate(tiles):
            rows = slice(t_ * 128, (t_ + 1) * 128)
            xo = x1[ti]
            k.ld('sp', xo[:], d['x1s'][rows, :])
            for half in range(2):
                hs = slice(half * 512, (half + 1) * 512)
                k.ins('dve', 'tensor_tensor', out=xo[:, hs], in0=xo[:, hs], in1=k.bank(2 * ti + half, [128, 512]), op=ALU.add)
            k.ld('sp', d['x2s'][rows, :], xo[:])
    k.release(m0)


def phase_c3(k, d, idb, nt=NT, final=False, outname='xout'):
    m0 = k.mark()
    stage = [k.al("c4st%d" % i, [128, 1024], F32) for i in range(2)]
    grep = k.al("c4g", [128, D], F32)
    k.ld('sp', grep[:], d['ng_ple'][0:1, :].pbc(128))
    fg = k.al("c4fg", [128, D], F32)
    if final:
        k.ld('sp', fg[:], d['final_g'][0:1, :].pbc(128))
    wgt = k.al("c4wg", [128, 8, D], BF16)
    load_w_bf16(k, wgt, d['w_gate'], D, stage)
    wpj = k.al("c4wp", [128, 2, D], BF16)
    load_w_bf16(k, wpj, d['w_proj'], D, stage)
    xb = [k.al("c4x%d" % i, [128, D], F32) for i in range(2)]
    junk = k.al("c4junk", [128, D], BF16)
    junkf = k.al("c4junkf", [128, D], F32)
    ss = k.al("c4ss", [128, 12], F32)
    hb = k.al("c4hb", [128, D], BF16)
    hT = k.al("c4hT", [128, 8, 128], BF16)
    sgp = k.al("c4sg", [128, D], F32)
    pst = [k.al("c4pst%d" % i, [128, 2, 128], F32) for i in range(2)]
    pbf = k.al("c4pbf", [128, 2, 128], BF16)
    xo = [k.al("c4xo%d" % i, [128, D], F32) for i in range(2)]
    rot = 0
    for tt in range(nt):
        rows = slice(tt * 128, (tt + 1) * 128)
        xt = xb[tt % 2]
        k.ld('sp', xt[:], d['x2s'][rows, :])
        rms_tile(k, xt[:], grep, hb[:], junk, ss, 0)
        transpose8(k, hT[:], hb, idb, tt % 2)
        k.ld('sp', pst[tt % 2][:], d['pT'][:, rows].re("(kc p) t -> p kc t", p=128))
        k.ins('act', 'copy', out=pbf[:], in_=pst[tt % 2][:])
        for half in range(2):
            hs = slice(half * 512, (half + 1) * 512)
            pc = k.bank(2 + rot % 4, [128, 512])
            rot += 1
            for kc in range(8):
                k.mm(pc, hT[:, kc, :], wgt[:, kc, hs], start=(kc == 0), stop=(kc == 7), inc=(kc == 7))
            k.ins('act', 'activation', out=sgp[:, hs], in_=pc, func=AF.Sigmoid)
            pc2 = k.bank(2 + rot % 4, [128, 512])
            rot += 1
            for kc in range(2):
                k.mm(pc2, pbf[:, kc, :], wpj[:, kc, hs], start=(kc == 0), stop=(kc == 1), inc=(kc == 1))
            k.ins('dve', 'tensor_tensor', out=sgp[:, hs], in0=sgp[:, hs], in1=pc2, op=ALU.mult)
        o = xo[tt % 2]
        k.ins('dve', 'tensor_tensor', out=o[:], in0=xt[:], in1=sgp[:], op=ALU.add)
        if final:
            k.ins('act', 'activation', out=junkf[:], in_=o[:], func=AF.Square, accum_out=ss[:, 3:4])
            k.ins('dve', 'tensor_scalar', out=ss[:, 4:5], in0=ss[:, 3:4], scalar1=1.0 / D, scalar2=EPS, op0=ALU.mult, op1=ALU.add)
            k.ins('dve', 'reciprocal', out=ss[:, 4:5], in_=ss[:, 4:5])
            k.ins('act', 'activation', out=ss[:, 5:6], in_=ss[:, 4:5], func=AF.Sqrt)
            k.ins('dve', 'scalar_tensor_tensor', out=o[:], in0=o[:], scalar=ss[:, 5:6], in1=fg[:], op0=ALU.mult, op1=ALU.mult)
        k.ld('sp', d[outname][rows, :], o[:])
    k.release(m0)


CIN = [("xtok", [TT, D]), ("ng_mix", [1, D]), ("wg", [D, 3072]), ("ymlT", [D, TT]), ("ysdT", [D, TT]), ("yfoxT", [D, TT]),
       ("ysd_tm", [TT, D]), ("sng", [128, 8]), ("w_ml", [D, D]), ("w_ssm", [D, D]), ("w_fox", [D, D]), ("w_out", [D, D]),
       ("ng_ffn", [1, D]), ("w_q", [D, 2048]), ("keysT", [16, 128, 128]), ("peer_u", [NE, D]), ("peer_v", [NE, D]),
       ("ng_ple", [1, D]), ("w_gate", [D, D]), ("w_proj", [256, D]), ("pT", [256, TT]), ("final_g", [1, D])]


def build_c(dbg=True, phases=('1', 'a', 'b', 'c', '3'), nt=NT, nblk=32, final=False):
    nc = bass.Bass("TRN2", target_bir_lowering=False)
    k = K(nc)
    kS = "ExternalOutput" if dbg else "Internal"
    d = {}
    for nm, shp in CIN:
        d[nm] = k.dram(nm, shp, F32, "ExternalInput")
    d['x1s'] = k.dram("x1s", [TT, D], F32, kS)
    d['x2s'] = k.dram("x2s", [TT, D], F32, kS)
    d['gps'] = k.dram("gps", [TT, 8 * 260], F32, kS)
    d['UT'] = k.dram("UT", [8, 128, NE], BF16, "Internal")
    d['Vb'] = k.dram("Vb", [NE, D], BF16, "Internal")
    d['xout'] = k.dram("xout", [TT, D], F32, "ExternalOutput")
    k.init_arena(200 * 1024)
    idf, idb = make_ident(k)
    outs = [d['xout']]
    if '1' in phases:
        phase_c1(k, d, idf, idb, nt)
        outs.append(d['x1s'])
    if 'a' in phases:
        phase_c2a(k, d, idb, nblk)
    xnT = k.al("xnT", [128, 8, TT], BF16)
    if 'b' in phases:
        phase_c2b(k, d, xnT, idb, nt)
        outs.append(d['gps'])
    if 'c' in phases:
        phase_c2c(k, d, xnT, idb, nt, nblk)
        outs.append(d['x2s'])
    if '3' in phases:
        phase_c3(k, d, idb, nt, final)
    k.final_wait('sp', outs)
    k.emit()
    print("instructions:", k.ninst, "sems:", len(k.sems))
    return nc


def prep_c(inp, layer, xfull, yml, ysd, yfox):
    w = inp['w_in'][layer]
    O = OFF
    wg = np.ascontiguousarray(w[:, O['g_ml']:O['g_ml'] + 3072])
    sng = np.ascontiguousarray(inp['ssm_norm_g'][layer].reshape(8, 128).T)
    keysT = np.empty((16, 128, 128), np.float32)
    for h in range(8):
        keysT[2 * h] = inp['peer_keys1'][layer][h].T
        keysT[2 * h + 1] = inp['peer_keys2'][layer][h].T
    maps = []
    xf = xfull.reshape(16384, D)
    ymlf, ysdf, yfoxf = yml.reshape(16384, D), ysd.reshape(16384, D), yfox.reshape(16384, D)
    pf = inp['p'][layer].reshape(16384, 256)
    for c in range(8):
        rows = slice(c * TT, (c + 1) * TT)
        m = {
            'xtok': np.ascontiguousarray(xf[rows]), 'ng_mix': inp['norm_mix_g'][layer].reshape(1, D), 'wg': wg,
            'ymlT': np.ascontiguousarray(ymlf[rows].T), 'ysdT': np.ascontiguousarray(ysdf[rows].T), 'yfoxT': np.ascontiguousarray(yfoxf[rows].T),
            'ysd_tm': np.ascontiguousarray(ysdf[rows]), 'sng': sng,
            'w_ml': inp['w_branch_ml'][layer], 'w_ssm': inp['w_branch_ssm'][layer], 'w_fox': inp['w_branch_fox'][layer], 'w_out': inp['w_out'][layer],
            'ng_ffn': inp['norm_ffn_g'][layer].reshape(1, D), 'w_q': inp['peer_w_q'][layer], 'keysT': keysT,
            'peer_u': inp['peer_u'][layer], 'peer_v': inp['peer_v'][layer],
            'ng_ple': inp['norm_ple_g'][layer].reshape(1, D), 'w_gate': inp['ple_w_gate'][layer], 'w_proj': inp['ple_w_proj'][layer],
            'pT': np.ascontiguousarray(pf[rows].T), 'final_g': inp['final_norm_g'].reshape(1, D),
        }
        maps.append({kk: np.ascontiguousarray(v, dtype=np.float32) for kk, v in m.items()})
    return maps


AB_IN = [("wcm", [D, NCM]), ("wtm", [D, NTM]), ("cw", [128, 6, 4]), ("cb", [128, 6]), ("ng", [1, D]), ("fqg", [1, 128]),
         ("fkg", [1, 128]), ("fbf", [1, 2]), ("sprm", [1, 12]), ("mprm", [1, 4]), ("mlg", [1, 256])]
C_SKIP = ("xtok", "ymlT", "ysdT", "yfoxT", "ysd_tm", "final_g")


def build_fused(depth=2):
    nc = bass.Bass("TRN2", target_bir_lowering=False)
    k = K(nc)
    g = {
        'x': k.dram("x", [T, D], F32, "ExternalInput"),
        'xtok': k.dram("xtok", [TT, D], F32, "ExternalInput"),
        'bmask': k.dram("bmask", [1, 4], F32, "ExternalInput"),
        'final_g': k.dram("final_g", [1, D], F32, "ExternalInput"),
    }
    L = []
    for l in range(depth):
        dl = {}
        for nm, shp in AB_IN:
            dl[nm] = k.dram("%s_%d" % (nm, l), shp, F32, "ExternalInput")
        for nm, shp in CIN:
            if nm not in C_SKIP:
                dl[nm] = k.dram("%s_%d" % (nm, l), shp, F32, "ExternalInput")
        L.append(dl)
    zcm = k.dram("zcm", [768, T], BF16, "Internal")
    zg = k.dram("zg", [128, T], F32, "Internal")
    ztm = k.dram("ztm", [T, NTM], F32, "Internal")
    ysrc = k.dram("ysrc", [T, 768], F32, "Internal")
    ydst = k.dram("ydst", [32, 4, 256, 768], F32, "Internal")
    xcur = k.dram("xcur", [TT, D], F32, "Internal")
    xg = k.dram("xg", [8, 4, 256, D], F32, "Internal")
    sc = {
        'x1s': k.dram("x1s", [TT, D], F32, "Internal"), 'x2s': k.dram("x2s", [TT, D], F32, "Internal"),
        'gps': k.dram("gps", [TT, 8 * 260], F32, "Internal"), 'UT': k.dram("UT", [8, 128, NE], BF16, "Internal"),
        'Vb': k.dram("Vb", [NE, D], BF16, "Internal"), 'xout': k.dram("xout", [TT, D], F32, "ExternalOutput"),
        'xcur': xcur, 'ydst': ydst, 'bmask': g['bmask'], 'final_g': g['final_g'],
    }
    k.init_arena(206 * 1024)
    idf, idb = make_ident(k)

    def ystore(c, col0, ncol, ref):
        k.ld('sp', ysrc[c * 128:(c + 1) * 128, col0:col0 + ncol], ref)

    for l in range(depth):
        dl = L[l]
        xsrc = g['x'] if l == 0 else (lambda r0: xg[(r0 % 2048) // 256, r0 // 2048, (r0 % 256):(r0 % 256) + 128, :])
        mp = k.mark()
        phase_p(k, xsrc, dl['ng'], dl['wcm'], dl['wtm'], dl['cw'], dl['cb'], zcm, zg, ztm, idb, NG)
        k.release(mp)
        phase_m(k, zcm, zg, ztm, dl['mprm'], dl['mlg'], None, idf, idb, nch=64, ystore=ystore)
        phase_s(k, zcm, zg, ztm, dl['sprm'], None, idf, idb, nch=64, ystore=ystore)
        phase_f(k, ztm, zg, dl['fqg'], dl['fkg'], dl['fbf'], None, idf, idb, nq=16, ystore=ystore)
        for cch in range(32):
            k.coll("AllGather", ydst[cch, :, :, :], ysrc[cch * 256:(cch + 1) * 256, :], [[0, 1, 2, 3], [4, 5, 6, 7]])
        d = dict(sc)
        d.update(dl)
        d['xtok'] = g['xtok'] if l == 0 else xcur
        final = (l == depth - 1)
        phase_c1(k, d, idf, idb, NT, fused=True)
        phase_c2a(k, d, idb, 32)
        mx = k.mark()
        xnT = k.al("xnT", [128, 8, TT], BF16)
        phase_c2b(k, d, xnT, idb, NT)
        phase_c2c(k, d, xnT, idb, NT, 32)
        k.release(mx)
        phase_c3(k, d, idb, NT, final, outname=('xout' if final else 'xcur'))
        if not final:
            for cch in range(8):
                k.coll("AllGather", xg[cch, :, :, :], xcur[cch * 256:(cch + 1) * 256, :], [[0, 1, 2, 3], [4, 5, 6, 7]])
    k.final_wait('sp', [sc['xout']])
    k.emit()
    return nc


def kernel(**inputs):
    inp = {k_: np.asarray(v) for k_, v in inputs.items()}
    x = np.ascontiguousarray(inp['x'], dtype=np.float32)
    depth = inp['w_in'].shape[0]
    zeros = np.zeros((2, T, 1024), np.float32)
    maps = [dict() for _ in range(8)]
    xf = x.reshape(16384, D)
    for l in range(depth):
        mab = prep_ab(inp, l, x)
        mc = prep_c(inp, l, x, zeros, zeros, zeros)
        for c in range(8):
            for nm, _ in AB_IN:
                maps[c]["%s_%d" % (nm, l)] = mab[c][nm]
            for nm, _ in CIN:
                if nm not in C_SKIP:
                    maps[c]["%s_%d" % (nm, l)] = mc[c][nm]
    for c in range(8):
        b = c // 4
        maps[c]['x'] = np.ascontiguousarray(x[b])
        maps[c]['xtok'] = np.ascontiguousarray(xf[c * TT:(c + 1) * TT])
        bmk = np.zeros((1, 4), np.float32)
        bmk[0, c % 4] = 1.0
        maps[c]['bmask'] = bmk
        maps[c]['final_g'] = np.ascontiguousarray(inp['final_norm_g'].reshape(1, D), dtype=np.float32)
    nc = build_fused(depth)
    res = run_bass_kernel_spmd(nc, maps, core_ids=list(range(8)))
    out = np.empty((16384, D), np.float32)
    for c in range(8):
        out[c * TT:(c + 1) * TT] = np.asarray(res.results[c]['xout'])
    return np.ascontiguousarray(out.reshape(2, T, D), dtype=np.float32)
```

```python
import os
import contextlib
import numpy as np
import concourse.bass as bass
import concourse.mybir as mybir
from concourse.bass_utils import run_bass_kernel_spmd

F32 = mybir.dt.float32
BF16 = mybir.dt.bfloat16
U32 = mybir.dt.uint32
AF = mybir.ActivationFunctionType
ALU = mybir.AluOpType
AX = mybir.AxisListType


class Buf:
    def __init__(self, K, name, ap_fn):
        self.K = K
        self.name = name
        self.ap_fn = ap_fn
        self.writes = {}
        self.reads = {}
        self.dsem = None
        self.dcnt = 0
        self.is_dram = False
        self.multi = False

    def __getitem__(self, idx):
        return Ref(self, self.ap_fn[idx])


class Ref:
    def __init__(self, buf, ap):
        self.buf = buf
        self.ap = ap

    @property
    def shape(self):
        return self.ap.shape

    def __getitem__(self, idx):
        return Ref(self.buf, self.ap[idx])

    def bc(self, shape):
        return Ref(self.buf, self.ap.to_broadcast(list(shape)))

    def unsq(self, axis):
        return Ref(self.buf, self.ap.unsqueeze(axis))

    def re(self, pattern_, **kw):
        return Ref(self.buf, self.ap.rearrange(pattern_, **kw))

    def pbc(self, n):
        return Ref(self.buf, self.ap.partition_broadcast(n))

    def bitcast(self, dt):
        return Ref(self.buf, self.ap.bitcast(dt))


class K:
    ENG = ('pe', 'dve', 'act', 'pool', 'sp')

    def __init__(self, nc, same_engine_sync=True):
        self.nc = nc
        self.es = contextlib.ExitStack()
        self.engobj = {'pe': nc.tensor, 'dve': nc.vector, 'act': nc.scalar,
                       'pool': nc.gpsimd, 'sp': nc.sync}
        self.sems = {}
        self.cnt = {}
        for e in self.ENG:
            self.sems[e] = self.es.enter_context(nc.semaphore("s_" + e))
            self.cnt[e] = 0
        self.waited = {e: {} for e in self.ENG}
        self.prog = {e: [] for e in self.ENG}
        self.same = same_engine_sync
        self.nbuf = 0
        self.ninst = 0
        self.dmax = {}
        self.coll_inc = False
        self.live = []
        self.free_sems = []

    def sb(self, name, shape, dt):
        t = self.es.enter_context(self.nc.sbuf_tensor(name, list(shape), dt))
        return Buf(self, name, t)

    def ps(self, name, shape, dt):
        t = self.es.enter_context(self.nc.psum_tensor(name, list(shape), dt))
        return Buf(self, name, t)

    def dram(self, name, shape, dt, kind):
        t = self.nc.dram_tensor(name, list(shape), dt, kind=kind)
        b = Buf(self, name, t.ap())
        b.is_dram = True
        b.multi = True
        return b

    def init_arena(self, nbytes=188 * 1024):
        self.arena_t = self.es.enter_context(self.nc.sbuf_tensor("arena", [128, nbytes // 4], F32))
        self.arena_n = nbytes // 4
        self.arena_off = 0
        self.banks = [self.ps("bank%d" % i, [128, 512], F32) for i in range(8)]

    def al(self, name, shape, dt):
        esz = 2 if dt == BF16 else 4
        n = 1
        for s_ in shape[1:]:
            n *= s_
        words = (n * esz + 3) // 4
        words = (words + 15) // 16 * 16
        assert self.arena_off + words <= self.arena_n, ("arena overflow", name, self.arena_off, words)
        ap = self.arena_t[0:shape[0], self.arena_off:self.arena_off + words]
        off0 = self.arena_off
        self.arena_off += words
        if dt != F32:
            ap = ap.bitcast(dt)
        ap = ap[:, 0:n]
        if len(shape) > 2:
            names = " ".join("d%d" % i for i in range(1, len(shape)))
            kw = {"d%d" % i: shape[i] for i in range(1, len(shape))}
            ap = ap.rearrange("p (%s) -> p %s" % (names, names), **kw)
        b = Buf(self, name, ap)
        self.live.append((off0, b))
        return b

    def bank(self, i, shape, dt=None):
        b = self.banks[i]
        ap = b.ap_fn[:, :]
        if dt is not None and dt != F32:
            ap = ap.bitcast(dt)
        n = 1
        for s_ in shape[1:]:
            n *= s_
        ap = ap[0:shape[0], 0:n]
        if len(shape) > 2:
            names = " ".join("d%d" % i for i in range(1, len(shape)))
            kw = {"d%d" % i: shape[i] for i in range(1, len(shape))}
            ap = ap.rearrange("p (%s) -> p %s" % (names, names), **kw)
        return Ref(b, ap)

    def mark(self):
        return self.arena_off

    def release(self, m):
        self.barrier()
        keep = []
        for off, b in self.live:
            if off >= m:
                if b.dsem is not None:
                    self.free_sems.append(b.dsem)
                    b.dsem = None
            else:
                keep.append((off, b))
        self.live = keep
        self.arena_off = m

    def barrier(self):
        snap = dict(self.cnt)
        for key, sem in self.sems.items():
            if key not in snap:
                snap[key] = None
        dvals = {}
        for key in self.sems:
            if key not in self.cnt:
                dvals[key] = self.dmax.get(key, 0)
        for e in self.ENG:
            waits = []
            for key, v in list(snap.items()):
                v = self.cnt[key] if key in self.cnt else dvals[key]
                if key == e or v == 0:
                    continue
                if self.waited[e].get(key, 0) >= v:
                    continue
                self.waited[e][key] = v
                waits.append((key, v))
            self._emit_waits(e, waits)

    def _need(self, eng, reads, writes, skip_waw=None):
        need = {}
        for b in reads:
            for k, v in b.writes.items():
                if need.get(k, 0) < v:
                    need[k] = v
        for b in writes:
            if not b.multi:
                for k, v in b.writes.items():
                    if k == skip_waw:
                        continue
                    if need.get(k, 0) < v:
                        need[k] = v
            for k, v in b.reads.items():
                if need.get(k, 0) < v:
                    need[k] = v
        out = []
        for k, v in need.items():
            if k == eng and (eng == 'pe' or not self.same):
                continue
            if self.waited[eng].get(k, 0) >= v:
                continue
            self.waited[eng][k] = v
            out.append((k, v))
        return out

    def _emit_waits(self, eng, waits):
        for k, v in waits:
            sem = self.sems[k]
            self.prog[eng].append(lambda e, sem=sem, v=v: e.wait_ge(sem, v))

    def _mark(self, key, val, reads, writes):
        for b in writes:
            if b.multi:
                if b.writes.get(key, 0) < val:
                    b.writes[key] = val
            else:
                b.writes = {key: val}
                b.reads = {}
        for b in reads:
            if b.reads.get(key, 0) < val:
                b.reads[key] = val

    def op(self, eng, fn, reads=(), writes=(), inc=True):
        waits = self._need(eng, reads, writes)
        self._emit_waits(eng, waits)
        val = self.cnt[eng] + 1
        if inc:
            self.cnt[eng] = val
            sem = self.sems[eng]
            self.prog[eng].append(lambda e, fn=fn, sem=sem: fn(e).then_inc(sem, 1))
        else:
            self.prog[eng].append(lambda e, fn=fn: fn(e))
        self._mark(eng, val, reads, writes)
        self.ninst += 1

    WRITE_KW = ('out', 'accum_out', 'out_max', 'out_indices', 'ap')

    def ins(self, eng, method, inc=True, **kw):
        reads, writes, args = [], [], {}
        for n, v in kw.items():
            if isinstance(v, Ref):
                (writes if n in self.WRITE_KW else reads).append(v.buf)
                args[n] = v.ap
            else:
                args[n] = v
        self.op(eng, lambda e, m=method, a=args: getattr(e, m)(**a), reads, writes, inc=inc)

    def mm(self, out, lhsT, rhs, start=True, stop=True, inc=True):
        self.op('pe', lambda e, o=out.ap, l=lhsT.ap, r=rhs.ap: e.matmul(o, lhsT=l, rhs=r, start=start, stop=stop),
                [lhsT.buf, rhs.buf], [out.buf], inc=inc)

    def tr(self, out, in_, ident):
        self.op('pe', lambda e, o=out.ap, i=in_.ap, d=ident.ap: e.transpose(out=o, in_=i, identity=d),
                [in_.buf, ident.buf], [out.buf])

    def ld(self, q, out, in_, **kw):
        self.dma(q, out.buf, out.ap, in_.buf, in_.ap, **kw)

    def dma(self, q, out_buf, out_ap, in_buf, in_ap, **kw):
        own = in_buf if out_buf.is_dram else out_buf
        if own.dsem is None:
            if self.free_sems:
                key = self.free_sems.pop()
            else:
                key = "d%d" % self.nbuf
                self.nbuf += 1
                self.sems[key] = self.es.enter_context(self.nc.semaphore(key))
            own.dsem = key
            own.dcnt = self.dmax.get(key, 0)
        key = own.dsem
        waits = self._need(q, [in_buf], [out_buf], skip_waw=key)
        self._emit_waits(q, waits)
        own.dcnt += 16
        val = own.dcnt
        self.dmax[key] = val
        sem = self.sems[key]
        self.prog[q].append(lambda e, o=out_ap, i=in_ap, sem=sem, kw=kw: e.dma_start(out=o, in_=i, **kw).then_inc(sem, 16))
        self._mark(key, val, [in_buf], [out_buf])
        self.ninst += 1

    def coll(self, kind, out, in_, groups, op=None):
        key = "coll"
        if key not in self.sems:
            self.sems[key] = self.es.enter_context(self.nc.semaphore(key))
            self.dmax[key] = 0
        waits = self._need('pool', [in_.buf], [out.buf])
        self._emit_waits('pool', waits)
        self.dmax[key] += 1
        val = self.dmax[key]
        sem = self.sems[key]
        aop = ALU.bypass if op is None else op
        self.prog['pool'].append(lambda e, o=out.ap.opt(), i=in_.ap.opt(), sem=sem: e.collective_compute(
            kind, aop, replica_groups=groups, ins=[i], outs=[o]).then_inc(sem, 1))
        self._mark(key, val, [in_.buf], [out.buf])
        self.ninst += 1

    def final_wait(self, eng, bufs):
        waits = self._need(eng, bufs, bufs)
        self._emit_waits(eng, waits)

    def emit(self):
        with self.nc.Block() as block:
            names = {'pe': 'tensor', 'dve': 'vector', 'act': 'scalar', 'pool': 'gpsimd', 'sp': 'sync'}
            for e in self.ENG:
                lst = self.prog[e]
                if not lst:
                    continue

                def run(engine, lst=lst):
                    for th in lst:
                        th(engine)
                getattr(block, names[e])(run)
        self.es.close()


T = 8192
D = 1024
NCM = 896
NTM = 1540
EPS = 1e-6
NG = T // 512

OFF = {}
_names = ['ml_q', 'ml_k', 'ml_v', 'ml_o', 'ml_i', 'ml_f', 's_z', 's_x', 's_b', 's_c', 's_dt',
          'f_q', 'f_k', 'f_v', 'f_f', 'g_ml', 'g_ssm', 'g_fox']
_sizes = [512, 512, 1024, 1024, 8, 8, 1024, 1024, 256, 256, 16, 1024, 1024, 1024, 8, 1024, 1024, 1024]
_o = 0
for _n, _s in zip(_names, _sizes):
    OFF[_n] = _o
    _o += _s
D_IN = _o


def prep_ab(inp, layer, xfull):
    w = inp['w_in'][layer]
    maps = []
    for c in range(8):
        b, g = c // 4, c % 4
        sg = g // 2
        wcm = np.zeros((D, NCM), np.float32)
        wcm[:, 0:128] = w[:, OFF['ml_q'] + 128 * g: OFF['ml_q'] + 128 * g + 128]
        wcm[:, 128:256] = w[:, OFF['ml_k'] + 128 * g: OFF['ml_k'] + 128 * g + 128]
        wcm[:, 256:512] = w[:, OFF['s_x'] + 256 * g: OFF['s_x'] + 256 * g + 256]
        wcm[:, 512:640] = w[:, OFF['s_b'] + 128 * sg: OFF['s_b'] + 128 * sg + 128]
        wcm[:, 640:768] = w[:, OFF['s_c'] + 128 * sg: OFF['s_c'] + 128 * sg + 128]
        wcm[:, 768:770] = w[:, OFF['ml_i'] + 2 * g: OFF['ml_i'] + 2 * g + 2]
        wcm[:, 800:802] = w[:, OFF['ml_f'] + 2 * g: OFF['ml_f'] + 2 * g + 2]
        wcm[:, 832:834] = w[:, OFF['f_f'] + 2 * g: OFF['f_f'] + 2 * g + 2]
        wcm[:, 864:868] = w[:, OFF['s_dt'] + 4 * g: OFF['s_dt'] + 4 * g + 4]
        wtm = np.concatenate([
            w[:, OFF['ml_v'] + 256 * g: OFF['ml_v'] + 256 * g + 256],
            w[:, OFF['ml_o'] + 256 * g: OFF['ml_o'] + 256 * g + 256],
            w[:, OFF['s_z'] + 256 * g: OFF['s_z'] + 256 * g + 256],
            w[:, OFF['f_q'] + 256 * g: OFF['f_q'] + 256 * g + 256],
            w[:, OFF['f_k'] + 256 * g: OFF['f_k'] + 256 * g + 256],
            w[:, OFF['f_v'] + 256 * g: OFF['f_v'] + 256 * g + 256],
            w[:, OFF['s_dt'] + 4 * g: OFF['s_dt'] + 4 * g + 4]], axis=1)
        mcw, mcb = inp['ml_conv_w'][layer], inp['ml_conv_b'][layer]
        scw, scb = inp['ssm_conv_w'][layer], inp['ssm_conv_b'][layer]
        chans_w = np.concatenate([
            mcw[:, 128 * g:128 * g + 128], mcw[:, 512 + 128 * g: 512 + 128 * g + 128],
            scw[:, 256 * g:256 * g + 256], scw[:, 1024 + 128 * sg:1024 + 128 * sg + 128],
            scw[:, 1280 + 128 * sg:1280 + 128 * sg + 128]], axis=1)
        chans_b = np.concatenate([
            mcb[128 * g:128 * g + 128], mcb[512 + 128 * g: 512 + 128 * g + 128],
            scb[256 * g:256 * g + 256], scb[1024 + 128 * sg:1024 + 128 * sg + 128],
            scb[1280 + 128 * sg:1280 + 128 * sg + 128]])
        cw = np.ascontiguousarray(chans_w.T.reshape(6, 128, 4).transpose(1, 0, 2))
        cb = np.ascontiguousarray(chans_b.reshape(6, 128).T)
        m = {
            'x': np.ascontiguousarray(xfull[b]),
            'ng': np.ascontiguousarray(inp['norm_mix_g'][layer].reshape(1, D)),
            'wcm': wcm, 'wtm': np.ascontiguousarray(wtm), 'cw': cw, 'cb': cb,
            'sprm': np.ascontiguousarray(np.concatenate([inp['ssm_dt_bias'][layer][4 * g:4 * g + 4], inp['ssm_a_log'][layer][4 * g:4 * g + 4], inp['ssm_d'][layer][4 * g:4 * g + 4]]).reshape(1, 12).astype(np.float32)),
            'mprm': np.ascontiguousarray(np.concatenate([inp['ml_b_i'][layer][2 * g:2 * g + 2], inp['ml_b_f'][layer][2 * g:2 * g + 2]]).reshape(1, 4).astype(np.float32)),
            'mlg': np.ascontiguousarray(inp['ml_norm_g'][layer][256 * g:256 * g + 256].reshape(1, 256)),
            'fqg': np.ascontiguousarray(inp['fox_q_norm_g'][layer].reshape(1, 128)),
            'fkg': np.ascontiguousarray(inp['fox_k_norm_g'][layer].reshape(1, 128)),
            'fbf': np.ascontiguousarray(inp['fox_b_f'][layer][2 * g:2 * g + 2].reshape(1, 2)),
        }
        maps.append(m)
    return maps


def make_ident(k, name="ident"):
    idf = k.al(name + "_f", [128, 128], F32)
    idb = k.al(name + "_b", [128, 128], BF16)
    k.ins('pool', 'memset', ap=idf[:], constant=0.0)
    k.op('pool', lambda e: e.affine_select(out=idf[:].ap, in_=idf[:].ap, pattern=[[-1, 128]],
                                           compare_op=ALU.not_equal, fill=1.0, base=0, channel_multiplier=1),
         [idf], [idf])
    k.ins('dve', 'tensor_copy', out=idb[:], in_=idf[:])
    return idf, idb


def phase_p(k, x, ng, wcm, wtm, cw, cb, zcm, zg, ztm, idb, ngroups=NG):
    ngrep = k.al("ngrep", [128, D], F32)
    k.ld('sp', ngrep[:], ng[0:1, :].pbc(128))
    wcm_sb = k.al("wcm_sb", [128, 8, NCM], BF16)
    wtm_sb = k.al("wtm_sb", [128, 8, NTM], BF16)
    wst = [k.al("wst%d" % i, [128, NTM], F32) for i in range(2)]
    for kc in range(8):
        st = wst[kc % 2]
        k.ld('sp', st[:, 0:NCM], wcm[kc * 128:(kc + 1) * 128, :])
        k.ins('act', 'copy', out=wcm_sb[:, kc, :], in_=st[:, 0:NCM])
    for kc in range(8):
        st = wst[kc % 2]
        k.ld('sp', st[:, :], wtm[kc * 128:(kc + 1) * 128, :])
        k.ins('dve', 'tensor_copy', out=wtm_sb[:, kc, :], in_=st[:, :])
    cw_sb = k.al("cw_sb", [128, 6, 4], F32)
    cb_sb = k.al("cb_sb", [128, 6], F32)
    k.ld('sp', cw_sb[:], cw[:, :, :])
    k.ld('sp', cb_sb[:], cb[:, :])
    xb = [k.al("xb%d" % i, [128, D], F32) for i in range(4)]
    junk = k.al("junk", [128, D], BF16)
    ss = k.al("ss", [128, 8], F32)
    hn = [k.al("hn%d" % i, [128, D], BF16) for i in range(2)]
    hT = [k.al("hT%d" % i, [128, 8, 512], BF16) for i in range(2)]
    zc = [k.al("zc%d" % i, [128, 515], F32) for i in range(2)]
    halo = k.al("halo", [128, 6, 3], F32)
    k.ins('pool', 'memset', ap=halo[:], constant=0.0)
    acc = [k.al("acc%d" % i, [128, 512], F32) for i in range(2)]
    ob = [k.al("ob%d" % i, [128, 512], BF16) for i in range(2)]
    gsb = [k.al("gsb%d" % i, [128, 512], F32) for i in range(2)]
    tmsb = [k.al("tmsb%d" % i, [128, NTM], F32) for i in range(2)]
    rot = 0
    nt = 0
    for tg in range(ngroups):
        h_T = hT[tg % 2]
        for tt in range(4):
            r0 = tg * 512 + tt * 128
            xt = xb[tt]
            k.ld('sp', xt[:], (x(r0) if callable(x) else x[r0:r0 + 128, :]))
            k.ins('act', 'activation', out=junk[:], in_=xt[:], func=AF.Square, accum_out=ss[:, tt:tt + 1])
        k.ins('dve', 'tensor_scalar', out=ss[:, 0:4], in0=ss[:, 0:4], scalar1=1.0 / D, scalar2=EPS, op0=ALU.mult, op1=ALU.add)
        k.ins('dve', 'reciprocal', out=ss[:, 0:4], in_=ss[:, 0:4])
        k.ins('act', 'activation', out=ss[:, 4:8], in_=ss[:, 0:4], func=AF.Sqrt)
        for tt in range(4):
            xt = xb[tt]
            hb = hn[nt % 2]
            k.ins('dve', 'scalar_tensor_tensor', out=hb[:], in0=xt[:], scalar=ss[:, 4 + tt:5 + tt], in1=ngrep[:], op0=ALU.mult, op1=ALU.mult)
            p_T = k.bank(nt % 2, [128, 8, 128], BF16)
            for kc in range(8):
                k.tr(p_T[:, kc, :], hb[:, kc * 128:(kc + 1) * 128], idb[:])
            k.ins('act', 'copy', out=h_T[:, :, tt * 128:(tt + 1) * 128], in_=p_T)
            nt += 1
        c0 = tg * 512
        for cc in range(7):
            pc = k.bank(2 + rot % 4, [128, 512])
            rot += 1
            for kc in range(8):
                k.mm(pc[:, :], wcm_sb[:, kc, cc * 128:(cc + 1) * 128], h_T[:, kc, :], start=(kc == 0), stop=(kc == 7), inc=(kc == 7))
            if cc < 6:
                z = zc[cc % 2]
                k.ins('act', 'copy', out=z[:, 3:515], in_=pc[:, :])
                k.ins('dve', 'tensor_copy', out=z[:, 0:3], in_=halo[:, cc, :])
                k.ins('dve', 'tensor_copy', out=halo[:, cc, :], in_=z[:, 512:515])
                a = acc[cc % 2]
                k.ins('dve', 'tensor_scalar', out=a[:], in0=z[:, 0:512], scalar1=cw_sb[:, cc, 0:1], scalar2=cb_sb[:, cc:cc + 1], op0=ALU.mult, op1=ALU.add)
                for tap in (1, 2, 3):
                    k.ins('dve', 'scalar_tensor_tensor', out=a[:], in0=z[:, tap:tap + 512], scalar=cw_sb[:, cc, tap:tap + 1], in1=a[:], op0=ALU.mult, op1=ALU.add)
                o = ob[cc % 2]
                k.ins('act', 'activation', out=o[:], in_=a[:], func=AF.Silu)
                k.ld('sp', zcm[cc * 128:(cc + 1) * 128, c0:c0 + 512], o[:])
            else:
                gs = gsb[tg % 2]
                k.ins('act', 'copy', out=gs[:], in_=pc[:, :])
                k.ld('sp', zg[:, c0:c0 + 512], gs[:, :])
        for tt in range(4):
            tm = tmsb[tt % 2]
            for ci, (a0, a1) in enumerate(((0, 512), (512, 1024), (1024, 1536), (1536, 1540))):
                pc = k.bank(2 + rot % 4, [128, 512])
                rot += 1
                for kc in range(8):
                    k.mm(pc[:, 0:a1 - a0], h_T[:, kc, tt * 128:(tt + 1) * 128], wtm_sb[:, kc, a0:a1], start=(kc == 0), stop=(kc == 7), inc=(kc == 7))
                if ci % 2 == 0:
                    k.ins('act', 'copy', out=tm[:, a0:a1], in_=pc[:, 0:a1 - a0])
                else:
                    k.ins('dve', 'tensor_copy', out=tm[:, a0:a1], in_=pc[:, 0:a1 - a0])
            r0 = tg * 512 + tt * 128
            k.ld('sp', ztm[r0:r0 + 128, :], tm[:])


def build_ab(dbg=True, phases=('p',), ngroups=NG, nq=16, stop=9):
    nc = bass.Bass("TRN2", target_bir_lowering=False)
    k = K(nc)
    kS = "ExternalOutput" if dbg else "Internal"
    x = k.dram("x", [T, D], F32, "ExternalInput")
    ng = k.dram("ng", [1, D], F32, "ExternalInput")
    wcm = k.dram("wcm", [D, NCM], F32, "ExternalInput")
    wtm = k.dram("wtm", [D, NTM], F32, "ExternalInput")
    cw = k.dram("cw", [128, 6, 4], F32, "ExternalInput")
    cb = k.dram("cb", [128, 6], F32, "ExternalInput")
    fqg = k.dram("fqg", [1, 128], F32, "ExternalInput")
    fkg = k.dram("fkg", [1, 128], F32, "ExternalInput")
    fbf = k.dram("fbf", [1, 2], F32, "ExternalInput")
    yfox = k.dram("yfox", [T, 256], F32, "ExternalOutput")
    sprm = k.dram("sprm", [1, 12], F32, "ExternalInput")
    ysd = k.dram("ysd", [T, 256], F32, "ExternalOutput")
    mprm = k.dram("mprm", [1, 4], F32, "ExternalInput")
    mlg = k.dram("mlg", [1, 256], F32, "ExternalInput")
    yml = k.dram("yml", [T, 256], F32, "ExternalOutput")
    zcm = k.dram("zcm", [768, T], BF16, kS)
    zg = k.dram("zg", [128, T], F32, kS)
    ztm = k.dram("ztm", [T, NTM], F32, kS)
    k.init_arena()
    idf, idb = make_ident(k)
    outs = []
    if 'p' in phases:
        mp = k.mark()
        phase_p(k, x, ng, wcm, wtm, cw, cb, zcm, zg, ztm, idb, ngroups)
        outs += [zcm, zg, ztm]
        k.release(mp)
    if 'm' in phases:
        phase_m(k, zcm, zg, ztm, mprm, mlg, yml, idf, idb, nch=min(64, 4 * nq), stop=stop)
        outs += [yml]
    if 's' in phases:
        phase_s(k, zcm, zg, ztm, sprm, ysd, idf, idb, nch=min(64, 4 * nq))
        outs += [ysd]
    if 'f' in phases:
        phase_f(k, ztm, zg, fqg, fkg, fbf, yfox, idf, idb, nq, stop)
        outs += [yfox]
    k.final_wait('sp', outs)
    k.emit()
    print("instructions:", k.ninst, "sems:", len(k.sems))
    return nc


def phase_f(k, ztm, zg, fqg, fkg, fbf, yfox, idf, idb, nq=16, stop=9, ystore=None):
    m0 = k.mark()
    ntile = 4 * nq
    gain = k.al("fgain", [128, 4, 128], F32)
    k.ld('sp', gain[:, 0, :], fqg[0:1, :].pbc(128))
    k.ld('sp', gain[:, 1, :], fqg[0:1, :].pbc(128))
    k.ld('sp', gain[:, 2, :], fkg[0:1, :].pbc(128))
    k.ld('sp', gain[:, 3, :], fkg[0:1, :].pbc(128))
    k.ins('dve', 'tensor_scalar', out=gain[:, 2:4, :], in0=gain[:, 2:4, :], scalar1=128.0 ** -0.5, scalar2=None, op0=ALU.mult)
    negb = k.al("fnegb", [64, 2], F32)
    k.ld('sp', negb[:], fbf[0:1, :].pbc(64))
    k.ins('dve', 'tensor_scalar', out=negb[:], in0=negb[:], scalar1=-1.0, scalar2=None, op0=ALU.mult)
    U = k.al("fU", [64, 64], F32)
    k.ins('pool', 'memset', ap=U[:], constant=1.0)
    k.op('pool', lambda e: e.affine_select(out=U[:].ap, in_=U[:].ap, pattern=[[1, 64]], compare_op=ALU.is_gt,
                                           fill=0.0, base=0, channel_multiplier=-1), [U], [U])
    negid = k.al("fnegid", [64, 64], F32)
    k.ins('dve', 'tensor_scalar', out=negid[:], in0=idf[0:64, 0:64], scalar1=-1.0, scalar2=None, op0=ALU.mult)
    maskneg = k.al("fmask", [128, 128], F32)
    k.ins('pool', 'memset', ap=maskneg[:], constant=0.0)
    k.op('pool', lambda e: e.affine_select(out=maskneg[:].ap, in_=maskneg[:].ap, pattern=[[1, 128]], compare_op=ALU.is_ge,
                                           fill=-1e30, base=0, channel_multiplier=-1), [maskneg], [maskneg])
    ones64 = k.al("fones", [64, 128], F32)
    k.ins('pool', 'memset', ap=ones64[:], constant=1.0)
    if stop <= 1:
        k.release(m0)
        return
    gF = k.al("gF", [64, 2, 128], F32)
    if ntile < 64:
        k.ins('pool', 'memset', ap=gF[:], constant=0.0)
    for h in range(2):
        k.ld('sp', gF[0:ntile, h, :], zg[64 + h, 0:ntile * 128].re("(j s) -> j s", s=128))
    e = k.al("fe", [64, 2, 128], F32)
    for h in range(2):
        k.ins('act', 'activation', out=e[:, h, :], in_=gF[:, h, :], func=AF.Exp, scale=-1.0, bias=negb[:, h:h + 1])
    k.ins('act', 'activation', out=e[:], in_=e[:], func=AF.Ln, bias=1.0)
    cs = k.al("fcs", [64, 2, 128], F32)
    for h in range(2):
        k.ins('dve', 'tensor_tensor_scan', out=cs[:, h, :], data0=ones64[:, :], data1=e[:, h, :], initial=0.0, op0=ALU.mult, op1=ALU.add)
    pb = k.bank(0, [64, 2])
    k.mm(pb, U[:, :], cs[:, :, 127])
    offs = k.al("foffs", [64, 2], F32)
    k.ins('dve', 'tensor_copy', out=offs[:], in_=pb)
    Fneg = k.al("Fneg", [64, 2, 128], F32)
    for h in range(2):
        k.ins('dve', 'tensor_scalar', out=Fneg[:, h, :], in0=cs[:, h, :], scalar1=offs[:, h:h + 1], scalar2=None, op0=ALU.add)
    colF = k.al("colF", [128, 2, 64], F32)
    for h in range(2):
        pb = k.bank(1, [128, 64])
        k.tr(pb, Fneg[:, h, :], idf[0:64, 0:64])
        k.ins('dve', 'tensor_copy', out=colF[:, h, :], in_=pb)
    if stop <= 2:
        k.release(m0)
        return
    qT = [k.al("qT%d" % h, [128, T], BF16) for h in range(2)]
    kT = [k.al("kT%d" % h, [128, T], BF16) for h in range(2)]
    Va = [k.al("Va%d" % h, [128, 64, 130], BF16) for h in range(2)]
    for h in range(2):
        k.ins('dve', 'memset', ap=Va[h][:, :, 128:130], constant=1.0)
    if stop <= 2.1:
        k.release(m0)
        return
    qkv = [k.al("qkv%d" % i, [128, 768], F32) for i in range(2)]
    sq = k.al("fsq", [128, 512], F32)
    ssq = k.al("fssq", [128, 8], F32)
    qn = [k.al("qn%d" % i, [128, 512], BF16) for i in range(2)]
    for j in range(ntile):
        t_ = qkv[j % 2]
        k.ld('sp', t_[:], ztm[j * 128:(j + 1) * 128, 768:1536])
        k.ins('act', 'activation', out=sq[:], in_=t_[:, 0:512], func=AF.Square)
        k.ins('dve', 'tensor_reduce', out=ssq[:, 0:4], in_=sq[:].re("p (a b) -> p a b", a=4), axis=AX.X, op=ALU.add)
        k.ins('dve', 'tensor_scalar', out=ssq[:, 0:4], in0=ssq[:, 0:4], scalar1=1.0 / 128, scalar2=EPS, op0=ALU.mult, op1=ALU.add)
        k.ins('dve', 'reciprocal', out=ssq[:, 0:4], in_=ssq[:, 0:4])
        k.ins('act', 'activation', out=ssq[:, 4:8], in_=ssq[:, 0:4], func=AF.Sqrt)
        if stop <= 2.2:
            continue
        n_ = qn[j % 2]
        for seg in range(4):
            k.ins('dve', 'scalar_tensor_tensor', out=n_[:, seg * 128:(seg + 1) * 128], in0=t_[:, seg * 128:(seg + 1) * 128],
                  scalar=ssq[:, 4 + seg:5 + seg], in1=gain[:, seg, :], op0=ALU.mult, op1=ALU.mult)
        pt = k.bank(j % 2, [128, 4, 128], BF16)
        for seg in range(4):
            k.tr(pt[:, seg, :], n_[:, seg * 128:(seg + 1) * 128], idb[:])
        if stop <= 2.3:
            continue
        cols = slice(j * 128, (j + 1) * 128)
        k.ins('act', 'copy', out=qT[0][:, cols], in_=pt[:, 0, :])
        k.ins('act', 'copy', out=qT[1][:, cols], in_=pt[:, 1, :])
        k.ins('act', 'copy', out=kT[0][:, cols], in_=pt[:, 2, :])
        k.ins('act', 'copy', out=kT[1][:, cols], in_=pt[:, 3, :])
        if stop <= 2.4:
            continue
        for h in range(2):
            k.ins('act', 'copy', out=Va[h][:, j, 0:128], in_=t_[:, 512 + h * 128:512 + (h + 1) * 128])
    if stop <= 3:
        k.release(m0)
        return
    NBUF, SKEW = 3, 2
    zs = [k.al("zs%d" % i, [128, 512], F32) for i in range(NBUF)]
    ptb = [k.al("ptb%d" % i, [128, 512], BF16) for i in range(NBUF)]
    frep = [k.al("frep%d" % i, [128, 512], F32) for i in range(2)]
    ysb = [k.al("ysb%d" % i, [128, 128], F32) for i in range(2)]
    rec = k.al("frec", [128, 4], F32)
    it = 0
    for h in range(2):
        for i in range(nq):
            fb = k.bank(0, [128, 512])
            for qq in range(4):
                jb = 4 * i + qq
                k.mm(fb[:, qq * 128:(qq + 1) * 128], negid[:, jb:jb + 1].bc([64, 128]), Fneg[:, h, :])
            fr = frep[i % 2]
            k.ins('act', 'copy', out=fr[:], in_=fb)
            O = [k.bank(4 + qq, [128, 129]) for qq in range(4)]
            J = 4 * i + 4

            def stage_a(j, slot):
                c0 = max(0, j - 4 * i) * 128
                n = 512 - c0
                ps = k.bank(1 + slot, [128, 512])
                k.mm(ps[:, 0:n], kT[h][:, j * 128:(j + 1) * 128], qT[h][:, i * 512 + c0:(i + 1) * 512])
                z = zs[slot]
                k.ins('dve', 'scalar_tensor_tensor', out=z[:, 0:n], in0=ps[:, 0:n], scalar=colF[:, h, j:j + 1],
                      in1=fr[:, c0:512], op0=ALU.add, op1=ALU.add)
                if j >= 4 * i:
                    k.ins('dve', 'tensor_tensor', out=z[:, 0:128], in0=z[:, 0:128], in1=maskneg[:], op=ALU.add)
                p = ptb[slot]
                k.ins('act', 'activation', out=p[:, 0:n], in_=z[:, 0:n], func=AF.Exp)
                return (p, c0)

            def stage_b(j, pc):
                p, c0 = pc
                for qq in range(max(j - 4 * i, 0), 4):
                    col = qq * 128 - c0
                    qb = 4 * i + qq
                    k.mm(O[qq], p[:, col:col + 128], Va[h][:, j, 0:129], start=(j == 0), stop=(j == qb))

            pend = {}
            for j in range(min(SKEW, J)):
                pend[j] = stage_a(j, it % NBUF)
                it += 1
            for j in range(J):
                if j + SKEW < J:
                    pend[j + SKEW] = stage_a(j + SKEW, it % NBUF)
                    it += 1
                stage_b(j, pend.pop(j))
            for qq in range(4):
                k.ins('dve', 'reciprocal', out=rec[:, qq:qq + 1], in_=O[qq][:, 128:129])
                y = ysb[qq % 2]
                k.ins('dve', 'tensor_scalar', out=y[:], in0=O[qq][:, 0:128], scalar1=rec[:, qq:qq + 1], scalar2=None, op0=ALU.mult)
                r0 = (4 * i + qq) * 128
                if ystore is None:
                    k.ld('sp', yfox[r0:r0 + 128, h * 128:(h + 1) * 128], y[:])
                else:
                    ystore(4 * i + qq, 512 + h * 128, 128, y[:])
    k.release(m0)


def phase_s(k, zcm, zg, ztm, sprm, ysd, idf, idb, nch=64, ystore=None):
    m0 = k.mark()
    prm = k.al("sprm", [128, 12], F32)
    k.ld('sp', prm[:], sprm[0:1, :].pbc(128))
    arep_ = k.al("sa", [128, 4], F32)
    k.ins('act', 'activation', out=arep_[:], in_=prm[:, 4:8], func=AF.Exp)
    k.ins('dve', 'tensor_scalar', out=arep_[:], in0=arep_[:], scalar1=-1.0, scalar2=None, op0=ALU.mult)
    maskneg = k.al("smask", [128, 128], F32)
    k.ins('pool', 'memset', ap=maskneg[:], constant=0.0)
    k.op('pool', lambda e: e.affine_select(out=maskneg[:].ap, in_=maskneg[:].ap, pattern=[[1, 128]], compare_op=ALU.is_ge,
                                           fill=-1e30, base=0, channel_multiplier=-1), [maskneg], [maskneg])
    ones64 = k.al("sones", [64, 128], F32)
    k.ins('pool', 'memset', ap=ones64[:], constant=1.0)
    dt = k.al("sdt", [64, 4, 128], F32)
    if nch < 64:
        k.ins('pool', 'memset', ap=dt[:], constant=0.0)
    for h in range(4):
        k.ld('sp', dt[0:nch, h, :], zg[96 + h, 0:nch * 128].re("(c t) -> c t", t=128))
    for h in range(4):
        k.ins('act', 'activation', out=dt[:, h, :], in_=dt[:, h, :], func=AF.Exp, bias=prm[0:64, h:h + 1])
    k.ins('act', 'activation', out=dt[:], in_=dt[:], func=AF.Ln, bias=1.0)
    da = k.al("sda", [64, 4, 128], F32)
    k.ins('dve', 'tensor_tensor', out=da[:], in0=dt[:], in1=arep_[0:64, :].unsq(2).bc([64, 4, 128]), op=ALU.mult)
    acum = k.al("sacum", [64, 4, 128], F32)
    for h in range(4):
        k.ins('dve', 'tensor_tensor_scan', out=acum[:, h, :], data0=ones64[:, :], data1=da[:, h, :], initial=0.0, op0=ALU.mult, op1=ALU.add)
    colA = k.al("scolA", [128, 4, 64], F32)
    colDT = k.al("scolDT", [128, 4, 64], F32)
    for h in range(4):
        pb = k.bank(h % 2, [128, 64])
        k.tr(pb, acum[:, h, :], idf[0:64, 0:64])
        k.ins('dve', 'tensor_copy', out=colA[:, h, :], in_=pb)
        pb2 = k.bank(2 + h % 2, [128, 64])
        k.tr(pb2, dt[:, h, :], idf[0:64, 0:64])
        k.ins('dve', 'tensor_copy', out=colDT[:, h, :], in_=pb2)
    Drep = k.al("sDrep", [128, 4, 64], F32)
    k.ins('dve', 'tensor_copy', out=Drep[:], in_=prm[:, 8:12].unsq(2).bc([128, 4, 64]))
    state = k.al("sstate", [128, 256], F32)
    state_bf = k.al("sstate_bf", [128, 256], BF16)
    k.ins('dve', 'memset', ap=state[:], constant=0.0)
    k.ins('dve', 'memset', ap=state_bf[:], constant=0.0)
    xT = [k.al("sxT%d" % i, [128, 2, 128], BF16) for i in range(2)]
    BT = [k.al("sBT%d" % i, [128, 128], BF16) for i in range(2)]
    CT = [k.al("sCT%d" % i, [128, 128], BF16) for i in range(2)]
    xbt = [k.al("sxbt%d" % i, [128, 384], BF16) for i in range(2)]
    arsb = [k.al("sarsb%d" % i, [128, 4, 128], F32) for i in range(2)]
    dm = [k.al("sdm%d" % i, [128, 4, 128], F32) for i in range(2)]
    mt = [k.al("smt%d" % i, [128, 4, 128], BF16) for i in range(2)]
    ea = [k.al("sea%d" % i, [128, 4, 128], F32) for i in range(2)]
    cts = [k.al("scts%d" % i, [128, 4, 128], BF16) for i in range(2)]
    xdt = [k.al("sxdt%d" % i, [128, 4, 64], BF16) for i in range(2)]
    xdd = [k.al("sxdd%d" % i, [128, 4, 64], BF16) for i in range(2)]
    decs = k.al("sdecs", [128, 8], F32)
    yd = k.al("syd", [128, 64, 256], F32)
    for c in range(nch):
        i2 = c % 2
        cols = slice(c * 128, (c + 1) * 128)
        k.ld('sp', xT[i2][:], zcm[256:512, cols].re("(a p) t -> p a t", p=128))
        k.ld('sp', BT[i2][:], zcm[512:640, cols])
        k.ld('sp', CT[i2][:], zcm[640:768, cols])
        ptr = k.bank(0, [128, 3, 128], BF16)
        k.tr(ptr[:, 0, :], xT[i2][:, 0, :], idb[:])
        k.tr(ptr[:, 1, :], xT[i2][:, 1, :], idb[:])
        k.tr(ptr[:, 2, :], BT[i2][:], idb[:])
        k.ins('act', 'copy', out=xbt[i2][:], in_=ptr)
        pg = k.bank(1, [128, 128])
        k.mm(pg, BT[i2][:], CT[i2][:])
        par = k.bank(2 + i2, [128, 512])
        k.mm(par, idf[0:64, c:c + 1].bc([64, 128]), acum[:].re("c h t -> c (h t)"))
        ar = arsb[i2]
        k.ins('act', 'copy', out=ar[:], in_=par)
        d_ = dm[i2]
        k.ins('dve', 'tensor_tensor', out=d_[:], in0=ar[:], in1=colA[:, :, c:c + 1].bc([128, 4, 128]), op=ALU.subtract)
        k.ins('dve', 'tensor_tensor', out=d_[:], in0=d_[:], in1=maskneg[:].unsq(1).bc([128, 4, 128]), op=ALU.add)
        k.ins('act', 'activation', out=d_[:], in_=d_[:], func=AF.Exp)
        k.ins('dve', 'tensor_tensor', out=mt[i2][:], in0=d_[:], in1=pg.unsq(1).bc([128, 4, 128]), op=ALU.mult)
        xv = xbt[i2][:, 0:256].re("p (h q) -> p h q", h=4)
        k.ins('dve', 'tensor_tensor', out=xdt[i2][:], in0=xv, in1=colDT[:, :, c:c + 1].bc([128, 4, 64]), op=ALU.mult)
        k.ins('act', 'activation', out=ea[i2][:], in_=ar[:], func=AF.Exp)
        k.ins('dve', 'tensor_tensor', out=cts[i2][:], in0=ea[i2][:], in1=CT[i2][:].unsq(1).bc([128, 4, 128]), op=ALU.mult)
        py = k.bank(4 + i2, [128, 256])
        for h in range(4):
            k.mm(py[:, h * 64:(h + 1) * 64], mt[i2][:, h, :], xdt[i2][:, h, :], start=True, stop=False)
            k.mm(py[:, h * 64:(h + 1) * 64], cts[i2][:, h, :], state_bf[:, h * 64:(h + 1) * 64], start=False, stop=True)
        ydc = yd[:, c, :]
        k.ins('dve', 'tensor_tensor', out=ydc, in0=xbt[i2][:, 0:256], in1=Drep[:].re("p h q -> p (h q)"), op=ALU.mult)
        k.ins('dve', 'tensor_tensor', out=ydc, in0=ydc, in1=py, op=ALU.add)
        k.ins('dve', 'tensor_tensor', out=decs[:, 0:4], in0=ar[:, :, 127], in1=colA[:, :, c], op=ALU.subtract)
        k.ins('act', 'activation', out=decs[:, 4:8], in_=decs[:, 0:4], func=AF.Exp)
        k.ins('dve', 'tensor_tensor', out=xdd[i2][:], in0=xdt[i2][:], in1=decs[:, 4:8].unsq(2).bc([128, 4, 64]), op=ALU.mult)
        pst = k.bank(6, [128, 256])
        k.mm(pst, xbt[i2][:, 256:384], xdd[i2][:].re("p h q -> p (h q)"))
        sv = state[:].re("p (h q) -> p h q", h=4)
        k.ins('dve', 'tensor_tensor', out=sv, in0=sv, in1=ea[i2][:, :, 127:128].bc([128, 4, 64]), op=ALU.mult)
        k.ins('dve', 'tensor_tensor', out=state[:], in0=state[:], in1=pst, op=ALU.add)
        k.ins('act', 'copy', out=state_bf[:], in_=state[:])
    zt = [k.al("szt%d" % i, [128, 256], F32) for i in range(2)]
    yo = [k.al("syo%d" % i, [128, 256], F32) for i in range(2)]
    for c in range(nch):
        i2 = c % 2
        rows = slice(c * 128, (c + 1) * 128)
        k.ld('sp', zt[i2][:], ztm[rows, 512:768])
        k.ins('act', 'activation', out=zt[i2][:], in_=zt[i2][:], func=AF.Silu)
        k.ins('dve', 'tensor_tensor', out=yo[i2][:], in0=zt[i2][:], in1=yd[:, c, :], op=ALU.mult)
        if ystore is None:
            k.ld('sp', ysd[rows, :], yo[i2][:])
        else:
            ystore(c, 256, 256, yo[i2][:])
    k.release(m0)


def phase_m(k, zcm, zg, ztm, mprm, mlg, yml, idf, idb, nch=64, stop=9, ystore=None):
    import math
    LN8 = math.log(0.125)
    m0 = k.mark()
    prm = k.al("mprm", [64, 4], F32)
    k.ld('sp', prm[:], mprm[0:1, :].pbc(64))
    nbf = k.al("mnbf", [64, 2], F32)
    k.ins('dve', 'tensor_scalar', out=nbf[:], in0=prm[:, 2:4], scalar1=-1.0, scalar2=None, op0=ALU.mult)
    grep = k.al("mgrep", [128, 256], F32)
    k.ld('sp', grep[:], mlg[0:1, :].pbc(128))
    maskneg = k.al("mmask", [128, 128], F32)
    k.ins('pool', 'memset', ap=maskneg[:], constant=0.0)
    k.op('pool', lambda e: e.affine_select(out=maskneg[:].ap, in_=maskneg[:].ap, pattern=[[1, 128]], compare_op=ALU.is_ge,
                                           fill=-1e30, base=0, channel_multiplier=-1), [maskneg], [maskneg])
    ones64 = k.al("mones", [64, 128], F32)
    k.ins('pool', 'memset', ap=ones64[:], constant=1.0)
    zeros64 = k.al("mzeros", [64, 128], F32)
    k.ins('pool', 'memset', ap=zeros64[:], constant=0.0)
    hmask = k.al("mhmask", [128, 2], F32)
    k.ins('pool', 'memset', ap=hmask[:], constant=1.0)
    k.op('pool', lambda e: e.affine_select(out=hmask[:, 0:1].ap, in_=hmask[:, 0:1].ap, pattern=[[0, 1]], compare_op=ALU.is_ge,
                                           fill=0.0, base=63, channel_multiplier=-1), [hmask], [hmask])
    k.op('pool', lambda e: e.affine_select(out=hmask[:, 1:2].ap, in_=hmask[:, 1:2].ap, pattern=[[0, 1]], compare_op=ALU.is_ge,
                                           fill=0.0, base=-64, channel_multiplier=1), [hmask], [hmask])
    ir = k.al("mir", [64, 2, 128], F32)
    fr = k.al("mfr", [64, 2, 128], F32)
    if nch < 64:
        k.ins('pool', 'memset', ap=ir[:], constant=0.0)
        k.ins('pool', 'memset', ap=fr[:], constant=0.0)
    for h in range(2):
        k.ld('sp', ir[0:nch, h, :], zg[h, 0:nch * 128].re("(c t) -> c t", t=128))
        k.ld('sp', fr[0:nch, h, :], zg[32 + h, 0:nch * 128].re("(c t) -> c t", t=128))
    for h in range(2):
        k.ins('act', 'activation', out=fr[:, h, :], in_=fr[:, h, :], func=AF.Exp, scale=-1.0, bias=nbf[:, h:h + 1])
    k.ins('act', 'activation', out=fr[:], in_=fr[:], func=AF.Ln, bias=1.0)
    bneg = k.al("mbneg", [64, 2, 128], F32)
    a_ = k.al("ma", [64, 2, 128], F32)
    cm = k.al("mcm", [64, 2, 128], F32)
    for h in range(2):
        k.ins('dve', 'tensor_tensor_scan', out=bneg[:, h, :], data0=ones64[:, :], data1=fr[:, h, :], initial=0.0, op0=ALU.mult, op1=ALU.add)
    for h in range(2):
        k.ins('dve', 'scalar_tensor_tensor', out=a_[:, h, :], in0=ir[:, h, :], scalar=prm[:, h:h + 1], in1=bneg[:, h, :], op0=ALU.add, op1=ALU.add)
    for h in range(2):
        k.ins('dve', 'tensor_tensor_scan', out=cm[:, h, :], data0=zeros64[:, :], data1=a_[:, h, :], initial=-1e30, op0=ALU.add, op1=ALU.max)
    gq = k.al("mgq", [64, 2], F32)
    cmq = k.al("mcmq", [64, 2], F32)
    k.ins('dve', 'tensor_scalar', out=gq[:], in0=bneg[:, :, 127], scalar1=-1.0, scalar2=None, op0=ALU.mult)
    k.ins('dve', 'tensor_copy', out=cmq[:], in_=cm[:, :, 127])
    if stop <= 1:
        k.release(m0)
        return
    gT = k.al("mgT", [2, 64], F32)
    cT = k.al("mcT", [2, 64], F32)
    pb = k.bank(0, [2, 64])
    k.tr(pb, gq[:], idf[0:64, 0:64])
    k.ins('dve', 'tensor_copy', out=gT[:], in_=pb)
    pb = k.bank(1, [2, 64])
    k.tr(pb, cmq[:], idf[0:64, 0:64])
    k.ins('dve', 'tensor_copy', out=cT[:], in_=pb)
    mnext = k.al("mmnext", [2, 64], F32)
    k.ins('dve', 'tensor_tensor_scan', out=mnext[:], data0=cT[:], data1=gT[:], initial=0.0, op0=ALU.max, op1=ALU.add)
    Mrow = k.al("mMrow", [2, 64], F32)
    k.ins('dve', 'memset', ap=Mrow[:], constant=0.0)
    k.ins('dve', 'tensor_copy', out=Mrow[:, 1:64], in_=mnext[:, 0:63])
    Mcol = k.al("mMcol", [64, 2], F32)
    mncol = k.al("mmncol", [64, 2], F32)
    pb = k.bank(2, [64, 2])
    k.tr(pb, Mrow[:], idf[0:2, 0:2])
    k.ins('dve', 'tensor_copy', out=Mcol[:], in_=pb)
    pb = k.bank(3, [64, 2])
    k.tr(pb, mnext[:], idf[0:2, 0:2])
    k.ins('dve', 'tensor_copy', out=mncol[:], in_=pb)
    if stop <= 2:
        k.release(m0)
        return
    mt_ = k.al("mmt", [64, 2, 128], F32)
    k.ins('dve', 'tensor_tensor', out=mt_[:], in0=cm[:], in1=Mcol[:].unsq(2).bc([64, 2, 128]), op=ALU.max)
    k.ins('dve', 'tensor_tensor', out=mt_[:], in0=mt_[:], in1=bneg[:], op=ALU.subtract)
    R = k.al("mR", [64, 258], F32)
    Rv = R[:, 0:256].re("p (h t) -> p h t", h=2)
    k.ins('dve', 'tensor_tensor', out=Rv, in0=bneg[:], in1=mt_[:], op=ALU.add)
    k.ins('dve', 'tensor_scalar', out=Rv, in0=Rv, scalar1=-1.0, scalar2=None, op0=ALU.mult)
    k.ins('dve', 'tensor_tensor', out=R[:, 256:258], in0=gq[:], in1=Mcol[:], op=ALU.add)
    k.ins('dve', 'tensor_tensor', out=R[:, 256:258], in0=R[:, 256:258], in1=mncol[:], op=ALU.subtract)
    Q = k.al("mQ", [64, 8, 128], F32)
    gm = k.al("mgm", [64, 2], F32)
    k.ins('dve', 'tensor_scalar', out=Q[:, 0:2, :], in0=a_[:], scalar1=LN8, scalar2=None, op0=ALU.add)
    k.ins('dve', 'tensor_tensor', out=gm[:], in0=gq[:], in1=mncol[:], op=ALU.subtract)
    k.ins('dve', 'tensor_tensor', out=Q[:, 2:4, :], in0=a_[:], in1=gm[:].unsq(2).bc([64, 2, 128]), op=ALU.add)
    k.ins('dve', 'tensor_tensor', out=Q[:, 4:6, :], in0=Rv, in1=Mcol[:].unsq(2).bc([64, 2, 128]), op=ALU.add)
    k.ins('dve', 'tensor_scalar', out=Q[:, 4:6, :], in0=Q[:, 4:6, :], scalar1=LN8, scalar2=None, op0=ALU.add)
    k.ins('dve', 'tensor_scalar', out=Q[:, 6:8, :], in0=mt_[:], scalar1=-1.0, scalar2=None, op0=ALU.mult)
    colQ = k.al("mcolQ", [128, 8, 64], F32)
    for q in range(8):
        pb = k.bank(q % 4, [128, 64])
        k.tr(pb, Q[:, q, :], idf[0:64, 0:64])
        k.ins('dve', 'tensor_copy', out=colQ[:, q, :], in_=pb)
    k.ins('act', 'activation', out=colQ[:, 2:8, :], in_=colQ[:, 2:8, :], func=AF.Exp)
    if stop <= 3:
        k.release(m0)
        return
    state = k.al("mstate", [128, 130], F32)
    state_bf = k.al("mstate_bf", [128, 2, 130], BF16)
    kz = [k.al("mkz%d" % i, [128, 2, 128], BF16) for i in range(2)]
    k.ins('dve', 'memset', ap=state[:], constant=0.0)
    k.ins('dve', 'memset', ap=state_bf[:], constant=0.0)
    qk = [k.al("mqk%d" % i, [128, 2, 128], BF16) for i in range(2)]
    vo = [k.al("mvo%d" % i, [128, 512], F32) for i in range(2)]
    Va = [k.al("mVa%d" % i, [128, 2, 130], BF16) for i in range(2)]
    for i in range(2):
        k.ins('dve', 'memset', ap=Va[i][:, :, 128:130], constant=1.0)
    kw = [k.al("mkw%d" % i, [128, 2, 64], BF16) for i in range(2)]
    rrsb = [k.al("mrr%d" % i, [128, 258], F32) for i in range(2)]
    dmx = [k.al("mdmx%d" % i, [128, 2, 128], F32) for i in range(2)]
    pm = [k.al("mpm%d" % i, [128, 2, 128], BF16) for i in range(2)]
    o1 = [k.al("mo1%d" % i, [128, 2, 129], F32) for i in range(2)]
    osb = [k.al("mos%d" % i, [128, 2, 129], F32) for i in range(2)]
    sm = k.al("msm", [128, 16], F32)
    hsb = [k.al("mhs%d" % i, [128, 2, 128], F32) for i in range(2)]
    junk = k.al("mjunk", [128, 128], F32)
    sg = [k.al("msg%d" % i, [128, 256], F32) for i in range(2)]
    yo = [k.al("myo%d" % i, [128, 256], F32) for i in range(2)]
    for c in range(nch):
        i2 = c % 2
        cols = slice(c * 128, (c + 1) * 128)
        rows = slice(c * 128, (c + 1) * 128)
        k.ld('sp', qk[i2][:], zcm[0:256, cols].re("(a p) t -> p a t", p=128))
        k.ld('sp', vo[i2][:], ztm[rows, 0:512])
        k.ins('act', 'copy', out=Va[i2][:, :, 0:128], in_=vo[i2][:, 0:256].re("p (h v) -> p h v", h=2))
        ptk = k.bank(0, [128, 128], BF16)
        k.tr(ptk, qk[i2][:, 1, :], idb[:])
        for h in range(2):
            k.ins('act', 'activation', out=kw[i2][:, h, :], in_=ptk[:, h * 64:(h + 1) * 64], func=AF.Copy, scale=colQ[:, 2 + h, c:c + 1])
        if stop <= 4:
            continue
        pS = k.bank(1, [128, 2, 128])
        for h in range(2):
            k.ins('dve', 'tensor_scalar', out=kz[i2][:, h, :], in0=qk[i2][:, 1, :], scalar1=hmask[:, h:h + 1], scalar2=None, op0=ALU.mult)
        for h in range(2):
            k.mm(pS[:, h, :], kz[i2][:, h, :], qk[i2][:, 0, :])
        if stop <= 4.2:
            continue
        prr = k.bank(2 + i2, [128, 258])
        k.mm(prr, idf[0:64, c:c + 1].bc([64, 128]), R[:, :])
        rr = rrsb[i2]
        k.ins('act', 'copy', out=rr[:], in_=prr)
        if stop <= 4.4:
            continue
        d_ = dmx[i2]
        k.ins('dve', 'tensor_tensor', out=d_[:], in0=rr[:, 0:256].re("p (h t) -> p h t", h=2), in1=colQ[:, 0:2, c:c + 1].bc([128, 2, 128]), op=ALU.add)
        k.ins('dve', 'tensor_tensor', out=d_[:], in0=d_[:], in1=maskneg[:].unsq(1).bc([128, 2, 128]), op=ALU.add)
        k.ins('act', 'activation', out=d_[:], in_=d_[:], func=AF.Exp)
        if stop <= 4.6:
            continue
        k.ins('dve', 'tensor_tensor', out=pm[i2][:], in0=d_[:], in1=pS, op=ALU.mult)
        if stop <= 5:
            continue
        pO1 = k.bank(4, [128, 2, 129])
        pO2 = k.bank(5, [128, 2, 129])
        for h in range(2):
            k.mm(pO1[:, h, :], pm[i2][:, h, :], Va[i2][:, h, 0:129])
        for h in range(2):
            k.mm(pO2[:, h, :], qk[i2][:, 0, :], state_bf[:, h, 0:129])
        k.ins('act', 'copy', out=o1[i2][:], in_=pO1)
        for h in range(2):
            k.ins('dve', 'scalar_tensor_tensor', out=osb[i2][:, h, :], in0=pO2[:, h, :], scalar=colQ[:, 4 + h, c:c + 1],
                  in1=o1[i2][:, h, :], op0=ALU.mult, op1=ALU.add)
        k.ins('dve', 'tensor_scalar', out=sm[:, 12:14], in0=osb[i2][:, :, 128], scalar1=-1.0, scalar2=None, op0=ALU.mult)
        k.ins('dve', 'tensor_tensor', out=sm[:, 0:2], in0=sm[:, 12:14], in1=osb[i2][:, :, 128], op=ALU.max)
        k.ins('dve', 'tensor_tensor', out=sm[:, 0:2], in0=sm[:, 0:2], in1=colQ[:, 6:8, c], op=ALU.max)
        k.ins('dve', 'reciprocal', out=sm[:, 2:4], in_=sm[:, 0:2])
        for h in range(2):
            k.ins('dve', 'tensor_scalar', out=hsb[i2][:, h, :], in0=osb[i2][:, h, 0:128], scalar1=sm[:, 2 + h:3 + h], scalar2=None, op0=ALU.mult)
        if stop <= 6:
            continue
        for h in range(2):
            k.ins('act', 'activation', out=junk[:], in_=hsb[i2][:, h, :], func=AF.Square, accum_out=sm[:, 4 + h:5 + h])
        k.ins('act', 'activation', out=sm[:, 6:8], in_=sm[:, 4:6], func=AF.Ln, scale=1.0 / 128, bias=EPS)
        k.ins('act', 'activation', out=sm[:, 8:10], in_=sm[:, 6:8], func=AF.Exp, scale=-0.5)
        k.ins('act', 'activation', out=sg[i2][:], in_=vo[i2][:, 256:512], func=AF.Exp, scale=-1.0)
        k.ins('dve', 'tensor_scalar', out=sg[i2][:], in0=sg[i2][:], scalar1=1.0, scalar2=None, op0=ALU.add)
        k.ins('dve', 'reciprocal', out=sg[i2][:], in_=sg[i2][:])
        for h in range(2):
            k.ins('dve', 'scalar_tensor_tensor', out=yo[i2][:, h * 128:(h + 1) * 128], in0=hsb[i2][:, h, :], scalar=sm[:, 8 + h:9 + h],
                  in1=grep[:, h * 128:(h + 1) * 128], op0=ALU.mult, op1=ALU.mult)
        k.ins('dve', 'tensor_tensor', out=yo[i2][:], in0=yo[i2][:], in1=sg[i2][:], op=ALU.mult)
        if ystore is None:
            k.ld('sp', yml[rows, :], yo[i2][:])
        else:
            ystore(c, 0, 256, yo[i2][:])
        if stop <= 7:
            continue
        pst = k.bank(6, [128, 260])
        k.mm(pst, kw[i2][:].re("p h d -> p (h d)"), Va[i2][:].re("p h v -> p (h v)"))
        k.ins('act', 'activation', out=sm[:, 10:12], in_=rr[:, 256:258], func=AF.Exp)
        for h in range(2):
            ps_ = slice(64 * h, 64 * h + 64)
            k.ins('dve', 'scalar_tensor_tensor', out=state[ps_, 0:129], in0=state[ps_, 0:129], scalar=sm[ps_, 10 + h:11 + h],
                  in1=pst[ps_, 130 * h:130 * h + 129], op0=ALU.mult, op1=ALU.add)
        for h in range(2):
            k.ins('act', 'activation', out=state_bf[:, h, :], in_=state[:], func=AF.Copy, scale=hmask[:, h:h + 1])
    k.release(m0)


TT = 2048
NT = TT // 128
D = 1024
NE = 16384


def load_w_bf16(k, dst, src, ncols, stage, row_scale=None):
    kcn = src[:, :].shape[0] // 128
    i = 0
    for kc in range(kcn):
        for c0 in range(0, ncols, 1024):
            c1 = min(ncols, c0 + 1024)
            st = stage[i % 2]
            i += 1
            k.ld('sp', st[:, 0:c1 - c0], src[kc * 128:(kc + 1) * 128, c0:c1])
            if row_scale is None:
                k.ins('act', 'copy', out=dst[:, kc, c0:c1], in_=st[:, 0:c1 - c0])
            else:
                k.ins('act', 'activation', out=dst[:, kc, c0:c1], in_=st[:, 0:c1 - c0], func=AF.Copy, scale=row_scale[:, kc:kc + 1])


def rms_tile(k, xt, grep, hb, junk, ss, idx):
    a, b, c = 3 * idx, 3 * idx + 1, 3 * idx + 2
    k.ins('act', 'activation', out=junk[:], in_=xt, func=AF.Square, accum_out=ss[:, a:a + 1])
    k.ins('dve', 'tensor_scalar', out=ss[:, b:b + 1], in0=ss[:, a:a + 1], scalar1=1.0 / D, scalar2=EPS, op0=ALU.mult, op1=ALU.add)
    k.ins('dve', 'reciprocal', out=ss[:, b:b + 1], in_=ss[:, b:b + 1])
    k.ins('act', 'activation', out=ss[:, c:c + 1], in_=ss[:, b:b + 1], func=AF.Sqrt)
    k.ins('dve', 'scalar_tensor_tensor', out=hb, in0=xt, scalar=ss[:, c:c + 1], in1=grep[:], op0=ALU.mult, op1=ALU.mult)


def transpose8(k, dstT, hb, idb, bank):
    p_T = k.bank(bank, [128, 8, 128], BF16)
    for kc in range(8):
        k.tr(p_T[:, kc, :], hb[:, kc * 128:(kc + 1) * 128], idb[:])
    k.ins('act', 'copy', out=dstT, in_=p_T)


def phase_c1(k, d, idf, idb, nt=NT, fused=False):
    m0 = k.mark()
    stage = [k.al("c1st%d" % i, [128, 1024], F32) for i in range(2)]
    grep = k.al("c1g", [128, D], F32)
    k.ld('sp', grep[:], d['ng_mix'][0:1, :].pbc(128))
    sng = k.al("c1sng", [128, 8], F32)
    k.ld('sp', sng[:], d['sng'][:, :])
    wg = k.al("c1wg", [128, 8, 3072], BF16)
    load_w_bf16(k, wg, d['wg'], 3072, stage)
    wb = {}
    for nm in ('w_ml', 'w_ssm', 'w_fox', 'w_out'):
        wb[nm] = k.al("c1" + nm, [128, 8, D], BF16)
        load_w_bf16(k, wb[nm], d[nm], D, stage, row_scale=(sng if nm == 'w_ssm' else None))
    xb = [k.al("c1x%d" % i, [128, D], F32) for i in range(2)]
    junk = k.al("c1junk", [128, D], BF16)
    ss = k.al("c1ss", [128, 12], F32)
    hb = k.al("c1hb", [128, D], BF16)
    hT = k.al("c1hT", [128, 8, 128], BF16)
    sg = k.al("c1sg", [128, 3072], F32)
    yst = [k.al("c1yst%d" % i, [128, 8, 128], F32) for i in range(2)] if not fused else None
    ybf = [k.al("c1ybf%d" % i, [128, 8, 128], BF16) for i in range(3)]
    ytm = k.al("c1ytm", [128, D], F32) if not fused else None
    mg = k.al("c1mg", [128, D], F32)
    tmp = k.al("c1tmp", [128, 512], F32)
    mgb = k.al("c1mgb", [128, D], BF16)
    mT = k.al("c1mT", [128, 8, 128], BF16)
    x1 = [k.al("c1x1%d" % i, [128, D], F32) for i in range(2)]
    if fused:
        bm = k.al("c1bm", [128, 4], F32)
        k.ld('sp', bm[:], d['bmask'][0:1, :].pbc(128))
        y3a = k.al("c1y3a", [128, 4, 768], F32)
        y3b = k.al("c1y3b", [128, 4, 768], F32)
        ytb = k.al("c1ytb", [128, 3, D], BF16)
    rot = 0
    for tt in range(nt):
        rows = slice(tt * 128, (tt + 1) * 128)
        xt = xb[tt % 2]
        k.ld('sp', xt[:], d['xtok'][rows, :])
        rms_tile(k, xt[:], grep, hb[:], junk, ss, 0)
        transpose8(k, hT[:], hb, idb, 0)
        for cc in range(6):
            pc = k.bank(2 + rot % 4, [128, 512])
            rot += 1
            for kc in range(8):
                k.mm(pc, hT[:, kc, :], wg[:, kc, cc * 512:(cc + 1) * 512], start=(kc == 0), stop=(kc == 7), inc=(kc == 7))
            k.ins('act', 'activation', out=sg[:, cc * 512:(cc + 1) * 512], in_=pc, func=AF.Sigmoid)
        if not fused:
            for bi, nm in enumerate(('ymlT', 'ysdT', 'yfoxT')):
                st = yst[bi % 2]
                k.ld('sp', st[:], d[nm][:, rows].re("(kc p) t -> p kc t", p=128))
                k.ins('act', 'copy', out=ybf[bi][:], in_=st[:])
            k.ld('sp', ytm[:], d['ysd_tm'][rows, :])
            for grp in range(2):
                k.ins('act', 'activation', out=junk[:, 0:512], in_=ytm[:, grp * 512:(grp + 1) * 512], func=AF.Square, accum_out=ss[:, 3 + grp:4 + grp])
        else:
            for rr in range(4):
                for q in range(4):
                    k.ld('sp', y3b[:, q, :], d['ydst'][q * 8 + tt // 2, rr, (tt % 2) * 128:(tt % 2) * 128 + 128, :])
                k.ins('dve', 'tensor_scalar', out=y3a[:, rr, :], in0=y3b[:, 0, :], scalar1=bm[:, 0:1], scalar2=None, op0=ALU.mult)
                for q in range(1, 4):
                    k.ins('dve', 'scalar_tensor_tensor', out=y3a[:, rr, :], in0=y3b[:, q, :], scalar=bm[:, q:q + 1], in1=y3a[:, rr, :], op0=ALU.mult, op1=ALU.add)
            for bi in range(3):
                k.ins('act', 'copy', out=ytb[:, bi, :].re("p (r c) -> p r c", r=4), in_=y3a[:, :, bi * 256:(bi + 1) * 256])
            for bi in range(3):
                transpose8(k, ybf[bi][:], ytb[:, bi, :], idb, bi % 2)
            for grp in range(2):
                k.ins('act', 'activation', out=junk[:, 0:512].re("p (r c) -> p r c", r=2), in_=y3a[:, 2 * grp:2 * grp + 2, 256:512], func=AF.Square, accum_out=ss[:, 3 + grp:4 + grp])
        k.ins('dve', 'tensor_scalar', out=ss[:, 5:7], in0=ss[:, 3:5], scalar1=1.0 / 512, scalar2=EPS, op0=ALU.mult, op1=ALU.add)
        k.ins('dve', 'reciprocal', out=ss[:, 5:7], in_=ss[:, 5:7])
        k.ins('act', 'activation', out=ss[:, 7:9], in_=ss[:, 5:7], func=AF.Sqrt)
        for half in range(2):
            hs = slice(half * 512, (half + 1) * 512)
            pc = k.bank(2 + rot % 4, [128, 512])
            rot += 1
            for kc in range(8):
                k.mm(pc, ybf[0][:, kc, :], wb['w_ml'][:, kc, hs], start=(kc == 0), stop=(kc == 7), inc=(kc == 7))
            k.ins('dve', 'tensor_tensor', out=mg[:, hs], in0=sg[:, hs], in1=pc, op=ALU.mult)
            pc = k.bank(2 + rot % 4, [128, 512])
            rot += 1
            for kc in range(8):
                k.mm(pc, ybf[2][:, kc, :], wb['w_fox'][:, kc, hs], start=(kc == 0), stop=(kc == 7), inc=(kc == 7))
            k.ins('dve', 'tensor_tensor', out=tmp[:], in0=sg[:, 2048 + half * 512:2048 + (half + 1) * 512], in1=pc, op=ALU.mult)
            k.ins('dve', 'tensor_tensor', out=mg[:, hs], in0=mg[:, hs], in1=tmp[:], op=ALU.add)
            pg = []
            for grp in range(2):
                pc = k.bank(2 + rot % 4, [128, 512])
                rot += 1
                for q in range(4):
                    kc = grp * 4 + q
                    k.mm(pc, ybf[1][:, kc, :], wb['w_ssm'][:, kc, hs], start=(q == 0), stop=(q == 3), inc=(q == 3))
                pg.append(pc)
            k.ins('dve', 'tensor_scalar', out=tmp[:], in0=pg[0], scalar1=ss[:, 7:8], scalar2=None, op0=ALU.mult)
            k.ins('dve', 'scalar_tensor_tensor', out=tmp[:], in0=pg[1], scalar=ss[:, 8:9], in1=tmp[:], op0=ALU.mult, op1=ALU.add)
            k.ins('dve', 'tensor_tensor', out=tmp[:], in0=tmp[:], in1=sg[:, 1024 + half * 512:1024 + (half + 1) * 512], op=ALU.mult)
            k.ins('dve', 'tensor_tensor', out=mg[:, hs], in0=mg[:, hs], in1=tmp[:], op=ALU.add)
        k.ins('act', 'copy', out=mgb[:], in_=mg[:])
        transpose8(k, mT[:], mgb, idb, 1)
        xo = x1[tt % 2]
        for half in range(2):
            hs = slice(half * 512, (half + 1) * 512)
            pc = k.bank(2 + rot % 4, [128, 512])
            rot += 1
            for kc in range(8):
                k.mm(pc, mT[:, kc, :], wb['w_out'][:, kc, hs], start=(kc == 0), stop=(kc == 7), inc=(kc == 7))
            k.ins('dve', 'tensor_tensor', out=xo[:, hs], in0=xt[:, hs], in1=pc, op=ALU.add)
        k.ld('sp', d['x1s'][rows, :], xo[:])
    k.release(m0)


def phase_c2a(k, d, idb, nblk=32):
    m0 = k.mark()
    ust = [k.al("c2ust%d" % i, [128, 4, D], F32) for i in range(2)]
    ub = [k.al("c2ub%d" % i, [128, 4, D], BF16) for i in range(2)]
    uts = [k.al("c2uts%d" % i, [128, 8, 512], BF16) for i in range(2)]
    vst = [k.al("c2vst%d" % i, [128, 4, D], F32) for i in range(2)]
    vbb = [k.al("c2vbb%d" % i, [128, 4, D], BF16) for i in range(2)]
    UTv = d['UT'][:, :, :].re("kc p e -> p kc e")
    for blk in range(nblk):
        i2 = blk % 2
        e0 = blk * 512
        k.ld('sp', ust[i2][:], d['peer_u'][e0:e0 + 512, :].re("(a p) d -> p a d", p=128))
        k.ins('act', 'copy', out=ub[i2][:], in_=ust[i2][:])
        for kc in range(8):
            pT = k.bank(kc % 4, [128, 4, 128], BF16)
            for a in range(4):
                k.tr(pT[:, a, :], ub[i2][:, a, kc * 128:(kc + 1) * 128], idb[:])
            k.ins('act', 'copy', out=uts[i2][:, kc, :], in_=pT)
        k.ld('act', UTv[:, :, e0:e0 + 512], uts[i2][:])
        k.ld('sp', vst[i2][:], d['peer_v'][e0:e0 + 512, :].re("(a p) d -> p a d", p=128))
        k.ins('dve', 'tensor_copy', out=vbb[i2][:], in_=vst[i2][:])
        k.ld('act', d['Vb'][e0:e0 + 512, :].re("(a p) d -> p a d", p=128), vbb[i2][:])
    k.release(m0)


def phase_c2b(k, d, xnT, idb, nt=NT):
    m0 = k.mark()
    stage = [k.al("c2st%d" % i, [128, 1024], F32) for i in range(2)]
    grep = k.al("c2g", [128, D], F32)
    k.ld('sp', grep[:], d['ng_ffn'][0:1, :].pbc(128))
    wq = k.al("c2wq", [128, 8, 2048], BF16)
    load_w_bf16(k, wq, d['w_q'], 2048, stage)
    kst = k.al("c2kst", [128, 16, 128], F32)
    k.ld('sp', kst[:], d['keysT'][:, :, :].re("j p i -> p j i"))
    keys = k.al("c2keys", [128, 16, 128], BF16)
    k.ins('act', 'copy', out=keys[:], in_=kst[:])
    xb = [k.al("c2x%d" % i, [128, D], F32) for i in range(2)]
    junk = k.al("c2junk", [128, D], BF16)
    ss = k.al("c2ss", [128, 12], F32)
    hb = k.al("c2hb", [128, D], BF16)
    qTs = k.al("c2qTs", [128, 16, 512], BF16)
    sc = k.al("c2sc", [128, 16, 128], F32)
    wk = k.al("c2wk", [128, 256], F32)
    M1 = k.al("c2M1", [128, 8, 16], F32)
    M2 = k.al("c2M2", [128, 8, 16], F32)
    C16 = k.al("c2C16", [128, 8, 16], F32)
    cand = k.al("c2cand", [128, 16, 16], F32)
    e16 = k.al("c2e16", [128, 8, 16], F32)
    st = k.al("c2stt", [128, 6, 8], F32)
    gp = [k.al("c2gp%d" % i, [128, 8, 260], F32) for i in range(2)]
    for i in range(2):
        k.ins('dve', 'memset', ap=gp[i][:], constant=0.0)
    ngrp = (nt + 3) // 4
    rot = 0
    for tg in range(ngrp):
        ntile = min(4, nt - tg * 4)
        ncol = ntile * 128
        for tt in range(ntile):
            t_ = tg * 4 + tt
            rows = slice(t_ * 128, (t_ + 1) * 128)
            xt = xb[t_ % 2]
            k.ld('sp', xt[:], d['x1s'][rows, :])
            rms_tile(k, xt[:], grep, hb[:], junk, ss, 0)
            transpose8(k, xnT[:, :, t_ * 128:(t_ + 1) * 128], hb, idb, t_ % 2)
        g0 = tg * 512
        for j in range(16):
            pc = k.bank(2 + rot % 4, [128, 512])
            rot += 1
            for kc in range(8):
                k.mm(pc[:, 0:ncol], wq[:, kc, j * 128:(j + 1) * 128], xnT[:, kc, g0:g0 + ncol], start=(kc == 0), stop=(kc == 7), inc=(kc == 7))
            k.ins('act', 'copy', out=qTs[:, j, 0:ncol], in_=pc[:, 0:ncol])
        for tt in range(ntile):
            t_ = tg * 4 + tt
            rows = slice(t_ * 128, (t_ + 1) * 128)
            for jb in range(4):
                pc = k.bank(2 + rot % 4, [128, 4, 128])
                rot += 1
                for q in range(4):
                    j = jb * 4 + q
                    k.mm(pc[:, q, :], qTs[:, j, tt * 128:(tt + 1) * 128], keys[:, j, :])
                k.ins('act', 'copy', out=sc[:, jb * 4:(jb + 1) * 4, :], in_=pc)
            g = gp[t_ % 2]
            for h in range(8):
                for half, MM in ((0, M1), (1, M2)):
                    s_ = sc[:, 2 * h + half, :]
                    k.ins('dve', 'max', out=MM[:, h, 0:8], in_=s_)
                    k.ins('dve', 'match_replace', out=wk[:, 0:128], in_to_replace=MM[:, h, 0:8], in_values=s_, imm_value=-1e30)
                    k.ins('dve', 'max', out=MM[:, h, 8:16], in_=wk[:, 0:128])
                k.ins('dve', 'tensor_tensor', out=cand[:], in0=M1[:, h, :].unsq(2).bc([128, 16, 16]), in1=M2[:, h, :].unsq(1).bc([128, 16, 16]), op=ALU.add)
                cf = cand[:].re("p a b -> p (a b)")
                k.ins('dve', 'max', out=C16[:, h, 0:8], in_=cf)
                k.ins('dve', 'match_replace', out=wk[:, :], in_to_replace=C16[:, h, 0:8], in_values=cf, imm_value=-1e30)
                k.ins('dve', 'max', out=C16[:, h, 8:16], in_=wk[:, :])
            k.ins('dve', 'tensor_tensor', out=e16[:], in0=C16[:], in1=C16[:, :, 0:1].bc([128, 8, 16]), op=ALU.subtract)
            k.ins('act', 'activation', out=e16[:], in_=e16[:], func=AF.Exp)
            k.ins('dve', 'tensor_reduce', out=st[:, 0, :], in_=e16[:], axis=AX.X, op=ALU.add)
            k.ins('act', 'activation', out=st[:, 1, :], in_=st[:, 0, :], func=AF.Ln)
            k.ins('dve', 'tensor_tensor', out=st[:, 2, :], in0=M1[:, :, 0], in1=st[:, 1, :], op=ALU.add)
            k.ins('dve', 'tensor_scalar', out=st[:, 2, :], in0=st[:, 2, :], scalar1=-1.0, scalar2=None, op0=ALU.mult)
            k.ins('dve', 'tensor_scalar', out=st[:, 3, :], in0=M2[:, :, 0], scalar1=-1.0, scalar2=None, op0=ALU.mult)
            for h in range(8):
                k.ins('act', 'activation', out=M1[:, h, :], in_=M1[:, h, :], func=AF.Exp, bias=st[:, 2, h:h + 1])
                k.ins('act', 'activation', out=M2[:, h, :], in_=M2[:, h, :], func=AF.Exp, bias=st[:, 3, h:h + 1])
            for h in range(8):
                k.ins('dve', 'tensor_tensor', out=cand[:], in0=M1[:, h, :].unsq(2).bc([128, 16, 16]), in1=M2[:, h, :].unsq(1).bc([128, 16, 16]), op=ALU.mult)
                cf = cand[:].re("p a b -> p (a b)")
                k.ins('dve', 'max', out=C16[:, h, 0:8], in_=cf)
                k.ins('dve', 'match_replace', out=wk[:, :], in_to_replace=C16[:, h, 0:8], in_values=cf, imm_value=-1e30)
                k.ins('dve', 'max', out=C16[:, h, 8:16], in_=wk[:, :])
            k.ins('dve', 'tensor_copy', out=g[:, :, 256], in_=C16[:, :, 15])
            for h in range(8):
                k.ins('act', 'activation', out=g[:, h, 0:128], in_=sc[:, 2 * h, :], func=AF.Exp, bias=st[:, 2, h:h + 1])
                k.ins('act', 'activation', out=g[:, h, 128:256], in_=sc[:, 2 * h + 1, :], func=AF.Exp, bias=st[:, 3, h:h + 1])
            k.ld('sp', d['gps'][rows, :], g[:].re("p h c -> p (h c)"))
    k.release(m0)


def phase_c2c(k, d, xnT, idb, nt=NT, nblk=32):
    m0 = k.mark()
    gp = [k.al("c3gp%d" % i, [128, 8, 260], F32) for i in range(2)]
    utb = [k.al("c3utb%d" % i, [128, 8, 512], BF16) for i in range(2)]
    vb = [k.al("c3vb%d" % i, [128, 4, D], BF16) for i in range(2)]
    A = [k.al("c3A%d" % i, [128, 512], F32) for i in range(2)]
    P = [k.al("c3P%d" % i, [128, 8, 4, 128], F32) for i in range(2)]
    Mb = [k.al("c3Mb%d" % i, [128, 8, 512], BF16) for i in range(2)]
    AW = [k.al("c3AW%d" % i, [128, 512], BF16) for i in range(2)]
    awt = [k.al("c3awt%d" % i, [128, 4, 128], BF16) for i in range(2)]
    x1 = [k.al("c3x1%d" % i, [128, D], F32) for i in range(2)]
    UTv = d['UT'][:, :, :].re("kc p e -> p kc e")
    cnt = 0
    for pr in range((nt + 1) // 2):
        tiles = [t for t in (2 * pr, 2 * pr + 1) if t < nt]
        for ti, t_ in enumerate(tiles):
            k.ld('sp', gp[ti][:].re("p h c -> p (h c)"), d['gps'][t_ * 128:(t_ + 1) * 128, :])
        for eb in range(nblk):
            e0 = eb * 512
            u_ = utb[eb % 2]
            v_ = vb[eb % 2]
            k.ld('sp', u_[:], UTv[:, :, e0:e0 + 512])
            k.ld('act', v_[:], d['Vb'][e0:e0 + 512, :].re("(a p) d -> p a d", p=128))
            for ti, t_ in enumerate(tiles):
                c2 = cnt % 2
                cnt += 1
                g = gp[ti]
                pA = k.bank(4 + c2, [128, 512])
                for kc in range(8):
                    k.mm(pA, xnT[:, kc, t_ * 128:(t_ + 1) * 128], u_[:, kc, :], start=(kc == 0), stop=(kc == 7), inc=(kc == 7))
                k.ins('act', 'activation', out=A[c2][:], in_=pA, func=AF.Gelu)
                Pt = P[c2]
                k.ins('dve', 'tensor_tensor', out=Pt[:], in0=g[:, :, 4 * eb:4 * eb + 4].unsq(3).bc([128, 8, 4, 128]),
                      in1=g[:, :, 128:256].unsq(2).bc([128, 8, 4, 128]), op=ALU.mult)
                Mt = Mb[c2]
                for h in range(8):
                    k.ins('dve', 'scalar_tensor_tensor', out=Mt[:, h, :], in0=Pt[:, h, :, :].re("p a i -> p (a i)"), scalar=g[:, h, 256:257],
                          in1=Pt[:, h, :, :].re("p a i -> p (a i)"), op0=ALU.is_ge, op1=ALU.mult)
                k.ins('dve', 'tensor_tensor', out=Mt[:, 0:4, :], in0=Mt[:, 0:4, :], in1=Mt[:, 4:8, :], op=ALU.add)
                k.ins('dve', 'tensor_tensor', out=Mt[:, 0:2, :], in0=Mt[:, 0:2, :], in1=Mt[:, 2:4, :], op=ALU.add)
                k.ins('dve', 'tensor_tensor', out=Mt[:, 0, :], in0=Mt[:, 0, :], in1=Mt[:, 1, :], op=ALU.add)
                k.ins('dve', 'tensor_tensor', out=AW[c2][:], in0=A[c2][:], in1=Mt[:, 0, :], op=ALU.mult)
                pT = k.bank(6 + c2, [128, 4, 128], BF16)
                for a in range(4):
                    k.tr(pT[:, a, :], AW[c2][:, a * 128:(a + 1) * 128], idb[:])
                k.ins('act', 'copy', out=awt[c2][:], in_=pT)
                for a in range(4):
                    for half in range(2):
                        last = (eb == nblk - 1 and a == 3)
                        k.mm(k.bank(2 * ti + half, [128, 512]), awt[c2][:, a, :], v_[:, a, half * 512:(half + 1) * 512],
                             start=(eb == 0 and a == 0), stop=last, inc=(last or (a == 3 and half == 1)))
        for ti, t_ in enumerate(tiles):
            rows = slice(t_ * 128, (t_ + 1) * 128)
            xo = x1[ti]
            k.ld('sp', xo[:], d['x1s'][rows, :])
            for half in range(2):
                hs = slice(half * 512, (half + 1) * 512)
                k.ins('dve', 'tensor_tensor', out=xo[:, hs], in0=xo[:, hs], in1=k.bank(2 * ti + half, [128, 512]), op=ALU.add)
            k.ld('sp', d['x2s'][rows, :], xo[:])
    k.release(m0)


def phase_c3(k, d, idb, nt=NT, final=False, outname='xout'):
    m0 = k.mark()
    stage = [k.al("c4st%d" % i, [128, 1024], F32) for i in range(2)]
    grep = k.al("c4g", [128, D], F32)
    k.ld('sp', grep[:], d['ng_ple'][0:1, :].pbc(128))
    fg = k.al("c4fg", [128, D], F32)
    if final:
        k.ld('sp', fg[:], d['final_g'][0:1, :].pbc(128))
    wgt = k.al("c4wg", [128, 8, D], BF16)
    load_w_bf16(k, wgt, d['w_gate'], D, stage)
    wpj = k.al("c4wp", [128, 2, D], BF16)
    load_w_bf16(k, wpj, d['w_proj'], D, stage)
    xb = [k.al("c4x%d" % i, [128, D], F32) for i in range(2)]
    junk = k.al("c4junk", [128, D], BF16)
    junkf = k.al("c4junkf", [128, D], F32)
    ss = k.al("c4ss", [128, 12], F32)
    hb = k.al("c4hb", [128, D], BF16)
    hT = k.al("c4hT", [128, 8, 128], BF16)
    sgp = k.al("c4sg", [128, D], F32)
    pst = [k.al("c4pst%d" % i, [128, 2, 128], F32) for i in range(2)]
    pbf = k.al("c4pbf", [128, 2, 128], BF16)
    xo = [k.al("c4xo%d" % i, [128, D], F32) for i in range(2)]
    rot = 0
    for tt in range(nt):
        rows = slice(tt * 128, (tt + 1) * 128)
        xt = xb[tt % 2]
        k.ld('sp', xt[:], d['x2s'][rows, :])
        rms_tile(k, xt[:], grep, hb[:], junk, ss, 0)
        transpose8(k, hT[:], hb, idb, tt % 2)
        k.ld('sp', pst[tt % 2][:], d['pT'][:, rows].re("(kc p) t -> p kc t", p=128))
        k.ins('act', 'copy', out=pbf[:], in_=pst[tt % 2][:])
        for half in range(2):
            hs = slice(half * 512, (half + 1) * 512)
            pc = k.bank(2 + rot % 4, [128, 512])
            rot += 1
            for kc in range(8):
                k.mm(pc, hT[:, kc, :], wgt[:, kc, hs], start=(kc == 0), stop=(kc == 7), inc=(kc == 7))
            k.ins('act', 'activation', out=sgp[:, hs], in_=pc, func=AF.Sigmoid)
            pc2 = k.bank(2 + rot % 4, [128, 512])
            rot += 1
            for kc in range(2):
                k.mm(pc2, pbf[:, kc, :], wpj[:, kc, hs], start=(kc == 0), stop=(kc == 1), inc=(kc == 1))
            k.ins('dve', 'tensor_tensor', out=sgp[:, hs], in0=sgp[:, hs], in1=pc2, op=ALU.mult)
        o = xo[tt % 2]
        k.ins('dve', 'tensor_tensor', out=o[:], in0=xt[:], in1=sgp[:], op=ALU.add)
        if final:
            k.ins('act', 'activation', out=junkf[:], in_=o[:], func=AF.Square, accum_out=ss[:, 3:4])
            k.ins('dve', 'tensor_scalar', out=ss[:, 4:5], in0=ss[:, 3:4], scalar1=1.0 / D, scalar2=EPS, op0=ALU.mult, op1=ALU.add)
            k.ins('dve', 'reciprocal', out=ss[:, 4:5], in_=ss[:, 4:5])
            k.ins('act', 'activation', out=ss[:, 5:6], in_=ss[:, 4:5], func=AF.Sqrt)
            k.ins('dve', 'scalar_tensor_tensor', out=o[:], in0=o[:], scalar=ss[:, 5:6], in1=fg[:], op0=ALU.mult, op1=ALU.mult)
        k.ld('sp', d[outname][rows, :], o[:])
    k.release(m0)


CIN = [("xtok", [TT, D]), ("ng_mix", [1, D]), ("wg", [D, 3072]), ("ymlT", [D, TT]), ("ysdT", [D, TT]), ("yfoxT", [D, TT]),
       ("ysd_tm", [TT, D]), ("sng", [128, 8]), ("w_ml", [D, D]), ("w_ssm", [D, D]), ("w_fox", [D, D]), ("w_out", [D, D]),
       ("ng_ffn", [1, D]), ("w_q", [D, 2048]), ("keysT", [16, 128, 128]), ("peer_u", [NE, D]), ("peer_v", [NE, D]),
       ("ng_ple", [1, D]), ("w_gate", [D, D]), ("w_proj", [256, D]), ("pT", [256, TT]), ("final_g", [1, D])]


def build_c(dbg=True, phases=('1', 'a', 'b', 'c', '3'), nt=NT, nblk=32, final=False):
    nc = bass.Bass("TRN2", target_bir_lowering=False)
    k = K(nc)
    kS = "ExternalOutput" if dbg else "Internal"
    d = {}
    for nm, shp in CIN:
        d[nm] = k.dram(nm, shp, F32, "ExternalInput")
    d['x1s'] = k.dram("x1s", [TT, D], F32, kS)
    d['x2s'] = k.dram("x2s", [TT, D], F32, kS)
    d['gps'] = k.dram("gps", [TT, 8 * 260], F32, kS)
    d['UT'] = k.dram("UT", [8, 128, NE], BF16, "Internal")
    d['Vb'] = k.dram("Vb", [NE, D], BF16, "Internal")
    d['xout'] = k.dram("xout", [TT, D], F32, "ExternalOutput")
    k.init_arena(200 * 1024)
    idf, idb = make_ident(k)
    outs = [d['xout']]
    if '1' in phases:
        phase_c1(k, d, idf, idb, nt)
        outs.append(d['x1s'])
    if 'a' in phases:
        phase_c2a(k, d, idb, nblk)
    xnT = k.al("xnT", [128, 8, TT], BF16)
    if 'b' in phases:
        phase_c2b(k, d, xnT, idb, nt)
        outs.append(d['gps'])
    if 'c' in phases:
        phase_c2c(k, d, xnT, idb, nt, nblk)
        outs.append(d['x2s'])
    if '3' in phases:
        phase_c3(k, d, idb, nt, final)
    k.final_wait('sp', outs)
    k.emit()
    print("instructions:", k.ninst, "sems:", len(k.sems))
    return nc


def prep_c(inp, layer, xfull, yml, ysd, yfox):
    w = inp['w_in'][layer]
    O = OFF
    wg = np.ascontiguousarray(w[:, O['g_ml']:O['g_ml'] + 3072])
    sng = np.ascontiguousarray(inp['ssm_norm_g'][layer].reshape(8, 128).T)
    keysT = np.empty((16, 128, 128), np.float32)
    for h in range(8):
        keysT[2 * h] = inp['peer_keys1'][layer][h].T
        keysT[2 * h + 1] = inp['peer_keys2'][layer][h].T
    maps = []
    xf = xfull.reshape(16384, D)
    ymlf, ysdf, yfoxf = yml.reshape(16384, D), ysd.reshape(16384, D), yfox.reshape(16384, D)
    pf = inp['p'][layer].reshape(16384, 256)
    for c in range(8):
        rows = slice(c * TT, (c + 1) * TT)
        m = {
            'xtok': np.ascontiguousarray(xf[rows]), 'ng_mix': inp['norm_mix_g'][layer].reshape(1, D), 'wg': wg,
            'ymlT': np.ascontiguousarray(ymlf[rows].T), 'ysdT': np.ascontiguousarray(ysdf[rows].T), 'yfoxT': np.ascontiguousarray(yfoxf[rows].T),
            'ysd_tm': np.ascontiguousarray(ysdf[rows]), 'sng': sng,
            'w_ml': inp['w_branch_ml'][layer], 'w_ssm': inp['w_branch_ssm'][layer], 'w_fox': inp['w_branch_fox'][layer], 'w_out': inp['w_out'][layer],
            'ng_ffn': inp['norm_ffn_g'][layer].reshape(1, D), 'w_q': inp['peer_w_q'][layer], 'keysT': keysT,
            'peer_u': inp['peer_u'][layer], 'peer_v': inp['peer_v'][layer],
            'ng_ple': inp['norm_ple_g'][layer].reshape(1, D), 'w_gate': inp['ple_w_gate'][layer], 'w_proj': inp['ple_w_proj'][layer],
            'pT': np.ascontiguousarray(pf[rows].T), 'final_g': inp['final_norm_g'].reshape(1, D),
        }
        maps.append({kk: np.ascontiguousarray(v, dtype=np.float32) for kk, v in m.items()})
    return maps


AB_IN = [("wcm", [D, NCM]), ("wtm", [D, NTM]), ("cw", [128, 6, 4]), ("cb", [128, 6]), ("ng", [1, D]), ("fqg", [1, 128]),
         ("fkg", [1, 128]), ("fbf", [1, 2]), ("sprm", [1, 12]), ("mprm", [1, 4]), ("mlg", [1, 256])]
C_SKIP = ("xtok", "ymlT", "ysdT", "yfoxT", "ysd_tm", "final_g")


def build_fused(depth=2):
    nc = bass.Bass("TRN2", target_bir_lowering=False)
    k = K(nc)
    g = {
        'x': k.dram("x", [T, D], F32, "ExternalInput"),
        'xtok': k.dram("xtok", [TT, D], F32, "ExternalInput"),
        'bmask': k.dram("bmask", [1, 4], F32, "ExternalInput"),
        'final_g': k.dram("final_g", [1, D], F32, "ExternalInput"),
    }
    L = []
    for l in range(depth):
        dl = {}
        for nm, shp in AB_IN:
            dl[nm] = k.dram("%s_%d" % (nm, l), shp, F32, "ExternalInput")
        for nm, shp in CIN:
            if nm not in C_SKIP:
                dl[nm] = k.dram("%s_%d" % (nm, l), shp, F32, "ExternalInput")
        L.append(dl)
    zcm = k.dram("zcm", [768, T], BF16, "Internal")
    zg = k.dram("zg", [128, T], F32, "Internal")
    ztm = k.dram("ztm", [T, NTM], F32, "Internal")
    ysrc = k.dram("ysrc", [T, 768], F32, "Internal")
    ydst = k.dram("ydst", [32, 4, 256, 768], F32, "Internal")
    xcur = k.dram("xcur", [TT, D], F32, "Internal")
    xg = k.dram("xg", [8, 4, 256, D], F32, "Internal")
    sc = {
        'x1s': k.dram("x1s", [TT, D], F32, "Internal"), 'x2s': k.dram("x2s", [TT, D], F32, "Internal"),
        'gps': k.dram("gps", [TT, 8 * 260], F32, "Internal"), 'UT': k.dram("UT", [8, 128, NE], BF16, "Internal"),
        'Vb': k.dram("Vb", [NE, D], BF16, "Internal"), 'xout': k.dram("xout", [TT, D], F32, "ExternalOutput"),
        'xcur': xcur, 'ydst': ydst, 'bmask': g['bmask'], 'final_g': g['final_g'],
    }
    k.init_arena(206 * 1024)
    idf, idb = make_ident(k)

    def ystore(c, col0, ncol, ref):
        k.ld('sp', ysrc[c * 128:(c + 1) * 128, col0:col0 + ncol], ref)

    for l in range(depth):
        dl = L[l]
        xsrc = g['x'] if l == 0 else (lambda r0: xg[(r0 % 2048) // 256, r0 // 2048, (r0 % 256):(r0 % 256) + 128, :])
        mp = k.mark()
        phase_p(k, xsrc, dl['ng'], dl['wcm'], dl['wtm'], dl['cw'], dl['cb'], zcm, zg, ztm, idb, NG)
        k.release(mp)
        phase_m(k, zcm, zg, ztm, dl['mprm'], dl['mlg'], None, idf, idb, nch=64, ystore=ystore)
        phase_s(k, zcm, zg, ztm, dl['sprm'], None, idf, idb, nch=64, ystore=ystore)
        phase_f(k, ztm, zg, dl['fqg'], dl['fkg'], dl['fbf'], None, idf, idb, nq=16, ystore=ystore)
        for cch in range(32):
            k.coll("AllGather", ydst[cch, :, :, :], ysrc[cch * 256:(cch + 1) * 256, :], [[0, 1, 2, 3], [4, 5, 6, 7]])
        d = dict(sc)
        d.update(dl)
        d['xtok'] = g['xtok'] if l == 0 else xcur
        final = (l == depth - 1)
        phase_c1(k, d, idf, idb, NT, fused=True)
        phase_c2a(k, d, idb, 32)
        mx = k.mark()
        xnT = k.al("xnT", [128, 8, TT], BF16)
        phase_c2b(k, d, xnT, idb, NT)
        phase_c2c(k, d, xnT, idb, NT, 32)
        k.release(mx)
        phase_c3(k, d, idb, NT, final, outname=('xout' if final else 'xcur'))
        if not final:
            for cch in range(8):
                k.coll("AllGather", xg[cch, :, :, :], xcur[cch * 256:(cch + 1) * 256, :], [[0, 1, 2, 3], [4, 5, 6, 7]])
    k.final_wait('sp', [sc['xout']])
    k.emit()
    return nc


def kernel(**inputs):
    inp = {k_: np.asarray(v) for k_, v in inputs.items()}
    x = np.ascontiguousarray(inp['x'], dtype=np.float32)
    depth = inp['w_in'].shape[0]
    zeros = np.zeros((2, T, 1024), np.float32)
    maps = [dict() for _ in range(8)]
    xf = x.reshape(16384, D)
    for l in range(depth):
        mab = prep_ab(inp, l, x)
        mc = prep_c(inp, l, x, zeros, zeros, zeros)
        for c in range(8):
            for nm, _ in AB_IN:
                maps[c]["%s_%d" % (nm, l)] = mab[c][nm]
            for nm, _ in CIN:
                if nm not in C_SKIP:
                    maps[c]["%s_%d" % (nm, l)] = mc[c][nm]
    for c in range(8):
        b = c // 4
        maps[c]['x'] = np.ascontiguousarray(x[b])
        maps[c]['xtok'] = np.ascontiguousarray(xf[c * TT:(c + 1) * TT])
        bmk = np.zeros((1, 4), np.float32)
        bmk[0, c % 4] = 1.0
        maps[c]['bmask'] = bmk
        maps[c]['final_g'] = np.ascontiguousarray(inp['final_norm_g'].reshape(1, D), dtype=np.float32)
    nc = build_fused(depth)
    res = run_bass_kernel_spmd(nc, maps, core_ids=list(range(8)))
    out = np.empty((16384, D), np.float32)
    for c in range(8):
        out[c * TT:(c + 1) * TT] = np.asarray(res.results[c]['xout'])
    return np.ascontiguousarray(out.reshape(2, T, D), dtype=np.float32)
```
